# Optimizing a Trainium2 kernel written in Bass

```python
import math
import jax, jax.numpy as jnp
from jax import lax
import numpy as np

D_MODEL = 1024
BATCH = 4
SEQ = 4096
DEPTH = 4

HEAD_DIM = 64
A_HEADS = 4
A_VDIM = 2 * HEAD_DIM
B_HEADS = 8
B_KV_HEADS = 2
B_WINDOW = 128
C_HEADS = 8
C_KV_HEADS = 2
CMP_BLOCK = 32
CMP_STRIDE = 16
CMP_HIDDEN = 256
SEL_BLOCK = 64
N_SELECT = 16
C_WINDOW = 512
MIX_WIDTH = B_HEADS * HEAD_DIM
N_BRANCH = 3
D_FF = 4 * D_MODEL
NUM_BUCKETS = 32
MAX_DISTANCE = 128
TOTAL_HEADS = A_HEADS + B_HEADS + C_HEADS
Q_BLOCK = 128
SEL_Q_BLOCK = 64
NEG_INF = -1e30
FORCE_BONUS = 1e4
EPS = 1e-6

SPLIT_SIZES = (
    A_HEADS * 2 * HEAD_DIM,
    A_HEADS * 2 * HEAD_DIM,
    A_HEADS * A_VDIM,
    B_HEADS * HEAD_DIM,
    B_KV_HEADS * HEAD_DIM,
    B_KV_HEADS * HEAD_DIM,
    C_HEADS * HEAD_DIM,
    6 * C_KV_HEADS * HEAD_DIM,
    C_HEADS * 3,
    N_BRANCH * D_MODEL,
)
IN_COLS = sum(SPLIT_SIZES)

kernel_name = 'hybrid_gated_diffattn_swa_nsa'


def rms_norm(x, g):
    xf = x.astype(jnp.float32)
    y = xf * lax.rsqrt(jnp.mean(xf * xf, axis=-1, keepdims=True) + EPS)
    return (y * g.astype(jnp.float32)).astype(x.dtype)


def rel_bucket(dist):
    n = jnp.maximum(dist, 0)
    max_exact = NUM_BUCKETS // 2
    nf = jnp.maximum(n, 1).astype(jnp.float32)
    large = max_exact + (jnp.log(nf / max_exact) / math.log(MAX_DISTANCE / max_exact)
                         * (NUM_BUCKETS - max_exact)).astype(jnp.int32)
    large = jnp.minimum(large, NUM_BUCKETS - 1)
    return jnp.where(n < max_exact, n, large)


def diff_attention(q, k, v, lam, bias_tab):
    B, S, H, _, D = q.shape
    nb = S // Q_BLOCK
    scale = D ** -0.5
    k_pos = jnp.arange(S)
    qb = q.reshape(B, nb, Q_BLOCK, H, 2, D).transpose(1, 0, 2, 3, 4, 5)

    def block(args):
        i, qi = args
        q_pos = i * Q_BLOCK + jnp.arange(Q_BLOCK)
        dist = q_pos[:, None] - k_pos[None, :]
        mask = dist >= 0
        bias = bias_tab[rel_bucket(dist)].transpose(2, 0, 1).astype(jnp.float32)
        s = jnp.einsum('bqhcd,bkhcd->cbhqk', qi, k).astype(jnp.float32) * scale + bias
        p = jax.nn.softmax(jnp.where(mask, s, NEG_INF), axis=-1)
        w = p[0] - lam * p[1]
        return jnp.einsum('bhqk,bkhe->bqhe', w.astype(v.dtype), v)

    out = lax.map(block, (jnp.arange(nb), qb))
    return out.transpose(1, 0, 2, 3, 4).reshape(B, S, H, v.shape[-1])


def banded_attention(q, k, v, bias_tab, window, sink=None):
    B, S, H, D = q.shape
    G = k.shape[2]
    R = H // G
    nb = S // Q_BLOCK
    n_prev = -(-window // Q_BLOCK)
    kw = (n_prev + 1) * Q_BLOCK

    def band(t):
        tb = t.reshape(B, nb, Q_BLOCK, G, D)
        tb = jnp.pad(tb, ((0, 0), (n_prev, 0), (0, 0), (0, 0), (0, 0)))
        return jnp.concatenate([tb[:, j:j + nb] for j in range(n_prev + 1)], axis=2)

    kb, vb = band(k), band(v)
    qb = q.reshape(B, nb, Q_BLOCK, G, R, D)
    r = jnp.arange(Q_BLOCK)
    c = jnp.arange(kw)
    dist = n_prev * Q_BLOCK + r[:, None] - c[None, :]
    k_pos = (jnp.arange(nb)[:, None] - n_prev) * Q_BLOCK + c[None, :]
    mask = ((dist >= 0) & (dist < window))[None] & (k_pos >= 0)[:, None, :]
    bias = bias_tab[rel_bucket(dist)].reshape(Q_BLOCK, kw, G, R).transpose(2, 3, 0, 1).astype(jnp.float32)
    s = jnp.einsum('bnqgrd,bnkgd->bngrqk', qb, kb).astype(jnp.float32) * D ** -0.5 + bias
    s = jnp.where(mask[:, None, None], s, NEG_INF)
    if sink is None:
        p = jax.nn.softmax(s, axis=-1)
    else:
        sk = sink.astype(jnp.float32).reshape(G, R, 1, 1)
        m = jnp.maximum(jnp.max(s, axis=-1, keepdims=True), sk)
        e = jnp.exp(s - m)
        p = e / (jnp.sum(e, axis=-1, keepdims=True) + jnp.exp(sk - m))
    o = jnp.einsum('bngrqk,bnkgd->bnqgrd', p.astype(v.dtype), vb)
    return o.reshape(B, S, H, D)


def compress_blocks(t, pos, w1, w2):
    B, S, G, D = t.shape
    nc = (S - CMP_BLOCK) // CMP_STRIDE + 1
    idx = jnp.arange(nc)[:, None] * CMP_STRIDE + jnp.arange(CMP_BLOCK)[None, :]
    blocks = t[:, idx] + pos[None, None, :, None, :]
    flat = blocks.transpose(0, 1, 3, 2, 4).reshape(B, nc, G, CMP_BLOCK * D)
    return jax.nn.gelu(flat @ w1) @ w2


def nsa_attention(q, kc, vc, ks, vs, kw, vw, gates, bias_tab):
    B, S, H, D = q.shape
    G = ks.shape[2]
    R = H // G
    nc = kc.shape[1]
    scale = D ** -0.5
    qg = q.reshape(B, S, G, R, D)
    t_pos = jnp.arange(S)

    c_start = jnp.arange(nc) * CMP_STRIDE
    cmask = (c_start + CMP_BLOCK - 1)[None, :] <= t_pos[:, None]
    s = jnp.einsum('bsgrd,bcgd->bgrsc', qg, kc).astype(jnp.float32) * scale
    p_cmp = jnp.where(cmask, jax.nn.softmax(jnp.where(cmask, s, NEG_INF), axis=-1), 0.0)
    o_cmp = jnp.einsum('bgrsc,bcgd->bsgrd', p_cmp.astype(vc.dtype), vc).reshape(B, S, H, D)

    n_blk = S // SEL_BLOCK
    j_start = jnp.arange(n_blk) * SEL_BLOCK
    overlap = ((c_start[:, None] < j_start[None, :] + SEL_BLOCK)
               & (c_start[:, None] + CMP_BLOCK > j_start[None, :])).astype(jnp.float32)
    imp = jnp.einsum('bgrsc,cj->bgsj', p_cmp, overlap)
    cur = t_pos // SEL_BLOCK
    jj = jnp.arange(n_blk)
    valid = j_start[None, :] <= t_pos[:, None]
    forced = (jj[None, :] == 0) | (jj[None, :] == cur[:, None]) | (jj[None, :] == cur[:, None] - 1)
    score = jnp.where(valid, imp + jnp.where(forced, FORCE_BONUS, 0.0), NEG_INF)
    n_sel = min(N_SELECT, n_blk)
    _, sel_idx = lax.top_k(score, n_sel)

    kbk = ks.reshape(B, n_blk, SEL_BLOCK, G, D).transpose(0, 3, 1, 2, 4)
    vbk = vs.reshape(B, n_blk, SEL_BLOCK, G, D).transpose(0, 3, 1, 2, 4)
    nqb = S // SEL_Q_BLOCK
    qs = qg.reshape(B, nqb, SEL_Q_BLOCK, G, R, D).transpose(1, 0, 2, 3, 4, 5)
    idx_b = sel_idx.reshape(B, G, nqb, SEL_Q_BLOCK, n_sel).transpose(2, 0, 1, 3, 4)
    bi = jnp.arange(B)[:, None, None, None]
    gi = jnp.arange(G)[None, :, None, None]
    bias_g = bias_tab.reshape(NUM_BUCKETS, G, R).transpose(1, 0, 2)

    def sel_block(args):
        i, qi, ii = args
        kg = kbk[bi, gi, ii]
        vg = vbk[bi, gi, ii].reshape(B, G, SEL_Q_BLOCK, n_sel * SEL_BLOCK, D)
        q_pos = i * SEL_Q_BLOCK + jnp.arange(SEL_Q_BLOCK)
        k_pos = (ii[..., None] * SEL_BLOCK + jnp.arange(SEL_BLOCK)).reshape(B, G, SEL_Q_BLOCK, n_sel * SEL_BLOCK)
        dist = q_pos[None, None, :, None] - k_pos
        mask = dist >= 0
        bias = bias_g[gi, rel_bucket(dist)].transpose(0, 1, 4, 2, 3).astype(jnp.float32)
        s = jnp.einsum('bqgrd,bgqnkd->bgrqnk', qi, kg).reshape(B, G, R, SEL_Q_BLOCK, n_sel * SEL_BLOCK)
        s = s.astype(jnp.float32) * scale + bias
        p = jax.nn.softmax(jnp.where(mask[:, :, None], s, NEG_INF), axis=-1)
        return jnp.einsum('bgrqk,bgqkd->bqgrd', p.astype(vg.dtype), vg)

    o_sel = lax.map(sel_block, (jnp.arange(nqb), qs, idx_b))
    o_sel = o_sel.transpose(1, 0, 2, 3, 4, 5).reshape(B, S, H, D)

    o_win = banded_attention(q, kw, vw, bias_tab, C_WINDOW)

    g = jax.nn.sigmoid(gates.astype(jnp.float32)).astype(q.dtype)
    out = g[..., 0:1] * o_cmp + g[..., 1:2] * o_sel + g[..., 2:3] * o_win
    return out.reshape(B, S, H * D)


def setup_inputs(seed: int = 0) -> dict:
    key = jax.random.key(seed)
    ks = jax.random.split(key, 17)

    def nrm(k, shape, scale):
        return jax.random.normal(k, shape, jnp.float32) * scale

    return {
        'x': nrm(ks[0], (BATCH, SEQ, D_MODEL), 1.0),
        'w_in': nrm(ks[1], (DEPTH, D_MODEL, IN_COLS), D_MODEL ** -0.5),
        'qk_gain': 1.0 + nrm(ks[2], (DEPTH, 8, HEAD_DIM), 0.1),
        'diff_lambda': nrm(ks[3], (DEPTH, 4, HEAD_DIM), 0.1),
        'diff_subln': 1.0 + nrm(ks[4], (DEPTH, A_VDIM), 0.1),
        'sinks': nrm(ks[5], (DEPTH, B_HEADS), 0.5),
        'cmp_pos': nrm(ks[6], (DEPTH, 2, CMP_BLOCK, HEAD_DIM), 0.5),
        'cmp_w1': nrm(ks[7], (DEPTH, 2, CMP_BLOCK * HEAD_DIM, CMP_HIDDEN), (CMP_BLOCK * HEAD_DIM) ** -0.5),
        'cmp_w2': nrm(ks[8], (DEPTH, 2, CMP_HIDDEN, HEAD_DIM), CMP_HIDDEN ** -0.5),
        'w_branch': nrm(ks[9], (DEPTH, N_BRANCH, MIX_WIDTH, D_MODEL), MIX_WIDTH ** -0.5),
        'w_out': nrm(ks[10], (DEPTH, D_MODEL, D_MODEL), D_MODEL ** -0.5),
        'norm_mix': 1.0 + nrm(ks[11], (DEPTH, D_MODEL), 0.1),
        'norm_mlp': 1.0 + nrm(ks[12], (DEPTH, D_MODEL), 0.1),
        'w_up': nrm(ks[13], (DEPTH, D_MODEL, D_FF), D_MODEL ** -0.5),
        'w_down': nrm(ks[14], (DEPTH, D_FF, D_MODEL), D_FF ** -0.5),
        'rel_bias': nrm(ks[15], (NUM_BUCKETS, TOTAL_HEADS), 0.5),
    }


def reference(x, w_in, qk_gain, diff_lambda, diff_subln, sinks, cmp_pos, cmp_w1, cmp_w2,
              w_branch, w_out, norm_mix, norm_mlp, w_up, w_down, rel_bias):
    B, S, _ = x.shape
    points = []
    acc = 0
    for size in SPLIT_SIZES[:-1]:
        acc += size
        points.append(acc)
    bias_a = rel_bias[:, :A_HEADS]
    bias_b = rel_bias[:, A_HEADS:A_HEADS + B_HEADS]
    bias_c = rel_bias[:, A_HEADS + B_HEADS:]

    for layer in range(DEPTH):
        h = rms_norm(x, norm_mix[layer])
        proj = h @ w_in[layer]
        aq, ak, av, bq, bk, bv, cq, ckv, cg, mg = jnp.split(proj, points, axis=-1)
        gains = qk_gain[layer]

        aq = rms_norm(aq.reshape(B, S, A_HEADS, 2, HEAD_DIM), gains[0])
        ak = rms_norm(ak.reshape(B, S, A_HEADS, 2, HEAD_DIM), gains[1])
        av = av.reshape(B, S, A_HEADS, A_VDIM)
        lmb = diff_lambda[layer].astype(jnp.float32)
        lam_init = 0.8 - 0.6 * math.exp(-0.3 * layer)
        lam = jnp.exp(jnp.sum(lmb[0] * lmb[1])) - jnp.exp(jnp.sum(lmb[2] * lmb[3])) + lam_init
        oa = diff_attention(aq, ak, av, lam, bias_a)
        oa = (rms_norm(oa, diff_subln[layer]) * (1.0 - lam_init)).reshape(B, S, MIX_WIDTH)

        bq = rms_norm(bq.reshape(B, S, B_HEADS, HEAD_DIM), gains[2])
        bk = rms_norm(bk.reshape(B, S, B_KV_HEADS, HEAD_DIM), gains[3])
        bv = bv.reshape(B, S, B_KV_HEADS, HEAD_DIM)
        ob = banded_attention(bq, bk, bv, bias_b, B_WINDOW, sinks[layer]).reshape(B, S, MIX_WIDTH)

        cq = rms_norm(cq.reshape(B, S, C_HEADS, HEAD_DIM), gains[4])
        ckv = ckv.reshape(B, S, 6, C_KV_HEADS, HEAD_DIM)
        kc = rms_norm(compress_blocks(ckv[:, :, 0], cmp_pos[layer, 0], cmp_w1[layer, 0], cmp_w2[layer, 0]), gains[5])
        vc = compress_blocks(ckv[:, :, 1], cmp_pos[layer, 1], cmp_w1[layer, 1], cmp_w2[layer, 1])
        ksel = rms_norm(ckv[:, :, 2], gains[6])
        kwin = rms_norm(ckv[:, :, 4], gains[7])
        oc = nsa_attention(cq, kc, vc, ksel, ckv[:, :, 3], kwin, ckv[:, :, 5],
                           cg.reshape(B, S, C_HEADS, 3), bias_c)

        branches = jnp.stack([oa, ob, oc], axis=0)
        y = jnp.einsum('nbsm,nmd->bsnd', branches, w_branch[layer])
        gate = jax.nn.sigmoid(mg.reshape(B, S, N_BRANCH, D_MODEL).astype(jnp.float32)).astype(y.dtype)
        x = x + jnp.sum(gate * y, axis=2) @ w_out[layer]

        h = rms_norm(x, norm_mlp[layer])
        x = x + jnp.square(jax.nn.relu(h @ w_up[layer])) @ w_down[layer]
    return x
```

```python
import numpy as np
import ml_dtypes
from contextlib import ExitStack
import concourse.bass as bass
import concourse.mybir as mybir
from concourse.bass_utils import run_bass_kernel_spmd

F32 = mybir.dt.float32
BF16 = mybir.dt.bfloat16
ALU = mybir.AluOpType
ACT = mybir.ActivationFunctionType
AX = mybir.AxisListType

ENGINES = ('tensor', 'vector', 'scalar', 'gpsimd', 'sync')
NDMA_SEMS = 24
SEM_EPOCH = 30000


class Buf:
    __slots__ = ('ap', 'w', 'r', 'name')

    def __init__(self, ap, name):
        self.ap = ap
        self.w = None
        self.r = {}
        self.name = name


class Tok(Buf):
    def __init__(self, name=''):
        Buf.__init__(self, None, name)


class _Rec:
    def __init__(self):
        self.name = None

    def __getattr__(self, name):
        def f(*args, **kwargs):
            self.name = name
            self.args = args
            self.kwargs = kwargs
            return self
        return f


class Prog:
    def __init__(self, nc, es, same_engine_sync=True):
        self.nc = nc
        self.es = es
        self.sem_es = es
        self.streams = {e: [] for e in ENGINES}
        self.cur = {}
        self.waited = {e: {} for e in ENGINES}
        self.nsem = 0
        self.dma_sems = []
        self.dma_pools = {}
        self.dma_rrs = {}
        self.same_engine_sync = same_engine_sync
        self.n_ops = 0

    def new_sem(self, name):
        s = self.sem_es.enter_context(self.nc.semaphore(name))
        sid = self.nsem
        self.nsem += 1
        return s, sid

    def sb(self, name, shape, dtype):
        self.n_sb = getattr(self, 'n_sb', 0) + 1
        t = self.es.enter_context(self.nc.sbuf_tensor("%s_u%d" % (name, self.n_sb), list(shape), dtype))
        return Buf(t.ap(), name)

    def ps(self, name):
        t = self.es.enter_context(self.nc.psum_tensor(name, [128, 512], F32))
        return Buf(t.ap(), name)

    def _event(self, e):
        c = self.cur.get(e)
        if c is None or c[1] >= SEM_EPOCH:
            s, sid = self.new_sem("s_%s_%d" % (e, self.nsem))
            c = [s, 0, sid]
            self.cur[e] = c
        c[1] += 1
        return (c[0], c[1], c[2], e)

    def _wait(self, e, ev):
        sem, val, sid, src = ev
        w = self.waited[e]
        if w.get(sid, 0) >= val:
            return
        w[sid] = val
        self.streams[e].append(('w', sem, val))

    def _deps(self, e, reads, writes):
        for t in reads:
            if t.w is not None:
                self._dep1(e, t.w)
        for t in writes:
            if t.w is not None:
                self._dep1(e, t.w)
            for ev in t.r.values():
                self._dep1(e, ev)

    def _dep1(self, e, ev):
        if ev[3] == e and (e == 'tensor' or not self.same_engine_sync):
            return
        self._wait(e, ev)

    def _mark(self, ev, reads, writes):
        for t in reads:
            t.r[ev[2]] = ev
        for t in writes:
            t.w = ev
            t.r = {}

    def op(self, e, fn, reads=(), writes=()):
        rec = _Rec()
        fn(rec)
        self._deps(e, reads, writes)
        ev = self._event(e)
        self.streams[e].append(('i', rec, ev[0]))
        self._mark(ev, reads, writes)
        self.n_ops += 1
        return ev

    def dma(self, e, out, in_, reads=(), writes=()):
        pool = self.dma_pools.setdefault(e, [])
        if not pool:
            for i in range(NDMA_SEMS if e == 'sync' else 8):
                s, sid = self.new_sem("s_dma_%s_%d" % (e, i))
                ent = [s, 0, sid]
                pool.append(ent)
                self.dma_sems.append(ent)
        self.dma_rrs[e] = self.dma_rrs.get(e, 0) + 1
        ds = pool[self.dma_rrs[e] % len(pool)]
        if ds[1] > 0:
            self._wait(e, (ds[0], ds[1], ds[2], 'dma'))
        self._deps(e, reads, writes)
        ds[1] += 16
        ev = (ds[0], ds[1], ds[2], 'dma')
        self.streams[e].append(('d', out, in_, ds[0]))
        self._mark(ev, reads, writes)
        self.n_ops += 1
        return ev

    def collective(self, in_ap, out_ap, reads, writes, groups):
        e = 'gpsimd'
        self._deps(e, reads, writes)
        sem, sid = self.new_sem("s_cc_%d" % self.nsem)
        ev = (sem, 1, sid, 'cc')
        self.streams[e].append(('c', in_ap, out_ap, sem, groups))
        self._mark(ev, reads, writes)
        self.n_ops += 1
        return ev

    def barrier(self):
        evs = []
        for e, c in self.cur.items():
            evs.append((c[0], c[1], c[2], e))
        for ds in self.dma_sems:
            if ds[1] > 0:
                evs.append((ds[0], ds[1], ds[2], 'dma'))
        for e in ENGINES:
            for ev in evs:
                if ev[3] != e:
                    self._wait(e, ev)

    def finish(self):
        for ds in self.dma_sems:
            if ds[1] > 0:
                self._wait('sync', (ds[0], ds[1], ds[2], 'dma'))
        streams = self.streams

        def replay(eng, name):
            for it in streams[name]:
                if it[0] == 'w':
                    eng.wait_ge(it[1], it[2])
                elif it[0] == 'i':
                    getattr(eng, it[1].name)(*it[1].args, **it[1].kwargs).then_inc(it[2], 1)
                elif it[0] == 'c':
                    eng.collective_compute("AllGather", ALU.bypass, replica_groups=it[4], ins=[it[1]], outs=[it[2]]).then_inc(it[3])
                else:
                    eng.dma_start(out=it[1], in_=it[2]).then_inc(it[3], 16)

        with self.nc.Block() as block:
            @block.sync
            def _(eng):
                replay(eng, 'sync')

            @block.tensor
            def _(eng):
                replay(eng, 'tensor')

            @block.vector
            def _(eng):
                replay(eng, 'vector')

            @block.scalar
            def _(eng):
                replay(eng, 'scalar')

            @block.gpsimd
            def _(eng):
                replay(eng, 'gpsimd')


D = 1024
S = 4096
NS = 16
TOK = NS * 128
EPS = 1e-6
NEG = -30000.0
C1 = 3608


def qi_of(j, s):
    m, odd = divmod(s, 2)
    if j == 0:
        return 4 * m + (3 if odd else 0)
    return 4 * m + (2 if odd else 1)


def nproc(s):
    m, odd = divmod(s, 2)
    return 4 * m + (4 if odd else 2)


def V_(P, fn, reads, writes):
    return P.op('vector', fn, reads, writes)


def A_(P, fn, reads, writes):
    return P.op('scalar', fn, reads, writes)


def T_(P, fn, reads, writes):
    return P.op('tensor', fn, reads, writes)


def G_(P, fn, reads, writes):
    return P.op('gpsimd', fn, reads, writes)


def emit_rstd(P, ss, n, eps_t):
    A_(P, lambda e: e.activation(out=ss.ap, in_=ss.ap, func=ACT.Sqrt, scale=1.0 / n, bias=eps_t.ap[:, 0:1]), [ss, eps_t], [ss])
    V_(P, lambda e: e.reciprocal(out=ss.ap, in_=ss.ap), [ss], [ss])


def emit_xnorm_T(P, xsrc_ap, xsrc_tok, hT, s, W):
    sq, ss, xb, pT, idn, eps_t = W['sq1024'], W['ss1'], W['xb'], W['pT'], W['idn'], W['eps']
    A_(P, lambda e: e.activation(out=sq.ap, in_=xsrc_ap, func=ACT.Square, accum_out=ss.ap[:, 0:1]), [xsrc_tok], [sq, ss])
    emit_rstd(P, ss, 1024, eps_t)
    V_(P, lambda e: e.tensor_scalar(out=xb.ap, in0=xsrc_ap, scalar1=ss.ap[:, 0:1], scalar2=None, op0=ALU.mult), [xsrc_tok, ss], [xb])
    pb = pT.ap.bitcast(BF16)
    for kc in range(8):
        T_(P, lambda e, kc=kc: e.transpose(out=pb[:, kc * 128:(kc + 1) * 128], in_=xb.ap[:, kc * 128:(kc + 1) * 128], identity=idn.ap), [xb, idn], [pT])
    A_(P, lambda e: e.copy(out=hT.ap[:, :, s * 128:(s + 1) * 128], in_=pb.rearrange("p (k t) -> p k t", k=8)), [pT], [hT])


PH1_GROUPS = [
    (0, 512, [(0, 512, 'n', dict(gi=0, q=True, dup=False, fm0=0))]),
    (512, 512, [(0, 512, 'n', dict(gi=1, q=False, dup=False, fm0=12))]),
    (1024, 512, [(0, 512, 'v', dict(tmcol=0))]),
    (1536, 512, [(0, 512, 'n', dict(gi=2, q=True, dup=False, fm0=4))]),
    (2048, 256, [(0, 128, 'n', dict(gi=3, q=False, dup=True, fm0=16)), (128, 128, 'v', dict(tmcol=512))]),
    (2304, 512, [(0, 512, 'n', dict(gi=4, q=True, dup=False, fm0=8))]),
    (2816, 512, [(0, 128, 'raw', dict(fm0=18)), (128, 128, 'raw', dict(fm0=19)),
                 (256, 128, 'n', dict(gi=6, q=False, dup=True, fm0=20)), (384, 128, 'v', dict(tmcol=640))]),
    (3328, 280, [(0, 128, 'n', dict(gi=7, q=False, dup=True, fm0=22)), (128, 128, 'v', dict(tmcol=768)),
                 (256, 24, 'cg', dict())]),
]


def emit_phase1(P, cx):
    dbg = False
    idn, eps_t, ps = cx['idn'], cx['eps_t'], cx['ps']
    wv, nm, gains = cx['wv'], cx['nm'], cx['gains']
    with ExitStack() as es1:
        P.es = es1
        W = dict(idn=idn, eps=eps_t, pT=ps[6])
        W['sq1024'] = P.sb("sq1024", [128, 1024], F32)
        W['ss1'] = P.sb("ss1", [128, 1], F32)
        W['xb'] = P.sb("xb", [128, 1024], BF16)
        nm_t = P.sb("nm_t", [128, 8], F32)
        g_t = P.sb("g_t", [128, 8, 64], F32)
        hT = P.sb("hT", [128, 8, TOK], BF16)
        xs = [P.sb("xs%d" % i, [128, 1024], F32) for i in range(2)]
        wst = [P.sb("wst%d" % i, [128, 8, 512], F32) for i in range(2)]
        wbf = [P.sb("wbf%d" % i, [128, 8, 512], BF16) for i in range(2)]
        fms = [P.sb("fms%d" % i, [128, 4, TOK], BF16) for i in range(2)]
        tms = P.sb("tms", [128, NS, 896], BF16)
        cg_t = P.sb("cg_t1", [128, NS, 24], F32)
        pj = ps[0:2]
        pq = ps[2:4]
        Tt = [P.sb("Tt%d" % i, [128, 512], F32) for i in range(2)]
        SQs = [P.sb("SQ%d" % i, [128, 512], F32) for i in range(2)]
        SSs = [P.sb("SS%d" % i, [128, 8], F32) for i in range(2)]
        TB = [P.sb("TB%d" % i, [128, 512], BF16) for i in range(2)]

        P.dma('sync', nm_t.ap, nm, [], [nm_t])
        P.dma('sync', g_t.ap, gains, [], [g_t])

        def load_w(gidx):
            c0, wd, _ = PH1_GROUPS[gidx]
            b = gidx % 2
            P.dma('sync', wst[b].ap[:, :, 0:wd], wv[:, :, c0:c0 + wd], [], [wst[b]])

        def conv_w(gidx):
            c0, wd, _ = PH1_GROUPS[gidx]
            b = gidx % 2
            for kc in range(8):
                if kc % 2 == 0:
                    V_(P, lambda e, kc=kc: e.tensor_scalar(out=wbf[b].ap[:, kc, 0:wd], in0=wst[b].ap[:, kc, 0:wd], scalar1=nm_t.ap[:, kc:kc + 1],
                                                           scalar2=None, op0=ALU.mult), [wst[b], nm_t], [wbf[b]])
                else:
                    A_(P, lambda e, kc=kc: e.activation(out=wbf[b].ap[:, kc, 0:wd], in_=wst[b].ap[:, kc, 0:wd], func=ACT.Copy,
                                                        scale=nm_t.ap[:, kc:kc + 1]), [wst[b], nm_t], [wbf[b]])

        load_w(0)
        for s in range(NS):
            b = s % 2
            P.dma('gpsimd', xs[b].ap, cx['x_rows'](s), [cx['x_tok']], [xs[b]])
            emit_xnorm_T(P, xs[b].ap, xs[b], hT, s, W)
        items = [(gidx, s_) for gidx in range(len(PH1_GROUPS)) for s_ in range(NS)]
        ginfo = {}
        for gidx, (c0, wd, segs) in enumerate(PH1_GROUPS):
            fpos = {}
            _p = 0
            for (off, sw, kind, pr) in segs:
                if kind in ('n', 'raw'):
                    fpos[pr['fm0']] = _p
                    _p += 2 if pr.get('dup') else (1 if kind == 'raw' else sw // 128)
            ginfo[gidx] = (fpos, _p, any(sg[2] == 'n' for sg in segs), any(sg[3].get('q') for sg in segs))

        def stageA(i):
            gidx, s = items[i]
            c0, wd, segs = PH1_GROUPS[gidx]
            if s == 0:
                if gidx + 1 < len(PH1_GROUPS):
                    load_w(gidx + 1)
                conv_w(gidx)
            wb = wbf[gidx % 2]
            pp, T = pj[i % 2], Tt[i % 2]
            for kc in range(8):
                T_(P, lambda e: e.matmul(out=pp.ap[:, 0:wd], lhsT=hT.ap[:, kc, s * 128:(s + 1) * 128], rhs=wb.ap[:, kc, 0:wd],
                                         start=(kc == 0), stop=(kc == 7)), [hT, wb], [pp])
            A_(P, lambda e: e.copy(out=T.ap[:, 0:wd], in_=pp.ap[:, 0:wd]), [pp], [T])

        def stageB(i):
            gidx, s = items[i]
            c0, wd, segs = PH1_GROUPS[gidx]
            fpos, ntot, has_n, isq = ginfo[gidx]
            T, tb, SQ, SS = Tt[i % 2], TB[i % 2], SQs[i % 2], SSs[i % 2]
            nh = wd // 64
            if has_n:
                nw = nh * 64
                V_(P, lambda e: e.tensor_tensor(out=SQ.ap[:, 0:nw], in0=T.ap[:, 0:nw], in1=T.ap[:, 0:nw], op=ALU.mult), [T], [SQ])
                V_(P, lambda e: e.tensor_reduce(out=SS.ap[:, 0:nh], in_=SQ.ap[:, 0:nw].rearrange("p (h d) -> p h d", d=64), axis=AX.X, op=ALU.add), [SQ], [SS])
                emit_rstd(P, SS, 64, eps_t)
            for (off, sw, kind, pr) in segs:
                if kind == 'n':
                    h0, hn = off // 64, sw // 64
                    t0 = fpos[pr['fm0']] * 128
                    t3 = T.ap[:, off:off + sw].rearrange("p (h d) -> p h d", d=64)
                    V_(P, lambda e: e.tensor_tensor(out=t3, in0=t3, in1=SS.ap[:, h0:h0 + hn].unsqueeze(2).broadcast_to([128, hn, 64]), op=ALU.mult), [T, SS], [T])
                    gb = g_t.ap[:, pr['gi'], :].unsqueeze(1).broadcast_to([128, hn, 64])
                    if pr['dup']:
                        o4 = tb.ap[:, t0:t0 + 256].rearrange("p (g c d) -> p g c d", g=2, c=2)
                        for c in range(2):
                            V_(P, lambda e: e.tensor_tensor(out=o4[:, :, c, :], in0=t3, in1=gb, op=ALU.mult), [T, g_t], [tb])
                    else:
                        o3 = tb.ap[:, t0:t0 + sw].rearrange("p (h d) -> p h d", d=64)
                        V_(P, lambda e: e.tensor_tensor(out=o3, in0=t3, in1=gb, op=ALU.mult), [T, g_t], [tb])
                elif kind == 'raw':
                    t0 = fpos[pr['fm0']] * 128
                    V_(P, lambda e: e.tensor_copy(out=tb.ap[:, t0:t0 + sw], in_=T.ap[:, off:off + sw]), [T], [tb])
                elif kind == 'v':
                    tc_ = pr['tmcol']
                    V_(P, lambda e: e.tensor_copy(out=tms.ap[:, s, tc_:tc_ + sw], in_=T.ap[:, off:off + sw]), [T], [tms])
                else:
                    A_(P, lambda e: e.activation(out=cg_t.ap[:, s, :], in_=T.ap[:, off:off + sw], func=ACT.Sigmoid), [T], [cg_t])

        def stageC(i):
            gidx, s = items[i]
            c0, wd, segs = PH1_GROUPS[gidx]
            fpos, ntot, has_n, isq = ginfo[gidx]
            fb = fms[gidx % 2]
            if ntot > 0:
                tb, ptr = TB[i % 2], pq[i % 2]
                pb = ptr.ap.bitcast(BF16)
                for k in range(ntot):
                    T_(P, lambda e: e.transpose(out=pb[:, k * 128:(k + 1) * 128], in_=tb.ap[:, k * 128:(k + 1) * 128], identity=idn.ap), [tb, idn], [ptr])
                dst = fb.ap[:, 0:ntot, s * 128:(s + 1) * 128]
                src = pb[:, 0:ntot * 128].rearrange("p (k t) -> p k t", k=ntot)
                if isq:
                    A_(P, lambda e: e.activation(out=dst, in_=src, func=ACT.Copy, scale=0.125), [ptr], [fb])
                else:
                    V_(P, lambda e: e.tensor_copy(out=dst, in_=src), [ptr], [fb])
            if s == NS - 1:
                for (off, sw, kind, pr) in segs:
                    if kind in ('n', 'raw'):
                        ntile = 2 if pr.get('dup') else (1 if kind == 'raw' else sw // 128)
                        f0 = pr['fm0']
                        fi = fpos[f0]
                        if f0 < 12:
                            P.dma('sync', cx['qt_d'][f0:f0 + ntile].rearrange("k p t -> p k t"), fb.ap[:, fi:fi + ntile, :], [fb], [cx['qt_tok']])
                        else:
                            kq, kr = divmod(f0 - 12, 4)
                            P.dma('sync', cx['exK3'][kq][kr:kr + ntile].rearrange("k p t -> p k t"), fb.ap[:, fi:fi + ntile, :], [fb], [cx['exK_tok'][kq]])

        nit = len(items)
        for step in range(nit + 2):
            if step < nit:
                stageA(step)
            if 0 <= step - 1 < nit:
                stageB(step - 1)
            if 0 <= step - 2 < nit:
                stageC(step - 2)
        for u in range(2):
            P.dma('sync', cx['exV'][u].rearrange("(s p) c -> p s c", p=128), tms.ap[:, 8 * u:8 * u + 8, :], [tms], [cx['exV_tok'][u]])
        P.dma('sync', cx['cgs_d'].rearrange("(s p) c -> p s c", p=128), cg_t.ap, [cg_t], [cx['cgs_tok']])


def attn_unit(P, Wk, q_ap, q_buf, kt_ap_fn, kt_buf, kbs, v_ap_fn, v_buf, ncols, O, bm_ap_fn, bm_buf, cfar_ap, sel=None):
    far = [(kb, r) for kb, r in kbs if r is None]
    near = [(kb, r) for kb, r in kbs if r is not None]
    chunks = [far[i:i + 4] for i in range(0, len(far), 4)] + [near[i:i + 4] for i in range(0, len(near), 4)]
    total = len(kbs)
    done = 0
    for ch in chunks:
        Sb = Wk['sps'][Wk['scnt'] % 2]
        PT = Wk['pts'][Wk['scnt'] % 2]
        Wk['scnt'] += 1
        n = len(ch)
        for i, (kb, r) in enumerate(ch):
            T_(P, lambda e: e.matmul(out=Sb.ap[:, i * 128:(i + 1) * 128], lhsT=kt_ap_fn(kb), rhs=q_ap, start=True, stop=(sel is None)),
               [kt_buf, q_buf], [Sb])
            if sel is not None:
                E, selT_ap, selT_buf, ep0 = sel
                T_(P, lambda e: e.matmul(out=Sb.ap[:, i * 128:(i + 1) * 128], lhsT=E.ap[ep0:ep0 + 64, kb * 128:(kb + 1) * 128], rhs=selT_ap,
                                         start=False, stop=True), [E, selT_buf], [Sb])
        if ch[0][1] is None:
            A_(P, lambda e: e.activation(out=PT.ap[:, 0:n * 128], in_=Sb.ap[:, 0:n * 128], func=ACT.Exp, bias=cfar_ap), [Sb, Wk['consts']], [PT])
        else:
            tmp = Wk['stmp']
            for i, (kb, r) in enumerate(ch):
                V_(P, lambda e: e.tensor_tensor(out=tmp.ap[:, i * 128:(i + 1) * 128], in0=Sb.ap[:, i * 128:(i + 1) * 128], in1=bm_ap_fn(r), op=ALU.add),
                   [Sb, bm_buf], [tmp])
            A_(P, lambda e: e.activation(out=PT.ap[:, 0:n * 128], in_=tmp.ap[:, 0:n * 128], func=ACT.Exp), [tmp], [PT])
        for i, (kb, r) in enumerate(ch):
            T_(P, lambda e: e.matmul(out=O.ap[:, 0:ncols], lhsT=PT.ap[:, i * 128:(i + 1) * 128], rhs=v_ap_fn(kb), start=(done == 0), stop=(done == total - 1)),
               [PT, v_buf], [O])
            done += 1


class Pipe:
    def __init__(self):
        self.pending = None
        self.deferred = []

    def push(self, s_fn, e_fn, pv_fn):
        s_fn()
        e_fn()
        if self.pending is not None:
            self.pending()
        self.pending = pv_fn
        self._tick()

    def _tick(self):
        cur, self.deferred = self.deferred, []
        for item in cur:
            item[0] -= 1
            if item[0] <= 0:
                item[1]()
            else:
                self.deferred.append(item)

    def defer(self, n, fn):
        self.deferred.append([n, fn])

    def flush(self):
        if self.pending is not None:
            self.pending()
            self.pending = None
        while self.deferred:
            self._tick()


def _push_chunk(pipe, P, Wk, ch, base, total, q_ap, q_buf, kt_aps, kt_buf, v_aps, v_buf, ncols, O, bm_aps, bm_buf, cfar_ap, sel, on_done):
    n = len(ch)
    k = Wk['scnt']
    Wk['scnt'] += 1
    Sb = Wk['sps'][k % len(Wk['sps'])]
    PT = Wk['pts'][k % len(Wk['pts'])]
    is_far = ch[0][1] is None

    def s_fn():
        for i, (kb, r) in enumerate(ch):
            T_(P, lambda e: e.matmul(out=Sb.ap[:, i * 128:(i + 1) * 128], lhsT=kt_aps[i], rhs=q_ap, start=True, stop=(sel is None)), [kt_buf, q_buf], [Sb])
            if sel is not None:
                E, selT_ap, selT_buf, ep0 = sel
                T_(P, lambda e: e.matmul(out=Sb.ap[:, i * 128:(i + 1) * 128], lhsT=E.ap[ep0:ep0 + 64, kb * 128:(kb + 1) * 128], rhs=selT_ap,
                                         start=False, stop=True), [E, selT_buf], [Sb])

    def e_fn():
        if is_far:
            A_(P, lambda e: e.activation(out=PT.ap[:, 0:n * 128], in_=Sb.ap[:, 0:n * 128], func=ACT.Exp, bias=cfar_ap), [Sb, Wk['consts']], [PT])
        else:
            tmp = Wk['stmps'][Wk['tcnt'] % len(Wk['stmps'])]
            Wk['tcnt'] += 1
            for i in range(n):
                V_(P, lambda e: e.tensor_tensor(out=tmp.ap[:, i * 128:(i + 1) * 128], in0=Sb.ap[:, i * 128:(i + 1) * 128], in1=bm_aps[i], op=ALU.add), [Sb, bm_buf], [tmp])
            A_(P, lambda e: e.activation(out=PT.ap[:, 0:n * 128], in_=tmp.ap[:, 0:n * 128], func=ACT.Exp), [tmp], [PT])

    def pv_fn():
        for i in range(n):
            T_(P, lambda e: e.matmul(out=O.ap[:, 0:ncols], lhsT=PT.ap[:, i * 128:(i + 1) * 128], rhs=v_aps[i], start=(base + i == 0), stop=(base + i == total - 1)),
               [PT, v_buf], [O])
        if on_done is not None:
            on_done()

    pipe.push(s_fn, e_fn, pv_fn)


def attn_unit_p(pipe, P, Wk, q_ap, q_buf, kt_ap_fn, kt_buf, kbs, v_ap_fn, v_buf, ncols, O, bm_ap_fn, bm_buf, cfar_ap, sel=None, on_done=None):
    far = [(kb, r) for kb, r in kbs if r is None]
    near = [(kb, r) for kb, r in kbs if r is not None]
    chunks = [far[i:i + 4] for i in range(0, len(far), 4)] + [near[i:i + 4] for i in range(0, len(near), 4)]
    base = 0
    for ci, ch in enumerate(chunks):
        _push_chunk(pipe, P, Wk, ch, base, len(kbs), q_ap, q_buf, [kt_ap_fn(kb) for kb, r in ch], kt_buf, [v_ap_fn(kb) for kb, r in ch], v_buf, ncols, O,
                    [bm_ap_fn(r) if r is not None else None for kb, r in ch], bm_buf, cfar_ap, sel, on_done if ci == len(chunks) - 1 else None)
        base += len(ch)


def _push_chunk2(pipe, P, Wk, ch, base, total, qbd_ap, q_buf, kt_aps, kt_buf, v_aps, v_buf, ncols, Os, bm_aps, bm_buf, cfar_aps, sel, on_done):
    n = len(ch)
    k = Wk['scnt']
    Wk['scnt'] += 1
    Sb = Wk['sps'][k % len(Wk['sps'])]
    PT = Wk['pts'][k % len(Wk['pts'])]
    is_far = ch[0][1] is None

    def s_fn():
        for i, (kb, r) in enumerate(ch):
            T_(P, lambda e: e.matmul(out=Sb.ap[:, i * 256:(i + 1) * 256], lhsT=kt_aps[i], rhs=qbd_ap, start=True, stop=(sel is None)), [kt_buf, q_buf], [Sb])
            if sel is not None:
                E, sel2_ap, sel2_buf = sel
                T_(P, lambda e: e.matmul(out=Sb.ap[:, i * 256:(i + 1) * 256], lhsT=E.ap[:, kb * 128:(kb + 1) * 128], rhs=sel2_ap,
                                         start=False, stop=True), [E, sel2_buf], [Sb])

    def e_fn():
        if is_far:
            if cfar_aps[0] is cfar_aps[1]:
                A_(P, lambda e: e.activation(out=PT.ap[:, 0:n * 256], in_=Sb.ap[:, 0:n * 256], func=ACT.Exp, bias=cfar_aps[0]), [Sb, Wk['consts']], [PT])
            else:
                for j in range(2):
                    A_(P, lambda e: e.activation(out=PT.ap[:, 0:n * 256].rearrange("p (i j q) -> p i j q", j=2, q=128)[:, :, j, :],
                                                 in_=Sb.ap[:, 0:n * 256].rearrange("p (i j q) -> p i j q", j=2, q=128)[:, :, j, :],
                                                 func=ACT.Exp, bias=cfar_aps[j]), [Sb, Wk['consts']], [PT])
        else:
            tmp = Wk['stmps'][Wk['tcnt'] % len(Wk['stmps'])]
            Wk['tcnt'] += 1
            for i in range(n):
                V_(P, lambda e: e.tensor_tensor(out=tmp.ap[:, i * 256:(i + 1) * 256].rearrange("p (j q) -> p j q", j=2),
                                                in0=Sb.ap[:, i * 256:(i + 1) * 256].rearrange("p (j q) -> p j q", j=2), in1=bm_aps[i], op=ALU.add), [Sb, bm_buf], [tmp])
            A_(P, lambda e: e.activation(out=PT.ap[:, 0:n * 256], in_=tmp.ap[:, 0:n * 256], func=ACT.Exp), [tmp], [PT])

    def pv_fn():
        for i in range(n):
            for j in range(2):
                T_(P, lambda e: e.matmul(out=Os[j].ap[:, 0:ncols], lhsT=PT.ap[:, i * 256 + j * 128:i * 256 + (j + 1) * 128], rhs=v_aps[i],
                                         start=(base + i == 0), stop=(base + i == total - 1)), [PT, v_buf], [Os[j]])
        if on_done is not None:
            on_done()

    pipe.push(s_fn, e_fn, pv_fn)


def attn_pair_p(pipe, P, Wk, qbd_ap, q_buf, kt_ap_fn, kt_buf, kbs, v_ap_fn, v_buf, ncols, Os, bm_ap_fn, bm_buf, cfar_aps, sel=None, on_done=None):
    far = [(kb, r) for kb, r in kbs if r is None]
    near = [(kb, r) for kb, r in kbs if r is not None]
    chunks = [far[i:i + 2] for i in range(0, len(far), 2)] + [near[i:i + 2] for i in range(0, len(near), 2)]
    base = 0
    for ci, ch in enumerate(chunks):
        _push_chunk2(pipe, P, Wk, ch, base, len(kbs), qbd_ap, q_buf, [kt_ap_fn(kb) for kb, r in ch], kt_buf, [v_ap_fn(kb) for kb, r in ch], v_buf, ncols, Os,
                     [bm_ap_fn(r) if r is not None else None for kb, r in ch], bm_buf, cfar_aps, sel, on_done if ci == len(chunks) - 1 else None)
        base += len(ch)


def near_far(n, nnear):
    kbs = []
    for kb in range(n):
        r = kb - (n - nnear)
        kbs.append((kb, r if r >= 0 else None))
    return kbs


_SKIP = set()
_DBG = False
_DBG_OUT = {}


def emit_phase23(P, c):
    (x_rows, x_tok, qt, qt_tok, cgs, cgs_tok, wmg, wbr, wout, wup, wdn, nm, nmlp, w1, w2, posT, g5, dl, lamc, subln, sinks, cfar,
     bma, bmb, bms, bmw, cm, selb, E_d, ov_d, xmid, xmid_tok, xout, xout_tok) = (c[k] for k in (
        'x_rows', 'x_tok', 'qt_d', 'qt_tok', 'cgs_d', 'cgs_tok', 'wmg', 'wbr', 'wout', 'wup', 'wdn', 'nm', 'nmlp', 'w1', 'w2', 'posT', 'g5', 'dl',
        'lamc', 'subln', 'sinks', 'cfar', 'bma', 'bmb', 'bms', 'bmw', 'cm', 'selb', 'E_d', 'ov_d', 'xmid', 'xmid_tok', 'xout', 'xout_tok'))
    idn, eps_t, ps = c['idn'], c['eps_t'], c['ps']
    load_kT, load_V = c['load_kT'], c['load_V']
    xdbg = c.get('xdbg')
    with ExitStack() as es:
        P.es = es
        consts = P.sb("consts", [128, 64], F32)
        slg = P.sb("slg", [128, 128], F32)
        cg_t = P.sb("cg_t", [128, NS, 24], F32)
        Wk = dict(sps=[ps[0], ps[1], ps[7]], scnt=0, tcnt=0, consts=consts)
        O_ps = ps[2:4]
        Ob = [ps[2], ps[3], ps[5], ps[6]]
        ptr = ps[4]
        pm = ps[5]
        P.dma('sync', consts.ap[:, 0:20], cfar, [], [consts])
        P.dma('sync', consts.ap[:, 20:28], sinks, [], [consts])
        P.dma('sync', consts.ap[:, 29:31], lamc, [], [consts])
        P.dma('sync', slg.ap, subln, [], [slg])
        P.dma('sync', cg_t.ap, cgs.rearrange("(s p) c -> p s c", p=128), [cgs_tok], [cg_t])
        V_(P, lambda e: e.memset(consts.ap[:, 31:32], 1e-30), [], [consts])
        A_(P, lambda e: e.activation(out=consts.ap[:, 20:28], in_=consts.ap[:, 20:28], func=ACT.Exp), [consts], [consts])
        V_(P, lambda e: e.tensor_scalar(out=slg.ap, in0=slg.ap, scalar1=consts.ap[:, 30:31], scalar2=None, op0=ALU.mult), [slg, consts], [slg])

        with ExitStack() as es2:
            P.es = es2
            OT = P.sb("OT", [128, 12, TOK], BF16)
            Wk['pts'] = [P.sb("pt%d" % i, [128, 512], BF16) for i in range(3)]
            Wk['stmps'] = [P.sb("stmp%d" % i, [128, 512], F32) for i in range(2)]
            Wk['stmp'] = Wk['stmps'][0]
            otms = [P.sb("otm%d" % i, [128, 256], BF16) for i in range(4)]
            otm = otms[0]
            smr = [P.sb("smr%d" % i, [128, 8], F32) for i in range(8)]
            a0s = [P.sb("a0_%d" % i, [128, 128], F32) for i in range(4)]
            oos = [P.sb("oo_%d" % i, [128, 128], F32) for i in range(4)]
            pipe = Pipe()
            sm = P.sb("sm", [128, 16], F32)
            dl_t = P.sb("dl_t", [128, 4, 64], F32)
            P.dma('sync', dl_t.ap, dl, [], [dl_t])
            d4 = dl_t.ap.rearrange("p (a b) d -> p a b d", b=2)
            lt = P.sb("lt", [128, 2, 64], F32)
            V_(P, lambda e: e.tensor_tensor(out=lt.ap, in0=d4[:, :, 0, :], in1=d4[:, :, 1, :], op=ALU.mult), [dl_t], [lt])
            V_(P, lambda e: e.tensor_reduce(out=sm.ap[:, 0:2], in_=lt.ap, axis=AX.X, op=ALU.add), [lt], [sm])
            A_(P, lambda e: e.activation(out=sm.ap[:, 0:2], in_=sm.ap[:, 0:2], func=ACT.Exp), [sm], [sm])
            V_(P, lambda e: e.tensor_tensor(out=sm.ap[:, 2:3], in0=sm.ap[:, 1:2], in1=sm.ap[:, 0:1], op=ALU.subtract), [sm], [sm])
            V_(P, lambda e: e.tensor_tensor(out=consts.ap[:, 28:29], in0=sm.ap[:, 2:3], in1=consts.ap[:, 29:30], op=ALU.subtract), [sm, consts], [consts])

            def store_ot(s, t0, ntile, otm=otm):
                pb = ptr.ap.bitcast(BF16)
                for i in range(ntile):
                    T_(P, lambda e: e.transpose(out=pb[:, i * 128:(i + 1) * 128], in_=otm.ap[:, i * 128:(i + 1) * 128], identity=idn.ap), [otm, idn], [ptr])
                A_(P, lambda e: e.copy(out=OT.ap[:, t0:t0 + ntile, s * 128:(s + 1) * 128], in_=pb[:, 0:ntile * 128].rearrange("p (k t) -> p k t", k=ntile)), [ptr], [OT])

            from functools import partial
            with ExitStack() as es3:
                P.es = es3
                KA = P.sb("KA", [128, S], BF16)
                VA = P.sb("VA", [128, 32, 129], BF16)
                QA = P.sb("QAbd", [128, NS, 2, 128], BF16)
                BMA = P.sb("BMA", [128, 6, 128], F32)
                V_(P, lambda e: e.memset(VA.ap[:, :, 128:129], 1.0), [], [VA])
                V_(P, lambda e: e.memset(QA.ap, 0.0), [], [QA])

                def epi_A(h, s, O0, O1, sm, a0, oo, otm_):
                    def st1():
                        V_(P, lambda e: e.reciprocal(out=sm.ap[:, 4:5], in_=O0.ap[:, 128:129]), [O0], [sm])
                        V_(P, lambda e: e.reciprocal(out=sm.ap[:, 5:6], in_=O1.ap[:, 128:129]), [O1], [sm])
                        V_(P, lambda e: e.tensor_tensor(out=sm.ap[:, 6:7], in0=sm.ap[:, 5:6], in1=consts.ap[:, 28:29], op=ALU.mult), [sm, consts], [sm])
                        V_(P, lambda e: e.tensor_scalar(out=a0.ap, in0=O0.ap[:, 0:128], scalar1=sm.ap[:, 4:5], scalar2=None, op0=ALU.mult), [O0, sm], [a0])
                        V_(P, lambda e: e.scalar_tensor_tensor(out=oo.ap, in0=O1.ap[:, 0:128], scalar=sm.ap[:, 6:7], in1=a0.ap, op0=ALU.mult, op1=ALU.add), [O1, sm, a0], [oo])
                        pipe.defer(1, st2)

                    def st2():
                        A_(P, lambda e: e.activation(out=a0.ap, in_=oo.ap, func=ACT.Square, accum_out=sm.ap[:, 7:8]), [oo], [a0, sm])
                        A_(P, lambda e: e.activation(out=sm.ap[:, 7:8], in_=sm.ap[:, 7:8], func=ACT.Sqrt, scale=1.0 / 128, bias=eps_t.ap[:, 0:1]), [sm, eps_t], [sm])
                        pipe.defer(1, st3)

                    def st3():
                        V_(P, lambda e: e.reciprocal(out=sm.ap[:, 7:8], in_=sm.ap[:, 7:8]), [sm], [sm])
                        V_(P, lambda e: e.scalar_tensor_tensor(out=otm_.ap[:, 0:128], in0=oo.ap, scalar=sm.ap[:, 7:8], in1=slg.ap, op0=ALU.mult, op1=ALU.mult), [oo, sm, slg], [otm_])
                        pipe.defer(1, partial(store_ot, s, h, 1, otm_))

                    pipe.defer(1, st1)

                ucnt = 0
                for h in range(4):
                    load_kT(P, KA, KA.ap, h)
                    load_V(P, VA, VA.ap[:, :, 0:128], h * 128, 128)
                    for c in range(2):
                        P.dma('sync', QA.ap[64 * c:64 * c + 64, :, c, :], qt[h][64 * c:64 * c + 64, :].rearrange("p (s q) -> p s q", q=128), [qt_tok], [QA])
                    P.dma('sync', BMA.ap, bma[h], [], [BMA])
                    for s in range(NS):
                        n = nproc(s)
                        kbs = near_far(n, 3)
                        O0, O1 = Ob[(2 * ucnt) % 4], Ob[(2 * ucnt + 1) % 4]
                        done = partial(epi_A, h, s, O0, O1, smr[ucnt % 8], a0s[ucnt % 4], oos[ucnt % 4], otms[ucnt % 4])
                        ucnt += 1
                        cf = consts.ap[:, h:h + 1]
                        attn_pair_p(pipe, P, Wk, QA.ap[:, s, :, :].rearrange("p j q -> p (j q)"), QA,
                                    lambda kb: KA.ap[:, kb * 128:(kb + 1) * 128], KA, kbs,
                                    lambda kb: VA.ap[:, kb, :], VA, 129, (O0, O1),
                                    lambda r: BMA.ap[:, (s % 2) * 3 + r, :].unsqueeze(1).broadcast_to([128, 2, 128]), BMA, (cf, cf), on_done=done)
                    pipe.flush()
            P.barrier()
            with ExitStack() as es3:
                P.es = es3
                KB = P.sb("KB", [128, S], BF16)
                VB = P.sb("VB", [128, 32, 65], BF16)
                QB = P.sb("QBbd", [128, 2, NS, 2, 128], BF16)
                BMB = P.sb("BMB", [128, 4, 6, 128], F32)
                V_(P, lambda e: e.memset(VB.ap[:, :, 64:65], 1.0), [], [VB])
                V_(P, lambda e: e.memset(QB.ap, 0.0), [], [QB])

                def epi_B(h, r, s, g, O, sm, otm_):
                    def st1():
                        V_(P, lambda e: e.tensor_tensor(out=sm.ap[:, 4:5], in0=O.ap[:, 64:65], in1=consts.ap[:, 20 + h:21 + h], op=ALU.add), [O, consts], [sm])
                        V_(P, lambda e: e.reciprocal(out=sm.ap[:, 4:5], in_=sm.ap[:, 4:5]), [sm], [sm])
                        V_(P, lambda e: e.tensor_scalar(out=otm_.ap[:, r * 64:(r + 1) * 64], in0=O.ap[:, 0:64], scalar1=sm.ap[:, 4:5], scalar2=None, op0=ALU.mult), [O, sm], [otm_])
                        if r == 3:
                            pipe.defer(1, partial(store_ot, s, 4 + 2 * g, 2, otm_))
                    pipe.defer(1, st1)

                ucnt = 0
                for g in range(2):
                    load_kT(P, KB, KB.ap, 4 + g)
                    load_V(P, VB, VB.ap[:, :, 0:64], 512 + g * 64, 64)
                    for pr_ in range(2):
                        for c in range(2):
                            P.dma('sync', QB.ap[64 * c:64 * c + 64, pr_, :, c, :], qt[4 + 2 * g + pr_][64 * c:64 * c + 64, :].rearrange("p (s q) -> p s q", q=128), [qt_tok], [QB])
                    P.dma('sync', BMB.ap, bmb[g], [], [BMB])
                    for s in range(NS):
                        n = nproc(s)
                        kbs = [(kb, kb - (n - 3)) for kb in range(max(0, n - 3), n)]
                        otm_ = otms[s % 4]
                        for pr_ in range(2):
                            Os = (Ob[(2 * ucnt) % 4], Ob[(2 * ucnt + 1) % 4])
                            d0 = partial(epi_B, 4 * g + 2 * pr_, 2 * pr_, s, g, Os[0], smr[(2 * ucnt) % 8], otm_)
                            d1 = partial(epi_B, 4 * g + 2 * pr_ + 1, 2 * pr_ + 1, s, g, Os[1], smr[(2 * ucnt + 1) % 8], otm_)
                            ucnt += 1

                            def done(d0=d0, d1=d1):
                                d0()
                                d1()
                            attn_pair_p(pipe, P, Wk, QB.ap[:, pr_, s, :, :].rearrange("p j q -> p (j q)"), QB,
                                        lambda kb: KB.ap[:, kb * 128:(kb + 1) * 128], KB, kbs,
                                        lambda kb: VB.ap[:, kb, :], VB, 65, Os,
                                        lambda rr: BMB.ap[:, 2 * pr_:2 * pr_ + 2, (s % 2) * 3 + rr, :], BMB, (None, None), on_done=done)
                    pipe.flush()
            P.barrier()
            emit_family_c(P, locals())
            P.barrier()
            P.es = es2
            emit_phase3a(P, locals())
            P.barrier()
        P.es = es
        emit_phase3b(P, locals())
        P.barrier()


def emit_phase3a(P, L):
    x_rows, x_tok, xmid, xmid_tok, wmg, wbr, wout, nm, OT, idn, eps_t, ps = (L[k] for k in ('x_rows', 'x_tok', 'xmid', 'xmid_tok', 'wmg', 'wbr', 'wout', 'nm', 'OT', 'idn', 'eps_t', 'ps'))
    with ExitStack() as es3:
        P.es = es3
        W = dict(idn=idn, eps=eps_t, pT=ps[6])
        W['sq1024'] = P.sb("sq1024", [128, 1024], F32)
        W['ss1'] = P.sb("ss1", [128, 1], F32)
        W['xb'] = P.sb("xb", [128, 1024], BF16)
        nm_t = P.sb("nm_t", [128, 8], F32)
        P.dma('sync', nm_t.ap, nm, [], [nm_t])
        hT = P.sb("hT", [128, 8, TOK], BF16)
        xs = [P.sb("xs%d" % i, [128, 1024], F32) for i in range(2)]
        for s in range(NS):
            b = s % 2
            P.dma('gpsimd', xs[b].ap, x_rows(s), [x_tok], [xs[b]])
            emit_xnorm_T(P, xs[b].ap, xs[b], hT, s, W)
        WB = P.sb("WB", [128, 12, 1024], BF16)
        WO = P.sb("WO", [128, 8, 1024], BF16)
        stg = [P.sb("stg%d" % i, [128, 2, 1024], F32) for i in range(2)]
        wbv = wbr.rearrange("n (mc p) d -> p (n mc) d", p=128)
        wov = wout.rearrange("(dc p) d -> p dc d", p=128)
        for i in range(10):
            st = stg[i % 2]
            if i < 6:
                P.dma('sync', st.ap, wbv[:, 2 * i:2 * i + 2, :], [], [st])
                dst = WB.ap[:, 2 * i:2 * i + 2, :]
                dbuf = WB
            else:
                P.dma('sync', st.ap, wov[:, 2 * (i - 6):2 * (i - 6) + 2, :], [], [st])
                dst = WO.ap[:, 2 * (i - 6):2 * (i - 6) + 2, :]
                dbuf = WO
            V_(P, lambda e: e.tensor_copy(out=dst[:, 0:1, :], in_=st.ap[:, 0:1, :]), [st], [dbuf])
            A_(P, lambda e: e.copy(out=dst[:, 1:2, :], in_=st.ap[:, 1:2, :]), [st], [dbuf])
        wg_s = [P.sb("wg_s0", [128, 8, 3, 128], F32)] * 2
        wg_b = [P.sb("wg_b%d" % i, [128, 8, 3, 128], BF16) for i in range(2)]
        zT = P.sb("zT", [128, 8, 512], BF16)
        Gt = [P.sb("Gt%d" % i, [128, 512], F32) for i in range(2)]
        zacc = P.sb("zacc", [128, 512], F32)
        ztmp = P.sb("ztmp", [128, 512], F32)
        xn = xs
        wgv = wmg.rearrange("(kc p) (n d) -> p kc n d", p=128, n=3)
        pg = ps[0:2]
        py = ps[2:4]
        po = ps[4:6]
        cnt = 0
        for T in range(4):
            ts = slice(T * 512, (T + 1) * 512)
            for dc in range(8):
                b = cnt % 2
                cnt += 1
                for n in range(3):
                    P.dma('sync', wg_s[b].ap[:, :, n, :], wgv[:, :, n, dc * 128:(dc + 1) * 128], [], [wg_s[b]])
                for kc in range(8):
                    if kc % 2 == 0:
                        V_(P, lambda e: e.tensor_scalar(out=wg_b[b].ap[:, kc], in0=wg_s[b].ap[:, kc], scalar1=nm_t.ap[:, kc:kc + 1], scalar2=None, op0=ALU.mult), [wg_s[b], nm_t], [wg_b[b]])
                    else:
                        A_(P, lambda e: e.activation(out=wg_b[b].ap[:, kc], in_=wg_s[b].ap[:, kc], func=ACT.Copy, scale=nm_t.ap[:, kc:kc + 1]), [wg_s[b], nm_t], [wg_b[b]])
                for n in range(3):
                    g_ps = pg[n % 2]
                    y_ps = py[n % 2]
                    G = Gt[n % 2]
                    for kc in range(8):
                        T_(P, lambda e: e.matmul(out=g_ps.ap, lhsT=wg_b[b].ap[:, kc, n, :], rhs=hT.ap[:, kc, ts], start=(kc == 0), stop=(kc == 7)), [wg_b[b], hT], [g_ps])
                    A_(P, lambda e: e.activation(out=G.ap, in_=g_ps.ap, func=ACT.Sigmoid), [g_ps], [G])
                    for mc in range(4):
                        T_(P, lambda e: e.matmul(out=y_ps.ap, lhsT=WB.ap[:, 4 * n + mc, dc * 128:(dc + 1) * 128], rhs=OT.ap[:, 4 * n + mc, ts], start=(mc == 0), stop=(mc == 3)), [WB, OT], [y_ps])
                    if n == 0:
                        V_(P, lambda e: e.tensor_tensor(out=zacc.ap, in0=G.ap, in1=y_ps.ap, op=ALU.mult), [G, y_ps], [zacc])
                    else:
                        V_(P, lambda e: e.tensor_tensor(out=ztmp.ap, in0=G.ap, in1=y_ps.ap, op=ALU.mult), [G, y_ps], [ztmp])
                        if n == 1:
                            V_(P, lambda e: e.tensor_tensor(out=zacc.ap, in0=zacc.ap, in1=ztmp.ap, op=ALU.add), [zacc, ztmp], [zacc])
                        else:
                            V_(P, lambda e: e.tensor_tensor(out=zT.ap[:, dc, :], in0=zacc.ap, in1=ztmp.ap, op=ALU.add), [zacc, ztmp], [zT])
            for si in range(4):
                s = T * 4 + si
                xb_ = xn[s % 2]
                P.dma('gpsimd', xb_.ap, x_rows(s), [x_tok], [xb_])
                for half in range(2):
                    o_ps = po[half]
                    for dc in range(8):
                        T_(P, lambda e: e.matmul(out=o_ps.ap, lhsT=zT.ap[:, dc, si * 128:(si + 1) * 128], rhs=WO.ap[:, dc, half * 512:(half + 1) * 512], start=(dc == 0), stop=(dc == 7)), [zT, WO], [o_ps])
                    V_(P, lambda e: e.tensor_tensor(out=xb_.ap[:, half * 512:(half + 1) * 512], in0=xb_.ap[:, half * 512:(half + 1) * 512], in1=o_ps.ap, op=ALU.add), [xb_, o_ps], [xb_])
                P.dma('sync', xmid[s * 128:(s + 1) * 128, :], xb_.ap, [xb_], [xmid_tok])


def emit_phase3b(P, L):
    xmid, xmid_tok, xout, xout_tok, wup, wdn, nmlp, idn, eps_t, ps = (L[k] for k in ('xmid', 'xmid_tok', 'xout', 'xout_tok', 'wup', 'wdn', 'nmlp', 'idn', 'eps_t', 'ps'))
    with ExitStack() as es3:
        P.es = es3
        W = dict(idn=idn, eps=eps_t, pT=ps[6])
        W['sq1024'] = P.sb("sq1024b", [128, 1024], F32)
        W['ss1'] = P.sb("ss1b", [128, 1], F32)
        W['xb'] = P.sb("xbb", [128, 1024], BF16)
        nm_t = P.sb("nmlp_t", [128, 8], F32)
        P.dma('sync', nm_t.ap, nmlp, [], [nm_t])
        X = P.sb("X", [128, NS, D], F32)
        hT = P.sb("hT2", [128, 8, TOK], BF16)
        P.dma('sync', X.ap, xmid.rearrange("(s p) d -> p s d", p=128), [xmid_tok], [X])
        for s in range(NS):
            emit_xnorm_T(P, X.ap[:, s, :], X, hT, s, W)
        wu_s = [P.sb("wu_s0", [128, 8, 512], F32)] * 2
        wu_b = [P.sb("wu_b%d" % i, [128, 8, 512], BF16) for i in range(2)]
        wd_s = [P.sb("wd_s0", [128, 4, 1024], F32)] * 2
        wd_b = [P.sb("wd_b%d" % i, [128, 4, 1024], BF16) for i in range(2)]
        aT = P.sb("aT", [128, 4, TOK], BF16)
        rl = [P.sb("rl%d" % i, [128, 512], F32) for i in range(2)]
        wuv = wup.rearrange("(kc p) f -> p kc f", p=128)
        wdv = wdn.rearrange("(fc p) d -> p fc d", p=128)
        pu = ps[0:2]
        pd = ps[2:4]

        def load(fg):
            b = fg % 2
            P.dma('sync', wu_s[b].ap, wuv[:, :, fg * 512:(fg + 1) * 512], [], [wu_s[b]])
            P.dma('gpsimd', wd_s[b].ap, wdv[:, fg * 4:(fg + 1) * 4, :], [], [wd_s[b]])

        load(0)
        cnt = 0
        for fg in range(8):
            b = fg % 2
            for kc in range(8):
                if kc % 2 == 0:
                    V_(P, lambda e: e.tensor_scalar(out=wu_b[b].ap[:, kc], in0=wu_s[b].ap[:, kc], scalar1=nm_t.ap[:, kc:kc + 1], scalar2=None, op0=ALU.mult), [wu_s[b], nm_t], [wu_b[b]])
                else:
                    A_(P, lambda e: e.activation(out=wu_b[b].ap[:, kc], in_=wu_s[b].ap[:, kc], func=ACT.Copy, scale=nm_t.ap[:, kc:kc + 1]), [wu_s[b], nm_t], [wu_b[b]])
            V_(P, lambda e: e.tensor_copy(out=wd_b[b].ap[:, 0:2], in_=wd_s[b].ap[:, 0:2]), [wd_s[b]], [wd_b[b]])
            A_(P, lambda e: e.copy(out=wd_b[b].ap[:, 2:4], in_=wd_s[b].ap[:, 2:4]), [wd_s[b]], [wd_b[b]])
            if fg + 1 < 8:
                load(fg + 1)
            for fc in range(4):
                for T in range(4):
                    u_ps = pu[cnt % 2]
                    r_ = rl[cnt % 2]
                    cnt += 1
                    for kc in range(8):
                        T_(P, lambda e: e.matmul(out=u_ps.ap, lhsT=wu_b[b].ap[:, kc, fc * 128:(fc + 1) * 128], rhs=hT.ap[:, kc, T * 512:(T + 1) * 512], start=(kc == 0), stop=(kc == 7)), [wu_b[b], hT], [u_ps])
                    A_(P, lambda e: e.activation(out=r_.ap, in_=u_ps.ap, func=ACT.Relu), [u_ps], [r_])
                    V_(P, lambda e: e.tensor_tensor(out=aT.ap[:, fc, T * 512:(T + 1) * 512], in0=r_.ap, in1=r_.ap, op=ALU.mult), [r_], [aT])
            for s in range(NS):
                for half in range(2):
                    d_ps = pd[half]
                    for fc in range(4):
                        T_(P, lambda e: e.matmul(out=d_ps.ap, lhsT=aT.ap[:, fc, s * 128:(s + 1) * 128], rhs=wd_b[b].ap[:, fc, half * 512:(half + 1) * 512], start=(fc == 0), stop=(fc == 3)), [aT, wd_b[b]], [d_ps])
                    V_(P, lambda e: e.tensor_tensor(out=X.ap[:, s, half * 512:(half + 1) * 512], in0=X.ap[:, s, half * 512:(half + 1) * 512], in1=d_ps.ap, op=ALU.add), [X, d_ps], [X])
        P.dma('sync', xout.rearrange("(s p) d -> p s d", p=128), X.ap, [X], [xout_tok])
        if L.get('xdbg') is not None:
            P.dma('sync', L['xdbg'].rearrange("(s p) d -> p s d", p=128), X.ap, [X], [])


def emit_family_c(P, L):
    qt, qt_tok, load_kT, load_V, w1, w2, posT, g5, bms, bmw, cm, selb, E_d, ov_d = (L[k] for k in
        ('qt', 'qt_tok', 'load_kT', 'load_V', 'w1', 'w2', 'posT', 'g5', 'bms', 'bmw', 'cm', 'selb', 'E_d', 'ov_d'))
    Wk, O_ps, ptr, pm, consts, eps_t, idn, cg_t, otm, sm, store_ot, ps = (L[k] for k in
        ('Wk', 'O_ps', 'ptr', 'pm', 'consts', 'eps_t', 'idn', 'cg_t', 'otm', 'sm', 'store_ot', 'ps'))
    ph = ps[6]
    with ExitStack() as es3:
        P.es = es3
        kcT = P.sb("kcT", [128, 2, 256], BF16)
        vca = P.sb("vca", [128, 2, 2, 129], BF16)
        V_(P, lambda e: e.memset(kcT.ap, 0.0), [], [kcT])
        V_(P, lambda e: e.memset(vca.ap, 0.0), [], [vca])
        V_(P, lambda e: e.memset(vca.ap[:, :, :, 64:65], 1.0), [], [vca])
        for g in range(2):
            P.dma('sync', vca.ap[:, g, :, 65:129], ov_d, [], [vca])
        with ExitStack() as es4:
            P.es = es4
            CKV = P.sb("CKV", [128, 2, S], BF16)
            load_kT(P, CKV, CKV.ap[:, 0, :], 6)
            load_kT(P, CKV, CKV.ap[:, 1, :], 7)
            W1s = P.sb("W1s", [128, 32, 256], F32)
            W1b = P.sb("W1b", [128, 32, 256], BF16)
            W2s = P.sb("W2s", [128, 2, 64], F32)
            W2b = P.sb("W2b", [128, 2, 64], BF16)
            pos_s = P.sb("pos_s", [128, 32], F32)
            pos_b = P.sb("pos_b", [128, 32], BF16)
            g5_t = P.sb("g5_t", [128, 64], F32)
            hb = P.sb("hb", [128, 1], F32)
            u = P.sb("u", [128, 256], F32)
            u2 = P.sb("u2", [128, 256], F32)
            gT = P.sb("gT", [128, 2, 256], BF16)
            kd = P.sb("kd", [128, 128], BF16)
            kf = P.sb("kf", [128, 64], F32)
            V_(P, lambda e: e.memset(gT.ap, 0.0), [], [gT])
            P.dma('sync', g5_t.ap, g5, [], [g5_t])
            for i in range(2):
                for half in range(2):
                    P.dma('sync', W1s.ap[64 * half:64 * half + 64], w1[i].rearrange("(l d) h -> d l h", d=64), [], [W1s])
                    P.dma('sync', pos_s.ap[64 * half:64 * half + 64], posT[i], [], [pos_s])
                P.dma('sync', W2s.ap, w2[i].rearrange("(c p) d -> p c d", p=128), [], [W2s])
                for q in range(4):
                    if q % 2 == 0:
                        V_(P, lambda e: e.tensor_copy(out=W1b.ap[:, q * 8:(q + 1) * 8, :], in_=W1s.ap[:, q * 8:(q + 1) * 8, :]), [W1s], [W1b])
                    else:
                        A_(P, lambda e: e.copy(out=W1b.ap[:, q * 8:(q + 1) * 8, :], in_=W1s.ap[:, q * 8:(q + 1) * 8, :]), [W1s], [W1b])
                V_(P, lambda e: e.tensor_copy(out=W2b.ap, in_=W2s.ap), [W2s], [W2b])
                V_(P, lambda e: e.tensor_copy(out=pos_b.ap, in_=pos_s.ap), [pos_s], [pos_b])
                for g in range(2):
                    r0 = 64 * g
                    src3 = CKV.ap[r0:r0 + 64, i, :].rearrange("p (c s) -> p c s", s=16)
                    for hc in range(2):
                        for l in range(32):
                            T_(P, lambda e: e.matmul(out=pm.ap[:, 0:1], lhsT=W1b.ap[r0:r0 + 64, l, hc * 128:(hc + 1) * 128], rhs=pos_b.ap[r0:r0 + 64, l:l + 1],
                                                     start=(l == 0), stop=(l == 31)), [W1b, pos_b], [pm])
                        A_(P, lambda e: e.copy(out=hb.ap, in_=pm.ap[:, 0:1]), [pm], [hb])
                        for l in range(32):
                            rhs = src3[:, 0:255, l] if l < 16 else src3[:, 1:256, l - 16]
                            T_(P, lambda e: e.matmul(out=ph.ap[:, 0:255], lhsT=W1b.ap[r0:r0 + 64, l, hc * 128:(hc + 1) * 128], rhs=rhs,
                                                     start=(l == 0), stop=(l == 31)), [W1b, CKV], [ph])
                        A_(P, lambda e: e.activation(out=u.ap[:, 0:255], in_=ph.ap[:, 0:255], func=ACT.Identity, bias=hb.ap[:, 0:1]), [ph, hb], [u])
                        V_(P, lambda e: e.tensor_tensor(out=u2.ap[:, 0:255], in0=u.ap[:, 0:255], in1=u.ap[:, 0:255], op=ALU.mult), [u], [u2])
                        V_(P, lambda e: e.tensor_scalar(out=u2.ap[:, 0:255], in0=u2.ap[:, 0:255], scalar1=0.044715, scalar2=1.0, op0=ALU.mult, op1=ALU.add), [u2], [u2])
                        V_(P, lambda e: e.tensor_tensor(out=u2.ap[:, 0:255], in0=u2.ap[:, 0:255], in1=u.ap[:, 0:255], op=ALU.mult), [u2, u], [u2])
                        A_(P, lambda e: e.activation(out=u2.ap[:, 0:255], in_=u2.ap[:, 0:255], func=ACT.Tanh, scale=0.7978845608028654), [u2], [u2])
                        V_(P, lambda e: e.tensor_scalar(out=u2.ap[:, 0:255], in0=u2.ap[:, 0:255], scalar1=1.0, scalar2=0.5, op0=ALU.add, op1=ALU.mult), [u2], [u2])
                        V_(P, lambda e: e.tensor_tensor(out=gT.ap[:, hc, 0:255], in0=u2.ap[:, 0:255], in1=u.ap[:, 0:255], op=ALU.mult), [u2, u], [gT])
                    for t in range(2):
                        cn = 128 if t == 0 else 127
                        for hc in range(2):
                            T_(P, lambda e: e.matmul(out=pm.ap[0:cn, 0:64], lhsT=gT.ap[:, hc, t * 128:t * 128 + cn], rhs=W2b.ap[:, hc, :],
                                                     start=(hc == 0), stop=(hc == 1)), [gT, W2b], [pm])
                        if i == 1:
                            A_(P, lambda e: e.copy(out=vca.ap[0:cn, g, t, 0:64], in_=pm.ap[0:cn, 0:64]), [pm], [vca])
                        else:
                            A_(P, lambda e: e.copy(out=kf.ap[0:cn, :], in_=pm.ap[0:cn, 0:64]), [pm], [kf])
                            A_(P, lambda e: e.activation(out=u.ap[0:cn, 0:64], in_=kf.ap[0:cn, :], func=ACT.Square, accum_out=sm.ap[0:cn, 8:9]), [kf], [u, sm])
                            A_(P, lambda e: e.activation(out=sm.ap[0:cn, 8:9], in_=sm.ap[0:cn, 8:9], func=ACT.Sqrt, scale=1.0 / 64, bias=eps_t.ap[0:cn, 0:1]), [sm, eps_t], [sm])
                            V_(P, lambda e: e.reciprocal(out=sm.ap[0:cn, 8:9], in_=sm.ap[0:cn, 8:9]), [sm], [sm])
                            V_(P, lambda e: e.memset(kd.ap, 0.0), [], [kd])
                            for c in range(2):
                                V_(P, lambda e: e.scalar_tensor_tensor(out=kd.ap[0:cn, c * 64:(c + 1) * 64], in0=kf.ap[0:cn, :], scalar=sm.ap[0:cn, 8:9], in1=g5_t.ap[0:cn, :],
                                                                       op0=ALU.mult, op1=ALU.mult), [kf, sm, g5_t], [kd])
                            pb = ptr.ap.bitcast(BF16)
                            T_(P, lambda e: e.transpose(out=pb[:, 0:128], in_=kd.ap, identity=idn.ap), [kd, idn], [ptr])
                            A_(P, lambda e: e.copy(out=kcT.ap[:, g, t * 128:t * 128 + cn], in_=pb[:, 0:cn]), [ptr], [kcT])
        P.barrier()
        P.es = es3
        from functools import partial
        pipe, Ob, smr, otms = L['pipe'], L['Ob'], L['smr'], L['otms']
        CM = P.sb("CM", [128, 2, NS, 128], F32)
        SELB = P.sb("SELB", [128, NS, 64], F32)
        Et = P.sb("Et", [128, S], BF16)
        P.dma('sync', CM.ap, cm, [], [CM])
        P.dma('sync', SELB.ap, selb, [], [SELB])
        P.dma('sync', Et.ap[0:64], E_d, [], [Et])
        P.dma('sync', Et.ap[64:128], E_d, [], [Et])
        KS = P.sb("KS", [128, S], BF16)
        KW = P.sb("KW", [128, S], BF16)
        VS = P.sb("VS", [128, 32, 65], BF16)
        VW = P.sb("VW", [128, 32, 65], BF16)
        QC = P.sb("QC", [128, 2, TOK], BF16)
        BMS = P.sb("BMS", [128, 4, 6, 128], F32)
        BMW = P.sb("BMW", [128, 4, 12, 128], F32)
        accs = [P.sb("acc%d" % i, [128, 4, 64], F32) for i in range(2)]
        imps = [P.sb("imp%d" % i, [128, 64], F32) for i in range(2)]
        scs = [P.sb("sc%d" % i, [128, 64], F32) for i in range(2)]
        sc2s = [P.sb("sc2_%d" % i, [128, 64], F32) for i in range(2)]
        m8s = [P.sb("m8_%d" % i, [128, 8], F32) for i in range(2)]
        selnbs = [P.sb("selnb%d" % i, [128, 128], BF16) for i in range(2)]
        selTs = [P.sb("selT%d" % i, [128, 128], BF16) for i in range(2)]
        V_(P, lambda e: e.memset(VS.ap[:, :, 64:65], 1.0), [], [VS])
        V_(P, lambda e: e.memset(VW.ap[:, :, 64:65], 1.0), [], [VW])

        def selT_store(selnb, selT):
            pb = ptr.ap.bitcast(BF16)
            T_(P, lambda e: e.transpose(out=pb[:, 0:128], in_=selnb.ap, identity=idn.ap), [selnb, idn], [ptr])
            A_(P, lambda e: e.copy(out=selT.ap, in_=pb[:, 0:128]), [ptr], [selT])

        def epi_cmp(h, r, s, O, sm, acc, imp, sc, sc2, m8, selnb, selT, state):
            V_(P, lambda e: e.tensor_tensor(out=sm.ap[:, 4:5], in0=O.ap[:, 64:65], in1=consts.ap[:, 31:32], op=ALU.add), [O, consts], [sm])
            V_(P, lambda e: e.reciprocal(out=sm.ap[:, 4:5], in_=sm.ap[:, 4:5]), [sm], [sm])
            V_(P, lambda e: e.tensor_tensor(out=sm.ap[:, 5:6], in0=sm.ap[:, 4:5], in1=cg_t.ap[:, s, 3 * h:3 * h + 1], op=ALU.mult), [sm, cg_t], [sm])
            V_(P, lambda e: e.tensor_scalar(out=acc.ap[:, r, :], in0=O.ap[:, 0:64], scalar1=sm.ap[:, 5:6], scalar2=None, op0=ALU.mult), [O, sm], [acc])
            if r == 0:
                V_(P, lambda e: e.tensor_scalar(out=imp.ap, in0=O.ap[:, 65:129], scalar1=sm.ap[:, 4:5], scalar2=None, op0=ALU.mult), [O, sm], [imp])
            else:
                V_(P, lambda e: e.scalar_tensor_tensor(out=imp.ap, in0=O.ap[:, 65:129], scalar=sm.ap[:, 4:5], in1=imp.ap, op0=ALU.mult, op1=ALU.add), [O, sm, imp], [imp])
            if r == 3:
                V_(P, lambda e: e.tensor_tensor(out=sc.ap, in0=imp.ap, in1=SELB.ap[:, s, :], op=ALU.add), [imp, SELB], [sc])
                V_(P, lambda e: e.max(out=m8.ap, in_=sc.ap), [sc], [m8])
                V_(P, lambda e: e.match_replace(out=sc2.ap, in_to_replace=m8.ap, in_values=sc.ap, imm_value=-3.0e9), [m8, sc], [sc2])
                V_(P, lambda e: e.max(out=m8.ap, in_=sc2.ap), [sc2], [m8])
                for c in range(2):
                    V_(P, lambda e: e.tensor_scalar(out=selnb.ap[:, c * 64:(c + 1) * 64], in0=sc.ap, scalar1=m8.ap[:, 7:8], scalar2=NEG, op0=ALU.is_lt, op1=ALU.mult), [sc, m8], [selnb])

                def fire():
                    selT_store(selnb, selT)
                    state['selT_ready'] = True
                pipe.defer(2, fire)

        def epi_br(h, r, s, g, br, O, sm, acc, otm_, last):
            V_(P, lambda e: e.reciprocal(out=sm.ap[:, 4:5], in_=O.ap[:, 64:65]), [O], [sm])
            V_(P, lambda e: e.tensor_tensor(out=sm.ap[:, 5:6], in0=sm.ap[:, 4:5], in1=cg_t.ap[:, s, 3 * h + 1 + br:3 * h + 2 + br], op=ALU.mult), [sm, cg_t], [sm])
            V_(P, lambda e: e.scalar_tensor_tensor(out=acc.ap[:, r, :], in0=O.ap[:, 0:64], scalar=sm.ap[:, 5:6], in1=acc.ap[:, r, :], op0=ALU.mult, op1=ALU.add), [O, sm, acc], [acc])
            if last:
                V_(P, lambda e: e.tensor_copy(out=otm_.ap[:, 0:256], in_=acc.ap.rearrange("p r d -> p (r d)")), [acc], [otm_])
                pipe.defer(2, partial(store_ot, s, 8 + 2 * g, 2, otm_))

        def push_cmp(g, s, r, nct, O, done):
            p0 = 64 * (r % 2)
            qap = QC.ap[p0:p0 + 64, r // 2, s * 128:(s + 1) * 128]
            k = Wk['scnt']
            Wk['scnt'] += 1
            Sb = Wk['sps'][k % len(Wk['sps'])]
            PT = Wk['pts'][k % len(Wk['pts'])]
            tmp = Wk['stmps'][Wk['tcnt'] % len(Wk['stmps'])]
            Wk['tcnt'] += 1

            def s_fn():
                for t in range(nct):
                    T_(P, lambda e: e.matmul(out=Sb.ap[:, t * 128:(t + 1) * 128], lhsT=kcT.ap[p0:p0 + 64, g, t * 128:(t + 1) * 128], rhs=qap, start=True, stop=True), [kcT, QC], [Sb])

            def e_fn():
                for t in range(nct):
                    V_(P, lambda e: e.tensor_tensor(out=tmp.ap[:, t * 128:(t + 1) * 128], in0=Sb.ap[:, t * 128:(t + 1) * 128], in1=CM.ap[:, t, s, :], op=ALU.add), [Sb, CM], [tmp])
                A_(P, lambda e: e.activation(out=PT.ap[:, 0:nct * 128], in_=tmp.ap[:, 0:nct * 128], func=ACT.Exp), [tmp], [PT])

            def pv_fn():
                for t in range(nct):
                    T_(P, lambda e: e.matmul(out=O.ap[:, 0:129], lhsT=PT.ap[:, t * 128:(t + 1) * 128], rhs=vca.ap[:, g, t, :], start=(t == 0), stop=(t == nct - 1)), [PT, vca], [O])
                done()

            pipe.push(s_fn, e_fn, pv_fn)

        ucnt = 0
        gs = 0
        for g in range(2):
            load_kT(P, KS, KS.ap, 8 + g)
            load_kT(P, KW, KW.ap, 10 + g)
            load_V(P, VS, VS.ap[:, :, 0:64], 640 + g * 64, 64)
            load_V(P, VW, VW.ap[:, :, 0:64], 768 + g * 64, 64)
            P.dma('sync', QC.ap, qt[8 + 2 * g:10 + 2 * g].rearrange("k p t -> p k t"), [qt_tok], [QC])
            P.dma('sync', BMS.ap, bms[g], [], [BMS])
            P.dma('sync', BMW.ap, bmw[g], [], [BMW])
            for s in range(NS):
                n = nproc(s)
                nct = 1 if 8 * n <= 128 else 2
                i2 = gs % 2
                gs += 1
                acc, imp, sc, sc2, m8, selnb, selT, otm_ = accs[i2], imps[i2], scs[i2], sc2s[i2], m8s[i2], selnbs[i2], selTs[i2], otms[gs % 4]
                state = dict(selT_ready=False)
                for r in range(4):
                    h = 4 * g + r
                    O = Ob[ucnt % 4]
                    done = partial(epi_cmp, h, r, s, O, smr[ucnt % 8], acc, imp, sc, sc2, m8, selnb, selT, state)
                    ucnt += 1
                    push_cmp(g, s, r, nct, O, done)
                for r in range(4):
                    h = 4 * g + r
                    p0 = 64 * (r % 2)
                    qap = QC.ap[p0:p0 + 64, r // 2, s * 128:(s + 1) * 128]
                    O = Ob[ucnt % 4]
                    done = partial(epi_br, h, r, s, g, 1, O, smr[ucnt % 8], acc, otm_, False)
                    ucnt += 1
                    kbs = [(kb, kb - (n - 6)) for kb in range(max(0, n - 6), n)]
                    attn_unit_p(pipe, P, Wk, qap, QC, lambda kb: KW.ap[p0:p0 + 64, kb * 128:(kb + 1) * 128], KW, kbs,
                                lambda kb: VW.ap[:, kb, :], VW, 65, O, lambda rr: BMW.ap[:, r, (s % 2) * 6 + rr, :], BMW, None, on_done=done)
                assert state['selT_ready'], "selection mask must be emitted before the selected-block branch"
                for r in range(4):
                    h = 4 * g + r
                    p0 = 64 * (r % 2)
                    qap = QC.ap[p0:p0 + 64, r // 2, s * 128:(s + 1) * 128]
                    O = Ob[ucnt % 4]
                    done = partial(epi_br, h, r, s, g, 0, O, smr[ucnt % 8], acc, otm_, r == 3)
                    ucnt += 1
                    attn_unit_p(pipe, P, Wk, qap, QC, lambda kb: KS.ap[p0:p0 + 64, kb * 128:(kb + 1) * 128], KS, near_far(n, 3),
                                lambda kb: VS.ap[:, kb, :], VS, 65, O, lambda rr: BMS.ap[:, r, (s % 2) * 3 + rr, :], BMS,
                                consts.ap[:, 12 + h:13 + h], sel=(Et, selT.ap[p0:p0 + 64, :], selT, p0), on_done=done)
            pipe.flush()


RANK_SLOT_C = {(0, 0): 0, (0, 1): 3, (1, 0): 1, (1, 1): 2}
PAIRS = [[0, 1], [2, 3], [4, 5], [6, 7]]
NL = 4
SAME_ENGINE_SYNC = True


def build_fused(n_layers=NL):
    nc = bass.Bass("TRN2", target_bir_lowering=False)

    def din(name, shape, dt=F32):
        return nc.dram_tensor(name, list(shape), dt, kind="ExternalInput").ap()

    x = din("x", [TOK, D])
    w_in = din("w_in", [NL, D, 6680])
    wbr = din("wbr", [NL, 3, 512, D])
    wout = din("wout", [NL, D, D])
    wup = din("wup", [NL, D, 4096])
    wdn = din("wdn", [NL, 4096, D])
    nm = din("nm", [NL, 128, 8])
    nmlp = din("nmlp", [NL, 128, 8])
    gains = din("gains", [NL, 128, 8, 64])
    w1 = din("w1", [NL, 2, 2048, 256])
    w2 = din("w2", [NL, 2, 256, 64])
    posT = din("posT", [NL, 2, 64, 32])
    dl = din("dl", [NL, 128, 4, 64])
    lamc = din("lamc", [NL, 128, 2])
    subln = din("subln", [NL, 128, 128])
    sinks = din("sinks", [NL, 128, 8])
    cfar = din("cfar", [128, 20])
    bma = din("bma", [4, 128, 6, 128])
    bmb = din("bmb", [2, 128, 4, 6, 128])
    bms = din("bms", [2, 128, 4, 6, 128])
    bmw = din("bmw", [2, 128, 4, 12, 128])
    cm = din("cm", [128, 2, NS, 128])
    selb = din("selb", [128, NS, 64])
    E_d = din("E", [64, S], BF16)
    ov_d = din("ov", [128, 2, 64], BF16)
    idn_d = din("idn", [128, 128], BF16)
    xo = nc.dram_tensor("xo", [TOK, D], F32, kind="ExternalOutput").ap()
    xdbg_all = nc.dram_tensor("xdbg", [NL, TOK, D], F32, kind="ExternalOutput").ap() if _DBG else None
    qt_d = nc.dram_tensor("qt_d", [12, 128, TOK], BF16).ap()
    cgs_d = nc.dram_tensor("cgs_d", [TOK, 24], F32).ap()
    xmid = nc.dram_tensor("xmid", [TOK, D], F32).ap()
    xbuf = [nc.dram_tensor("xbuf%d" % i, [TOK, D], F32).ap() for i in range(2)]
    exK = [[nc.dram_tensor("exK%d_%d" % (i, q), [4 * 128, TOK], BF16).ap() for q in range(3)] for i in range(2)]
    exV = [[nc.dram_tensor("exV%d_%d" % (i, u), [TOK // 2, 896], BF16).ap() for u in range(2)] for i in range(2)]
    exKo = [[nc.dram_tensor("exKo%d_%d" % (i, q), [2 * 4 * 128, TOK], BF16).ap() for q in range(3)] for i in range(2)]
    exVo = [[nc.dram_tensor("exVo%d_%d" % (i, u), [TOK, 896], BF16).ap() for u in range(2)] for i in range(2)]

    with ExitStack() as es:
        P = Prog(nc, es, same_engine_sync=SAME_ENGINE_SYNC)
        idn = P.sb("idn_t", [128, 128], BF16)
        eps_t = P.sb("eps_t", [128, 1], F32)
        ps = [P.ps("ps%d" % i) for i in range(8)]
        P.dma('sync', idn.ap, idn_d, [], [idn])
        V_(P, lambda e: e.memset(eps_t.ap, EPS), [], [eps_t])
        toks = dict(qt=Tok("qt"), cgs=Tok("cgs"), xmid=Tok("xmid"), xb=[Tok("xb0"), Tok("xb1")], xo=Tok("xo"), x=Tok("x"),
                    exK=[[Tok("exK") for q in range(3)] for i in range(2)], exV=[[Tok("exV") for u in range(2)] for i in range(2)],
                    exKo=[[Tok("exKo") for q in range(3)] for i in range(2)], exVo=[[Tok("exVo") for u in range(2)] for i in range(2)])
        for l in range(n_layers):
            par = l % 2
            if l == 0:
                xsrc, xsrc_tok = x, toks['x']
            else:
                xsrc, xsrc_tok = xbuf[(l - 1) % 2], toks['xb'][(l - 1) % 2]
            if l == n_layers - 1:
                xdst, xdst_tok = xo, toks['xo']
            else:
                xdst, xdst_tok = xbuf[l % 2], toks['xb'][l % 2]
            Ko, Vo = exKo[par], exVo[par]
            Ko_tok, Vo_tok = toks['exKo'][par], toks['exVo'][par]

            def load_kT(P, buf, dst_ap, tile, Ko=Ko, Ko_tok=Ko_tok):
                d4 = dst_ap.rearrange("p (m c t) -> p m c t", c=4, t=128)
                kq, kr = divmod(tile, 4)
                for rank in range(2):
                    src = Ko[kq][rank * 512 + kr * 128:rank * 512 + (kr + 1) * 128, :].rearrange("p (m o t) -> p m o t", o=2, t=128)
                    for odd in range(2):
                        P.dma('sync', d4[:, :, RANK_SLOT_C[(rank, odd)], :], src[:, :, odd, :], [Ko_tok[kq]], [buf])

            def load_V(P, buf, dst_ap3, col0, ncols, Vo=Vo, Vo_tok=Vo_tok):
                d4 = dst_ap3.rearrange("p (m c) v -> p m c v", c=4)
                for u in range(2):
                    for rank in range(2):
                        src = Vo[u][rank * 1024:(rank + 1) * 1024, col0:col0 + ncols].rearrange("(m o p) v -> p m o v", o=2, p=128)
                        for odd in range(2):
                            P.dma('sync', d4[:, 4 * u:4 * u + 4, RANK_SLOT_C[(rank, odd)], :], src[:, :, odd, :], [Vo_tok[u]], [buf])

            c = dict(idn=idn, eps_t=eps_t, ps=ps,
                     x_rows=(lambda s, xsrc=xsrc: xsrc[s * 128:(s + 1) * 128, :]), x_tok=xsrc_tok,
                     wv=w_in[l].rearrange("(kc p) c -> p kc c", p=128), nm=nm[l], gains=gains[l],
                     qt_d=qt_d, qt_tok=toks['qt'], exK3=[a.rearrange("(k p) t -> k p t", p=128) for a in exK[par]], exK_tok=toks['exK'][par],
                     exV=exV[par], exV_tok=toks['exV'][par], cgs_d=cgs_d, cgs_tok=toks['cgs'],
                     wmg=w_in[l][:, C1:6680], wbr=wbr[l], wout=wout[l], wup=wup[l], wdn=wdn[l], nmlp=nmlp[l],
                     w1=w1[l], w2=w2[l], posT=posT[l], g5=gains[l][:, 5, :], dl=dl[l], lamc=lamc[l], subln=subln[l], sinks=sinks[l],
                     cfar=cfar, bma=bma, bmb=bmb, bms=bms, bmw=bmw, cm=cm, selb=selb, E_d=E_d, ov_d=ov_d,
                     xdbg=(xdbg_all[l] if _DBG else None), xmid=xmid, xmid_tok=toks['xmid'], xout=xdst, xout_tok=xdst_tok, load_kT=load_kT, load_V=load_V)
            emit_phase1(P, c)
            P.barrier()
            for q in range(3):
                P.collective(exK[par][q], Ko[q], [toks['exK'][par][q]], [Ko_tok[q]], PAIRS)
            for u in range(2):
                P.collective(exV[par][u], Vo[u], [toks['exV'][par][u]], [Vo_tok[u]], PAIRS)
            emit_phase23(P, c)
        P.es = es
        P.finish()
    return nc


BF = ml_dtypes.bfloat16


def _bucket(dist):
    n = np.maximum(dist, 0)
    nf = np.maximum(n, 1).astype(np.float32)
    large = 16 + (np.log(nf / np.float32(16)) / np.float32(np.log(8.0)) * np.float32(16)).astype(np.int32)
    large = np.minimum(large, 31)
    return np.where(n < 16, n, large)


def _bm_tile(rel_bias, col, delta, window):
    k = np.arange(128)[:, None]
    q = np.arange(128)[None, :]
    dist = delta * 128 + q - k
    vis = dist >= 0
    if window is not None:
        vis = vis & (dist < window)
    vals = rel_bias[_bucket(dist), col].astype(np.float32)
    return np.where(vis, vals, np.float32(NEG)).astype(np.float32)


def _bm_table(rel_bias, j, cols, nnear, window):
    out = np.zeros((len(cols), 128, 2 * nnear, 128), np.float32)
    for par in range(2):
        for r in range(nnear):
            delta = qi_of(j, par) - nproc(par) + nnear - r
            for ci, col in enumerate(cols):
                out[ci, :, par * nnear + r, :] = _bm_tile(rel_bias, col, delta, window)
    return out


def _core_tables(rel_bias, j):
    t = {}
    t['bma'] = _bm_table(rel_bias, j, list(range(0, 4)), 3, None)
    bmb = _bm_table(rel_bias, j, list(range(4, 12)), 3, 128)
    t['bmb'] = np.ascontiguousarray(bmb.reshape(2, 4, 128, 6, 128).transpose(0, 2, 1, 3, 4))
    bms = _bm_table(rel_bias, j, list(range(12, 20)), 3, None)
    t['bms'] = np.ascontiguousarray(bms.reshape(2, 4, 128, 6, 128).transpose(0, 2, 1, 3, 4))
    bmw = _bm_table(rel_bias, j, list(range(12, 20)), 6, 512)
    t['bmw'] = np.ascontiguousarray(bmw.reshape(2, 4, 128, 12, 128).transpose(0, 2, 1, 3, 4))
    cm = np.zeros((128, 2, NS, 128), np.float32)
    selb = np.zeros((128, NS, 64), np.float32)
    for s in range(NS):
        qi = qi_of(j, s)
        tq = 128 * qi + np.arange(128)
        for tl in range(2):
            c = tl * 128 + np.arange(128)
            vis = (16 * c[:, None] + 31 <= tq[None, :]) & (c[:, None] < 255)
            cm[:, tl, s, :] = np.where(vis, 0.0, NEG)
        jb = np.arange(64)[None, :]
        valid = 64 * jb <= tq[:, None]
        cur = tq[:, None] // 64
        forced = (jb == 0) | (jb == cur) | (jb == cur - 1)
        selb[:, s, :] = np.where(valid, np.where(forced, 1.0e4, 0.0), -1.0e9)
    t['cm'] = cm
    t['selb'] = selb
    return t


_CONST = {}


def _consts():
    if not _CONST:
        tt = np.arange(S)
        _CONST['E'] = (np.arange(64)[:, None] == (tt[None, :] // 64)).astype(BF)
        c = (np.arange(2)[None, :, None] * 128 + np.arange(128)[:, None, None])
        jb = np.arange(64)[None, None, :]
        ov = (16 * c < 64 * jb + 64) & (16 * c + 32 > 64 * jb) & (c < 255)
        _CONST['ov'] = ov.astype(BF)
        _CONST['idn'] = np.eye(128).astype(BF)
    return _CONST


_PROGS = {}


def _rep(v):
    v = np.asarray(v, np.float32)
    return np.ascontiguousarray(np.broadcast_to(v[None], (128,) + v.shape))


def _repl(v):
    v = np.asarray(v, np.float32)
    return np.ascontiguousarray(np.broadcast_to(v[:, None], (v.shape[0], 128) + v.shape[1:]))


def kernel(x, w_in, qk_gain, diff_lambda, diff_subln, sinks, cmp_pos, cmp_w1, cmp_w2,
           w_branch, w_out, norm_mix, norm_mlp, w_up, w_down, rel_bias, _n_layers=NL):
    import math
    x = np.asarray(x, np.float32)
    rel_bias = np.asarray(rel_bias, np.float32)
    if ('fused', _n_layers) not in _PROGS:
        _PROGS[('fused', _n_layers)] = build_fused(_n_layers)
    cst = _consts()
    cores = [(b, j) for b in range(4) for j in range(2)]
    tabs = [_core_tables(rel_bias, j) for j in range(2)]
    f32 = lambda a: np.ascontiguousarray(np.asarray(a, np.float32))
    lamc = np.array([[0.8 - 0.6 * math.exp(-0.3 * l), 1.0 - (0.8 - 0.6 * math.exp(-0.3 * l))] for l in range(NL)], np.float32)
    shared = dict(
        w_in=f32(w_in), wbr=f32(w_branch), wout=f32(w_out), wup=f32(w_up), wdn=f32(w_down),
        nm=np.ascontiguousarray(f32(norm_mix).reshape(NL, 8, 128).transpose(0, 2, 1)),
        nmlp=np.ascontiguousarray(f32(norm_mlp).reshape(NL, 8, 128).transpose(0, 2, 1)),
        gains=_repl(qk_gain), w1=f32(cmp_w1), w2=f32(cmp_w2),
        posT=np.ascontiguousarray(f32(cmp_pos).transpose(0, 1, 3, 2)),
        dl=_repl(diff_lambda), lamc=_repl(lamc), subln=_repl(diff_subln), sinks=_repl(sinks),
        cfar=_rep(rel_bias[31]), E=cst['E'], ov=cst['ov'], idn=cst['idn'])
    in_maps = []
    for (b, j) in cores:
        d = dict(shared)
        d['x'] = np.concatenate([x[b, qi_of(j, s) * 128:(qi_of(j, s) + 1) * 128] for s in range(NS)], 0)
        d.update(tabs[j])
        in_maps.append(d)
    res = run_bass_kernel_spmd(_PROGS[('fused', _n_layers)], in_maps, core_ids=list(range(8))).results
    if _DBG:
        _DBG_OUT['res'] = res
    out = np.zeros((4, S, D), np.float32)
    for i, (b, j) in enumerate(cores):
        xo = np.asarray(res[i]['xo'])
        for s in range(NS):
            qi = qi_of(j, s)
            out[b, qi * 128:(qi + 1) * 128] = xo[s * 128:(s + 1) * 128]
    return out
```

```python
import numpy as np
import ml_dtypes
from contextlib import ExitStack
import concourse.bass as bass
import concourse.mybir as mybir
from concourse.bass_utils import run_bass_kernel_spmd

F32 = mybir.dt.float32
BF16 = mybir.dt.bfloat16
ALU = mybir.AluOpType
ACT = mybir.ActivationFunctionType
AX = mybir.AxisListType

ENGINES = ('tensor', 'vector', 'scalar', 'gpsimd', 'sync')
NDMA_SEMS = 24
SEM_EPOCH = 30000


class Buf:
    __slots__ = ('ap', 'w', 'r', 'name')

    def __init__(self, ap, name):
        self.ap = ap
        self.w = None
        self.r = {}
        self.name = name


class Tok(Buf):
    def __init__(self, name=''):
        Buf.__init__(self, None, name)


class _Rec:
    def __init__(self):
        self.name = None

    def __getattr__(self, name):
        def f(*args, **kwargs):
            self.name = name
            self.args = args
            self.kwargs = kwargs
            return self
        return f


class Prog:
    def __init__(self, nc, es, same_engine_sync=True):
        self.nc = nc
        self.es = es
        self.sem_es = es
        self.streams = {e: [] for e in ENGINES}
        self.cur = {}
        self.waited = {e: {} for e in ENGINES}
        self.nsem = 0
        self.dma_sems = []
        self.dma_pools = {}
        self.dma_rrs = {}
        self.same_engine_sync = same_engine_sync
        self.n_ops = 0

    def new_sem(self, name):
        s = self.sem_es.enter_context(self.nc.semaphore(name))
        sid = self.nsem
        self.nsem += 1
        return s, sid

    def sb(self, name, shape, dtype):
        self.n_sb = getattr(self, 'n_sb', 0) + 1
        t = self.es.enter_context(self.nc.sbuf_tensor("%s_u%d" % (name, self.n_sb), list(shape), dtype))
        return Buf(t.ap(), name)

    def ps(self, name):
        t = self.es.enter_context(self.nc.psum_tensor(name, [128, 512], F32))
        return Buf(t.ap(), name)

    def _event(self, e):
        c = self.cur.get(e)
        if c is None or c[1] >= SEM_EPOCH:
            s, sid = self.new_sem("s_%s_%d" % (e, self.nsem))
            c = [s, 0, sid]
            self.cur[e] = c
        c[1] += 1
        return (c[0], c[1], c[2], e)

    def _wait(self, e, ev):
        sem, val, sid, src = ev
        w = self.waited[e]
        if w.get(sid, 0) >= val:
            return
        w[sid] = val
        self.streams[e].append(('w', sem, val))

    def _deps(self, e, reads, writes):
        for t in reads:
            if t.w is not None:
                self._dep1(e, t.w)
        for t in writes:
            if t.w is not None:
                self._dep1(e, t.w)
            for ev in t.r.values():
                self._dep1(e, ev)

    def _dep1(self, e, ev):
        if ev[3] == e and (e == 'tensor' or not self.same_engine_sync):
            return
        self._wait(e, ev)

    def _mark(self, ev, reads, writes):
        for t in reads:
            t.r[ev[2]] = ev
        for t in writes:
            t.w = ev
            t.r = {}

    def op(self, e, fn, reads=(), writes=()):
        rec = _Rec()
        fn(rec)
        self._deps(e, reads, writes)
        ev = self._event(e)
        self.streams[e].append(('i', rec, ev[0]))
        self._mark(ev, reads, writes)
        self.n_ops += 1
        return ev

    def dma(self, e, out, in_, reads=(), writes=()):
        pool = self.dma_pools.setdefault(e, [])
        if not pool:
            for i in range(NDMA_SEMS if e == 'sync' else 8):
                s, sid = self.new_sem("s_dma_%s_%d" % (e, i))
                ent = [s, 0, sid]
                pool.append(ent)
                self.dma_sems.append(ent)
        self.dma_rrs[e] = self.dma_rrs.get(e, 0) + 1
        ds = pool[self.dma_rrs[e] % len(pool)]
        if ds[1] > 0:
            self._wait(e, (ds[0], ds[1], ds[2], 'dma'))
        self._deps(e, reads, writes)
        ds[1] += 16
        ev = (ds[0], ds[1], ds[2], 'dma')
        self.streams[e].append(('d', out, in_, ds[0]))
        self._mark(ev, reads, writes)
        self.n_ops += 1
        return ev

    def collective(self, in_ap, out_ap, reads, writes, groups):
        e = 'gpsimd'
        self._deps(e, reads, writes)
        sem, sid = self.new_sem("s_cc_%d" % self.nsem)
        ev = (sem, 1, sid, 'cc')
        self.streams[e].append(('c', in_ap, out_ap, sem, groups))
        self._mark(ev, reads, writes)
        self.n_ops += 1
        return ev

    def barrier(self):
        evs = []
        for e, c in self.cur.items():
            evs.append((c[0], c[1], c[2], e))
        for ds in self.dma_sems:
            if ds[1] > 0:
                evs.append((ds[0], ds[1], ds[2], 'dma'))
        for e in ENGINES:
            for ev in evs:
                if ev[3] != e:
                    self._wait(e, ev)

    def finish(self):
        for ds in self.dma_sems:
            if ds[1] > 0:
                self._wait('sync', (ds[0], ds[1], ds[2], 'dma'))
        streams = self.streams

        def replay(eng, name):
            for it in streams[name]:
                if it[0] == 'w':
                    eng.wait_ge(it[1], it[2])
                elif it[0] == 'i':
                    getattr(eng, it[1].name)(*it[1].args, **it[1].kwargs).then_inc(it[2], 1)
                elif it[0] == 'c':
                    eng.collective_compute("AllGather", ALU.bypass, replica_groups=it[4], ins=[it[1]], outs=[it[2]]).then_inc(it[3])
                else:
                    eng.dma_start(out=it[1], in_=it[2]).then_inc(it[3], 16)

        with self.nc.Block() as block:
            @block.sync
            def _(eng):
                replay(eng, 'sync')

            @block.tensor
            def _(eng):
                replay(eng, 'tensor')

            @block.vector
            def _(eng):
                replay(eng, 'vector')

            @block.scalar
            def _(eng):
                replay(eng, 'scalar')

            @block.gpsimd
            def _(eng):
                replay(eng, 'gpsimd')


D = 1024
S = 4096
NS = 16
TOK = NS * 128
EPS = 1e-6
NEG = -30000.0
C1 = 3608


def qi_of(j, s):
    m, odd = divmod(s, 2)
    if j == 0:
        return 4 * m + (3 if odd else 0)
    return 4 * m + (2 if odd else 1)


def nproc(s):
    m, odd = divmod(s, 2)
    return 4 * m + (4 if odd else 2)


def V_(P, fn, reads, writes):
    return P.op('vector', fn, reads, writes)


def A_(P, fn, reads, writes):
    return P.op('scalar', fn, reads, writes)


def T_(P, fn, reads, writes):
    return P.op('tensor', fn, reads, writes)


def G_(P, fn, reads, writes):
    return P.op('gpsimd', fn, reads, writes)


def emit_rstd(P, ss, n, eps_t):
    A_(P, lambda e: e.activation(out=ss.ap, in_=ss.ap, func=ACT.Sqrt, scale=1.0 / n, bias=eps_t.ap[:, 0:1]), [ss, eps_t], [ss])
    V_(P, lambda e: e.reciprocal(out=ss.ap, in_=ss.ap), [ss], [ss])


def emit_xnorm_T(P, xsrc_ap, xsrc_tok, hT, s, W):
    sq, ss, xb, pT, idn, eps_t = W['sq1024'], W['ss1'], W['xb'], W['pT'], W['idn'], W['eps']
    A_(P, lambda e: e.activation(out=sq.ap, in_=xsrc_ap, func=ACT.Square, accum_out=ss.ap[:, 0:1]), [xsrc_tok], [sq, ss])
    emit_rstd(P, ss, 1024, eps_t)
    V_(P, lambda e: e.tensor_scalar(out=xb.ap, in0=xsrc_ap, scalar1=ss.ap[:, 0:1], scalar2=None, op0=ALU.mult), [xsrc_tok, ss], [xb])
    pb = pT.ap.bitcast(BF16)
    for kc in range(8):
        T_(P, lambda e, kc=kc: e.transpose(out=pb[:, kc * 128:(kc + 1) * 128], in_=xb.ap[:, kc * 128:(kc + 1) * 128], identity=idn.ap), [xb, idn], [pT])
    A_(P, lambda e: e.copy(out=hT.ap[:, :, s * 128:(s + 1) * 128], in_=pb.rearrange("p (k t) -> p k t", k=8)), [pT], [hT])


PH1_GROUPS = [
    (0, 512, [(0, 512, 'n', dict(gi=0, q=True, dup=False, fm0=0))]),
    (512, 512, [(0, 512, 'n', dict(gi=1, q=False, dup=False, fm0=12))]),
    (1024, 512, [(0, 512, 'v', dict(tmcol=0))]),
    (1536, 512, [(0, 512, 'n', dict(gi=2, q=True, dup=False, fm0=4))]),
    (2048, 256, [(0, 128, 'n', dict(gi=3, q=False, dup=True, fm0=16)), (128, 128, 'v', dict(tmcol=512))]),
    (2304, 512, [(0, 512, 'n', dict(gi=4, q=True, dup=False, fm0=8))]),
    (2816, 512, [(0, 128, 'raw', dict(fm0=18)), (128, 128, 'raw', dict(fm0=19)),
                 (256, 128, 'n', dict(gi=6, q=False, dup=True, fm0=20)), (384, 128, 'v', dict(tmcol=640))]),
    (3328, 280, [(0, 128, 'n', dict(gi=7, q=False, dup=True, fm0=22)), (128, 128, 'v', dict(tmcol=768)),
                 (256, 24, 'cg', dict())]),
]


def emit_phase1(P, cx):
    dbg = False
    idn, eps_t, ps = cx['idn'], cx['eps_t'], cx['ps']
    wv, nm, gains = cx['wv'], cx['nm'], cx['gains']
    with ExitStack() as es1:
        P.es = es1
        W = dict(idn=idn, eps=eps_t, pT=ps[6])
        W['sq1024'] = P.sb("sq1024", [128, 1024], F32)
        W['ss1'] = P.sb("ss1", [128, 1], F32)
        W['xb'] = P.sb("xb", [128, 1024], BF16)
        nm_t = P.sb("nm_t", [128, 8], F32)
        g_t = P.sb("g_t", [128, 8, 64], F32)
        hT = P.sb("hT", [128, 8, TOK], BF16)
        xs = [P.sb("xs%d" % i, [128, 1024], F32) for i in range(2)]
        wst = [P.sb("wst%d" % i, [128, 8, 512], F32) for i in range(2)]
        wbf = [P.sb("wbf%d" % i, [128, 8, 512], BF16) for i in range(2)]
        fms = [P.sb("fms%d" % i, [128, 4, TOK], BF16) for i in range(2)]
        tms = P.sb("tms", [128, NS, 896], BF16)
        cg_t = P.sb("cg_t1", [128, NS, 24], F32)
        pj = ps[0:2]
        pq = ps[2:4]
        Tt = [P.sb("Tt%d" % i, [128, 512], F32) for i in range(2)]
        SQs = [P.sb("SQ%d" % i, [128, 512], F32) for i in range(2)]
        SSs = [P.sb("SS%d" % i, [128, 8], F32) for i in range(2)]
        TB = [P.sb("TB%d" % i, [128, 512], BF16) for i in range(2)]

        P.dma('sync', nm_t.ap, nm, [], [nm_t])
        P.dma('sync', g_t.ap, gains, [], [g_t])

        def load_w(gidx):
            c0, wd, _ = PH1_GROUPS[gidx]
            b = gidx % 2
            P.dma('sync', wst[b].ap[:, :, 0:wd], wv[:, :, c0:c0 + wd], [], [wst[b]])

        def conv_w(gidx):
            c0, wd, _ = PH1_GROUPS[gidx]
            b = gidx % 2
            for kc in range(8):
                if kc % 2 == 0:
                    V_(P, lambda e, kc=kc: e.tensor_scalar(out=wbf[b].ap[:, kc, 0:wd], in0=wst[b].ap[:, kc, 0:wd], scalar1=nm_t.ap[:, kc:kc + 1],
                                                           scalar2=None, op0=ALU.mult), [wst[b], nm_t], [wbf[b]])
                else:
                    A_(P, lambda e, kc=kc: e.activation(out=wbf[b].ap[:, kc, 0:wd], in_=wst[b].ap[:, kc, 0:wd], func=ACT.Copy,
                                                        scale=nm_t.ap[:, kc:kc + 1]), [wst[b], nm_t], [wbf[b]])

        load_w(0)
        for s in range(NS):
            b = s % 2
            P.dma('gpsimd', xs[b].ap, cx['x_rows'](s), [cx['x_tok']], [xs[b]])
            emit_xnorm_T(P, xs[b].ap, xs[b], hT, s, W)
        items = [(gidx, s_) for gidx in range(len(PH1_GROUPS)) for s_ in range(NS)]
        ginfo = {}
        for gidx, (c0, wd, segs) in enumerate(PH1_GROUPS):
            fpos = {}
            _p = 0
            for (off, sw, kind, pr) in segs:
                if kind in ('n', 'raw'):
                    fpos[pr['fm0']] = _p
                    _p += 2 if pr.get('dup') else (1 if kind == 'raw' else sw // 128)
            ginfo[gidx] = (fpos, _p, any(sg[2] == 'n' for sg in segs), any(sg[3].get('q') for sg in segs))

        def stageA(i):
            gidx, s = items[i]
            c0, wd, segs = PH1_GROUPS[gidx]
            if s == 0:
                if gidx + 1 < len(PH1_GROUPS):
                    load_w(gidx + 1)
                conv_w(gidx)
            wb = wbf[gidx % 2]
            pp, T = pj[i % 2], Tt[i % 2]
            for kc in range(8):
                T_(P, lambda e: e.matmul(out=pp.ap[:, 0:wd], lhsT=hT.ap[:, kc, s * 128:(s + 1) * 128], rhs=wb.ap[:, kc, 0:wd],
                                         start=(kc == 0), stop=(kc == 7)), [hT, wb], [pp])
            A_(P, lambda e: e.copy(out=T.ap[:, 0:wd], in_=pp.ap[:, 0:wd]), [pp], [T])

        def stageB(i):
            gidx, s = items[i]
            c0, wd, segs = PH1_GROUPS[gidx]
            fpos, ntot, has_n, isq = ginfo[gidx]
            T, tb, SQ, SS = Tt[i % 2], TB[i % 2], SQs[i % 2], SSs[i % 2]
            nh = wd // 64
            if has_n:
                nw = nh * 64
                V_(P, lambda e: e.tensor_tensor(out=SQ.ap[:, 0:nw], in0=T.ap[:, 0:nw], in1=T.ap[:, 0:nw], op=ALU.mult), [T], [SQ])
                V_(P, lambda e: e.tensor_reduce(out=SS.ap[:, 0:nh], in_=SQ.ap[:, 0:nw].rearrange("p (h d) -> p h d", d=64), axis=AX.X, op=ALU.add), [SQ], [SS])
                emit_rstd(P, SS, 64, eps_t)
            for (off, sw, kind, pr) in segs:
                if kind == 'n':
                    h0, hn = off // 64, sw // 64
                    t0 = fpos[pr['fm0']] * 128
                    t3 = T.ap[:, off:off + sw].rearrange("p (h d) -> p h d", d=64)
                    V_(P, lambda e: e.tensor_tensor(out=t3, in0=t3, in1=SS.ap[:, h0:h0 + hn].unsqueeze(2).broadcast_to([128, hn, 64]), op=ALU.mult), [T, SS], [T])
                    gb = g_t.ap[:, pr['gi'], :].unsqueeze(1).broadcast_to([128, hn, 64])
                    if pr['dup']:
                        o4 = tb.ap[:, t0:t0 + 256].rearrange("p (g c d) -> p g c d", g=2, c=2)
                        for c in range(2):
                            V_(P, lambda e: e.tensor_tensor(out=o4[:, :, c, :], in0=t3, in1=gb, op=ALU.mult), [T, g_t], [tb])
                    else:
                        o3 = tb.ap[:, t0:t0 + sw].rearrange("p (h d) -> p h d", d=64)
                        V_(P, lambda e: e.tensor_tensor(out=o3, in0=t3, in1=gb, op=ALU.mult), [T, g_t], [tb])
                elif kind == 'raw':
                    t0 = fpos[pr['fm0']] * 128
                    V_(P, lambda e: e.tensor_copy(out=tb.ap[:, t0:t0 + sw], in_=T.ap[:, off:off + sw]), [T], [tb])
                elif kind == 'v':
                    tc_ = pr['tmcol']
                    V_(P, lambda e: e.tensor_copy(out=tms.ap[:, s, tc_:tc_ + sw], in_=T.ap[:, off:off + sw]), [T], [tms])
                else:
                    A_(P, lambda e: e.activation(out=cg_t.ap[:, s, :], in_=T.ap[:, off:off + sw], func=ACT.Sigmoid), [T], [cg_t])

        def stageC(i):
            gidx, s = items[i]
            c0, wd, segs = PH1_GROUPS[gidx]
            fpos, ntot, has_n, isq = ginfo[gidx]
            fb = fms[gidx % 2]
            if ntot > 0:
                tb, ptr = TB[i % 2], pq[i % 2]
                pb = ptr.ap.bitcast(BF16)
                for k in range(ntot):
                    T_(P, lambda e: e.transpose(out=pb[:, k * 128:(k + 1) * 128], in_=tb.ap[:, k * 128:(k + 1) * 128], identity=idn.ap), [tb, idn], [ptr])
                dst = fb.ap[:, 0:ntot, s * 128:(s + 1) * 128]
                src = pb[:, 0:ntot * 128].rearrange("p (k t) -> p k t", k=ntot)
                if isq:
                    A_(P, lambda e: e.activation(out=dst, in_=src, func=ACT.Copy, scale=0.125), [ptr], [fb])
                else:
                    V_(P, lambda e: e.tensor_copy(out=dst, in_=src), [ptr], [fb])
            if s == NS - 1:
                for (off, sw, kind, pr) in segs:
                    if kind in ('n', 'raw'):
                        ntile = 2 if pr.get('dup') else (1 if kind == 'raw' else sw // 128)
                        f0 = pr['fm0']
                        fi = fpos[f0]
                        if f0 < 12:
                            P.dma('sync', cx['qt_d'][f0:f0 + ntile].rearrange("k p t -> p k t"), fb.ap[:, fi:fi + ntile, :], [fb], [cx['qt_tok']])
                        else:
                            kq, kr = divmod(f0 - 12, 4)
                            P.dma('sync', cx['exK3'][kq][kr:kr + ntile].rearrange("k p t -> p k t"), fb.ap[:, fi:fi + ntile, :], [fb], [cx['exK_tok'][kq]])

        nit = len(items)
        for step in range(nit + 2):
            if step < nit:
                stageA(step)
            if 0 <= step - 1 < nit:
                stageB(step - 1)
            if 0 <= step - 2 < nit:
                stageC(step - 2)
        for u in range(2):
            P.dma('sync', cx['exV'][u].rearrange("(s p) c -> p s c", p=128), tms.ap[:, 8 * u:8 * u + 8, :], [tms], [cx['exV_tok'][u]])
        P.dma('sync', cx['cgs_d'].rearrange("(s p) c -> p s c", p=128), cg_t.ap, [cg_t], [cx['cgs_tok']])


def attn_unit(P, Wk, q_ap, q_buf, kt_ap_fn, kt_buf, kbs, v_ap_fn, v_buf, ncols, O, bm_ap_fn, bm_buf, cfar_ap, sel=None):
    far = [(kb, r) for kb, r in kbs if r is None]
    near = [(kb, r) for kb, r in kbs if r is not None]
    chunks = [far[i:i + 4] for i in range(0, len(far), 4)] + [near[i:i + 4] for i in range(0, len(near), 4)]
    total = len(kbs)
    done = 0
    for ch in chunks:
        Sb = Wk['sps'][Wk['scnt'] % 2]
        PT = Wk['pts'][Wk['scnt'] % 2]
        Wk['scnt'] += 1
        n = len(ch)
        for i, (kb, r) in enumerate(ch):
            T_(P, lambda e: e.matmul(out=Sb.ap[:, i * 128:(i + 1) * 128], lhsT=kt_ap_fn(kb), rhs=q_ap, start=True, stop=(sel is None)),
               [kt_buf, q_buf], [Sb])
            if sel is not None:
                E, selT_ap, selT_buf, ep0 = sel
                T_(P, lambda e: e.matmul(out=Sb.ap[:, i * 128:(i + 1) * 128], lhsT=E.ap[ep0:ep0 + 64, kb * 128:(kb + 1) * 128], rhs=selT_ap,
                                         start=False, stop=True), [E, selT_buf], [Sb])
        if ch[0][1] is None:
            A_(P, lambda e: e.activation(out=PT.ap[:, 0:n * 128], in_=Sb.ap[:, 0:n * 128], func=ACT.Exp, bias=cfar_ap), [Sb, Wk['consts']], [PT])
        else:
            tmp = Wk['stmp']
            for i, (kb, r) in enumerate(ch):
                V_(P, lambda e: e.tensor_tensor(out=tmp.ap[:, i * 128:(i + 1) * 128], in0=Sb.ap[:, i * 128:(i + 1) * 128], in1=bm_ap_fn(r), op=ALU.add),
                   [Sb, bm_buf], [tmp])
            A_(P, lambda e: e.activation(out=PT.ap[:, 0:n * 128], in_=tmp.ap[:, 0:n * 128], func=ACT.Exp), [tmp], [PT])
        for i, (kb, r) in enumerate(ch):
            T_(P, lambda e: e.matmul(out=O.ap[:, 0:ncols], lhsT=PT.ap[:, i * 128:(i + 1) * 128], rhs=v_ap_fn(kb), start=(done == 0), stop=(done == total - 1)),
               [PT, v_buf], [O])
            done += 1


class Pipe:
    def __init__(self):
        self.pending = None
        self.deferred = []

    def push(self, s_fn, e_fn, pv_fn):
        s_fn()
        e_fn()
        if self.pending is not None:
            self.pending()
        self.pending = pv_fn
        self._tick()

    def _tick(self):
        cur, self.deferred = self.deferred, []
        for item in cur:
            item[0] -= 1
            if item[0] <= 0:
                item[1]()
            else:
                self.deferred.append(item)

    def defer(self, n, fn):
        self.deferred.append([n, fn])

    def flush(self):
        if self.pending is not None:
            self.pending()
            self.pending = None
        while self.deferred:
            self._tick()


def _push_chunk(pipe, P, Wk, ch, base, total, q_ap, q_buf, kt_aps, kt_buf, v_aps, v_buf, ncols, O, bm_aps, bm_buf, cfar_ap, sel, on_done):
    n = len(ch)
    k = Wk['scnt']
    Wk['scnt'] += 1
    Sb = Wk['sps'][k % len(Wk['sps'])]
    PT = Wk['pts'][k % len(Wk['pts'])]
    is_far = ch[0][1] is None

    def s_fn():
        for i, (kb, r) in enumerate(ch):
            T_(P, lambda e: e.matmul(out=Sb.ap[:, i * 128:(i + 1) * 128], lhsT=kt_aps[i], rhs=q_ap, start=True, stop=(sel is None)), [kt_buf, q_buf], [Sb])
            if sel is not None:
                E, selT_ap, selT_buf, ep0 = sel
                T_(P, lambda e: e.matmul(out=Sb.ap[:, i * 128:(i + 1) * 128], lhsT=E.ap[ep0:ep0 + 64, kb * 128:(kb + 1) * 128], rhs=selT_ap,
                                         start=False, stop=True), [E, selT_buf], [Sb])

    def e_fn():
        if is_far:
            A_(P, lambda e: e.activation(out=PT.ap[:, 0:n * 128], in_=Sb.ap[:, 0:n * 128], func=ACT.Exp, bias=cfar_ap), [Sb, Wk['consts']], [PT])
        else:
            tmp = Wk['stmps'][Wk['tcnt'] % len(Wk['stmps'])]
            Wk['tcnt'] += 1
            for i in range(n):
                V_(P, lambda e: e.tensor_tensor(out=tmp.ap[:, i * 128:(i + 1) * 128], in0=Sb.ap[:, i * 128:(i + 1) * 128], in1=bm_aps[i], op=ALU.add), [Sb, bm_buf], [tmp])
            A_(P, lambda e: e.activation(out=PT.ap[:, 0:n * 128], in_=tmp.ap[:, 0:n * 128], func=ACT.Exp), [tmp], [PT])

    def pv_fn():
        for i in range(n):
            T_(P, lambda e: e.matmul(out=O.ap[:, 0:ncols], lhsT=PT.ap[:, i * 128:(i + 1) * 128], rhs=v_aps[i], start=(base + i == 0), stop=(base + i == total - 1)),
               [PT, v_buf], [O])
        if on_done is not None:
            on_done()

    pipe.push(s_fn, e_fn, pv_fn)


def attn_unit_p(pipe, P, Wk, q_ap, q_buf, kt_ap_fn, kt_buf, kbs, v_ap_fn, v_buf, ncols, O, bm_ap_fn, bm_buf, cfar_ap, sel=None, on_done=None):
    far = [(kb, r) for kb, r in kbs if r is None]
    near = [(kb, r) for kb, r in kbs if r is not None]
    chunks = [far[i:i + 4] for i in range(0, len(far), 4)] + [near[i:i + 4] for i in range(0, len(near), 4)]
    base = 0
    for ci, ch in enumerate(chunks):
        _push_chunk(pipe, P, Wk, ch, base, len(kbs), q_ap, q_buf, [kt_ap_fn(kb) for kb, r in ch], kt_buf, [v_ap_fn(kb) for kb, r in ch], v_buf, ncols, O,
                    [bm_ap_fn(r) if r is not None else None for kb, r in ch], bm_buf, cfar_ap, sel, on_done if ci == len(chunks) - 1 else None)
        base += len(ch)


def _push_chunk2(pipe, P, Wk, ch, base, total, qbd_ap, q_buf, kt_aps, kt_buf, v_aps, v_buf, ncols, Os, bm_aps, bm_buf, cfar_aps, sel, on_done):
    n = len(ch)
    k = Wk['scnt']
    Wk['scnt'] += 1
    Sb = Wk['sps'][k % len(Wk['sps'])]
    PT = Wk['pts'][k % len(Wk['pts'])]
    is_far = ch[0][1] is None

    def s_fn():
        for i, (kb, r) in enumerate(ch):
            T_(P, lambda e: e.matmul(out=Sb.ap[:, i * 256:(i + 1) * 256], lhsT=kt_aps[i], rhs=qbd_ap, start=True, stop=(sel is None)), [kt_buf, q_buf], [Sb])
            if sel is not None:
                E, sel2_ap, sel2_buf = sel
                T_(P, lambda e: e.matmul(out=Sb.ap[:, i * 256:(i + 1) * 256], lhsT=E.ap[:, kb * 128:(kb + 1) * 128], rhs=sel2_ap,
                                         start=False, stop=True), [E, sel2_buf], [Sb])

    def e_fn():
        if is_far:
            if cfar_aps[0] is cfar_aps[1]:
                A_(P, lambda e: e.activation(out=PT.ap[:, 0:n * 256], in_=Sb.ap[:, 0:n * 256], func=ACT.Exp, bias=cfar_aps[0]), [Sb, Wk['consts']], [PT])
            else:
                for j in range(2):
                    A_(P, lambda e: e.activation(out=PT.ap[:, 0:n * 256].rearrange("p (i j q) -> p i j q", j=2, q=128)[:, :, j, :],
                                                 in_=Sb.ap[:, 0:n * 256].rearrange("p (i j q) -> p i j q", j=2, q=128)[:, :, j, :],
                                                 func=ACT.Exp, bias=cfar_aps[j]), [Sb, Wk['consts']], [PT])
        else:
            tmp = Wk['stmps'][Wk['tcnt'] % len(Wk['stmps'])]
            Wk['tcnt'] += 1
            for i in range(n):
                V_(P, lambda e: e.tensor_tensor(out=tmp.ap[:, i * 256:(i + 1) * 256].rearrange("p (j q) -> p j q", j=2),
                                                in0=Sb.ap[:, i * 256:(i + 1) * 256].rearrange("p (j q) -> p j q", j=2), in1=bm_aps[i], op=ALU.add), [Sb, bm_buf], [tmp])
            A_(P, lambda e: e.activation(out=PT.ap[:, 0:n * 256], in_=tmp.ap[:, 0:n * 256], func=ACT.Exp), [tmp], [PT])

    def pv_fn():
        for i in range(n):
            for j in range(2):
                T_(P, lambda e: e.matmul(out=Os[j].ap[:, 0:ncols], lhsT=PT.ap[:, i * 256 + j * 128:i * 256 + (j + 1) * 128], rhs=v_aps[i],
                                         start=(base + i == 0), stop=(base + i == total - 1)), [PT, v_buf], [Os[j]])
        if on_done is not None:
            on_done()

    pipe.push(s_fn, e_fn, pv_fn)


def attn_pair_p(pipe, P, Wk, qbd_ap, q_buf, kt_ap_fn, kt_buf, kbs, v_ap_fn, v_buf, ncols, Os, bm_ap_fn, bm_buf, cfar_aps, sel=None, on_done=None):
    far = [(kb, r) for kb, r in kbs if r is None]
    near = [(kb, r) for kb, r in kbs if r is not None]
    chunks = [far[i:i + 2] for i in range(0, len(far), 2)] + [near[i:i + 2] for i in range(0, len(near), 2)]
    base = 0
    for ci, ch in enumerate(chunks):
        _push_chunk2(pipe, P, Wk, ch, base, len(kbs), qbd_ap, q_buf, [kt_ap_fn(kb) for kb, r in ch], kt_buf, [v_ap_fn(kb) for kb, r in ch], v_buf, ncols, Os,
                     [bm_ap_fn(r) if r is not None else None for kb, r in ch], bm_buf, cfar_aps, sel, on_done if ci == len(chunks) - 1 else None)
        base += len(ch)


def near_far(n, nnear):
    kbs = []
    for kb in range(n):
        r = kb - (n - nnear)
        kbs.append((kb, r if r >= 0 else None))
    return kbs


_SKIP = set()
_DBG = False
_DBG_OUT = {}


def emit_phase23(P, c):
    (x_rows, x_tok, qt, qt_tok, cgs, cgs_tok, wmg, wbr, wout, wup, wdn, nm, nmlp, w1, w2, posT, g5, dl, lamc, subln, sinks, cfar,
     bma, bmb, bms, bmw, cm, selb, E_d, ov_d, xmid, xmid_tok, xout, xout_tok) = (c[k] for k in (
        'x_rows', 'x_tok', 'qt_d', 'qt_tok', 'cgs_d', 'cgs_tok', 'wmg', 'wbr', 'wout', 'wup', 'wdn', 'nm', 'nmlp', 'w1', 'w2', 'posT', 'g5', 'dl',
        'lamc', 'subln', 'sinks', 'cfar', 'bma', 'bmb', 'bms', 'bmw', 'cm', 'selb', 'E_d', 'ov_d', 'xmid', 'xmid_tok', 'xout', 'xout_tok'))
    idn, eps_t, ps = c['idn'], c['eps_t'], c['ps']
    load_kT, load_V = c['load_kT'], c['load_V']
    xdbg = c.get('xdbg')
    with ExitStack() as es:
        P.es = es
        consts = P.sb("consts", [128, 64], F32)
        slg = P.sb("slg", [128, 128], F32)
        cg_t = P.sb("cg_t", [128, NS, 24], F32)
        Wk = dict(sps=[ps[0], ps[1], ps[7]], scnt=0, tcnt=0, consts=consts)
        O_ps = ps[2:4]
        Ob = [ps[2], ps[3], ps[5], ps[6]]
        ptr = ps[4]
        pm = ps[5]
        P.dma('sync', consts.ap[:, 0:20], cfar, [], [consts])
        P.dma('sync', consts.ap[:, 20:28], sinks, [], [consts])
        P.dma('sync', consts.ap[:, 29:31], lamc, [], [consts])
        P.dma('sync', slg.ap, subln, [], [slg])
        P.dma('sync', cg_t.ap, cgs.rearrange("(s p) c -> p s c", p=128), [cgs_tok], [cg_t])
        V_(P, lambda e: e.memset(consts.ap[:, 31:32], 1e-30), [], [consts])
        A_(P, lambda e: e.activation(out=consts.ap[:, 20:28], in_=consts.ap[:, 20:28], func=ACT.Exp), [consts], [consts])
        V_(P, lambda e: e.tensor_scalar(out=slg.ap, in0=slg.ap, scalar1=consts.ap[:, 30:31], scalar2=None, op0=ALU.mult), [slg, consts], [slg])

        with ExitStack() as es2:
            P.es = es2
            OT = P.sb("OT", [128, 12, TOK], BF16)
            Wk['pts'] = [P.sb("pt%d" % i, [128, 512], BF16) for i in range(3)]
            Wk['stmps'] = [P.sb("stmp%d" % i, [128, 512], F32) for i in range(2)]
            Wk['stmp'] = Wk['stmps'][0]
            otms = [P.sb("otm%d" % i, [128, 256], BF16) for i in range(4)]
            otm = otms[0]
            smr = [P.sb("smr%d" % i, [128, 8], F32) for i in range(8)]
            a0s = [P.sb("a0_%d" % i, [128, 128], F32) for i in range(4)]
            oos = [P.sb("oo_%d" % i, [128, 128], F32) for i in range(4)]
            pipe = Pipe()
            sm = P.sb("sm", [128, 16], F32)
            dl_t = P.sb("dl_t", [128, 4, 64], F32)
            P.dma('sync', dl_t.ap, dl, [], [dl_t])
            d4 = dl_t.ap.rearrange("p (a b) d -> p a b d", b=2)
            lt = P.sb("lt", [128, 2, 64], F32)
            V_(P, lambda e: e.tensor_tensor(out=lt.ap, in0=d4[:, :, 0, :], in1=d4[:, :, 1, :], op=ALU.mult), [dl_t], [lt])
            V_(P, lambda e: e.tensor_reduce(out=sm.ap[:, 0:2], in_=lt.ap, axis=AX.X, op=ALU.add), [lt], [sm])
            A_(P, lambda e: e.activation(out=sm.ap[:, 0:2], in_=sm.ap[:, 0:2], func=ACT.Exp), [sm], [sm])
            V_(P, lambda e: e.tensor_tensor(out=sm.ap[:, 2:3], in0=sm.ap[:, 1:2], in1=sm.ap[:, 0:1], op=ALU.subtract), [sm], [sm])
            V_(P, lambda e: e.tensor_tensor(out=consts.ap[:, 28:29], in0=sm.ap[:, 2:3], in1=consts.ap[:, 29:30], op=ALU.subtract), [sm, consts], [consts])

            def store_ot(s, t0, ntile, otm=otm):
                pb = ptr.ap.bitcast(BF16)
                for i in range(ntile):
                    T_(P, lambda e: e.transpose(out=pb[:, i * 128:(i + 1) * 128], in_=otm.ap[:, i * 128:(i + 1) * 128], identity=idn.ap), [otm, idn], [ptr])
                A_(P, lambda e: e.copy(out=OT.ap[:, t0:t0 + ntile, s * 128:(s + 1) * 128], in_=pb[:, 0:ntile * 128].rearrange("p (k t) -> p k t", k=ntile)), [ptr], [OT])

            from functools import partial
            with ExitStack() as es3:
                P.es = es3
                KA = P.sb("KA", [128, S], BF16)
                VA = P.sb("VA", [128, 32, 129], BF16)
                QA = P.sb("QAbd", [128, NS, 2, 128], BF16)
                BMA = P.sb("BMA", [128, 6, 128], F32)
                V_(P, lambda e: e.memset(VA.ap[:, :, 128:129], 1.0), [], [VA])
                V_(P, lambda e: e.memset(QA.ap, 0.0), [], [QA])

                def epi_A(h, s, O0, O1, sm, a0, oo, otm_):
                    def st1():
                        V_(P, lambda e: e.reciprocal(out=sm.ap[:, 4:5], in_=O0.ap[:, 128:129]), [O0], [sm])
                        V_(P, lambda e: e.reciprocal(out=sm.ap[:, 5:6], in_=O1.ap[:, 128:129]), [O1], [sm])
                        V_(P, lambda e: e.tensor_tensor(out=sm.ap[:, 6:7], in0=sm.ap[:, 5:6], in1=consts.ap[:, 28:29], op=ALU.mult), [sm, consts], [sm])
                        V_(P, lambda e: e.tensor_scalar(out=a0.ap, in0=O0.ap[:, 0:128], scalar1=sm.ap[:, 4:5], scalar2=None, op0=ALU.mult), [O0, sm], [a0])
                        V_(P, lambda e: e.scalar_tensor_tensor(out=oo.ap, in0=O1.ap[:, 0:128], scalar=sm.ap[:, 6:7], in1=a0.ap, op0=ALU.mult, op1=ALU.add), [O1, sm, a0], [oo])
                        pipe.defer(1, st2)

                    def st2():
                        A_(P, lambda e: e.activation(out=a0.ap, in_=oo.ap, func=ACT.Square, accum_out=sm.ap[:, 7:8]), [oo], [a0, sm])
                        A_(P, lambda e: e.activation(out=sm.ap[:, 7:8], in_=sm.ap[:, 7:8], func=ACT.Sqrt, scale=1.0 / 128, bias=eps_t.ap[:, 0:1]), [sm, eps_t], [sm])
                        pipe.defer(1, st3)

                    def st3():
                        V_(P, lambda e: e.reciprocal(out=sm.ap[:, 7:8], in_=sm.ap[:, 7:8]), [sm], [sm])
                        V_(P, lambda e: e.scalar_tensor_tensor(out=otm_.ap[:, 0:128], in0=oo.ap, scalar=sm.ap[:, 7:8], in1=slg.ap, op0=ALU.mult, op1=ALU.mult), [oo, sm, slg], [otm_])
                        pipe.defer(1, partial(store_ot, s, h, 1, otm_))

                    pipe.defer(1, st1)

                ucnt = 0
                for h in range(4):
                    load_kT(P, KA, KA.ap, h)
                    load_V(P, VA, VA.ap[:, :, 0:128], h * 128, 128)
                    for c in range(2):
                        P.dma('sync', QA.ap[64 * c:64 * c + 64, :, c, :], qt[h][64 * c:64 * c + 64, :].rearrange("p (s q) -> p s q", q=128), [qt_tok], [QA])
                    P.dma('sync', BMA.ap, bma[h], [], [BMA])
                    for s in range(NS):
                        n = nproc(s)
                        kbs = near_far(n, 3)
                        O0, O1 = Ob[(2 * ucnt) % 4], Ob[(2 * ucnt + 1) % 4]
                        done = partial(epi_A, h, s, O0, O1, smr[ucnt % 8], a0s[ucnt % 4], oos[ucnt % 4], otms[ucnt % 4])
                        ucnt += 1
                        cf = consts.ap[:, h:h + 1]
                        attn_pair_p(pipe, P, Wk, QA.ap[:, s, :, :].rearrange("p j q -> p (j q)"), QA,
                                    lambda kb: KA.ap[:, kb * 128:(kb + 1) * 128], KA, kbs,
                                    lambda kb: VA.ap[:, kb, :], VA, 129, (O0, O1),
                                    lambda r: BMA.ap[:, (s % 2) * 3 + r, :].unsqueeze(1).broadcast_to([128, 2, 128]), BMA, (cf, cf), on_done=done)
                    pipe.flush()
            P.barrier()
            with ExitStack() as es3:
                P.es = es3
                KB = P.sb("KB", [128, S], BF16)
                VB = P.sb("VB", [128, 32, 65], BF16)
                QB = P.sb("QBbd", [128, 2, NS, 2, 128], BF16)
                BMB = P.sb("BMB", [128, 4, 6, 128], F32)
                V_(P, lambda e: e.memset(VB.ap[:, :, 64:65], 1.0), [], [VB])
                V_(P, lambda e: e.memset(QB.ap, 0.0), [], [QB])

                def epi_B(h, r, s, g, O, sm, otm_):
                    def st1():
                        V_(P, lambda e: e.tensor_tensor(out=sm.ap[:, 4:5], in0=O.ap[:, 64:65], in1=consts.ap[:, 20 + h:21 + h], op=ALU.add), [O, consts], [sm])
                        V_(P, lambda e: e.reciprocal(out=sm.ap[:, 4:5], in_=sm.ap[:, 4:5]), [sm], [sm])
                        V_(P, lambda e: e.tensor_scalar(out=otm_.ap[:, r * 64:(r + 1) * 64], in0=O.ap[:, 0:64], scalar1=sm.ap[:, 4:5], scalar2=None, op0=ALU.mult), [O, sm], [otm_])
                        if r == 3:
                            pipe.defer(1, partial(store_ot, s, 4 + 2 * g, 2, otm_))
                    pipe.defer(1, st1)

                ucnt = 0
                for g in range(2):
                    load_kT(P, KB, KB.ap, 4 + g)
                    load_V(P, VB, VB.ap[:, :, 0:64], 512 + g * 64, 64)
                    for pr_ in range(2):
                        for c in range(2):
                            P.dma('sync', QB.ap[64 * c:64 * c + 64, pr_, :, c, :], qt[4 + 2 * g + pr_][64 * c:64 * c + 64, :].rearrange("p (s q) -> p s q", q=128), [qt_tok], [QB])
                    P.dma('sync', BMB.ap, bmb[g], [], [BMB])
                    for s in range(NS):
                        n = nproc(s)
                        kbs = [(kb, kb - (n - 3)) for kb in range(max(0, n - 3), n)]
                        otm_ = otms[s % 4]
                        for pr_ in range(2):
                            Os = (Ob[(2 * ucnt) % 4], Ob[(2 * ucnt + 1) % 4])
                            d0 = partial(epi_B, 4 * g + 2 * pr_, 2 * pr_, s, g, Os[0], smr[(2 * ucnt) % 8], otm_)
                            d1 = partial(epi_B, 4 * g + 2 * pr_ + 1, 2 * pr_ + 1, s, g, Os[1], smr[(2 * ucnt + 1) % 8], otm_)
                            ucnt += 1

                            def done(d0=d0, d1=d1):
                                d0()
                                d1()
                            attn_pair_p(pipe, P, Wk, QB.ap[:, pr_, s, :, :].rearrange("p j q -> p (j q)"), QB,
                                        lambda kb: KB.ap[:, kb * 128:(kb + 1) * 128], KB, kbs,
                                        lambda kb: VB.ap[:, kb, :], VB, 65, Os,
                                        lambda rr: BMB.ap[:, 2 * pr_:2 * pr_ + 2, (s % 2) * 3 + rr, :], BMB, (None, None), on_done=done)
                    pipe.flush()
            P.barrier()
            emit_family_c(P, locals())
            P.barrier()
            P.es = es2
            emit_phase3a(P, locals())
            P.barrier()
        P.es = es
        emit_phase3b(P, locals())
        P.barrier()


def emit_phase3a(P, L):
    x_rows, x_tok, xmid, xmid_tok, wmg, wbr, wout, nm, OT, idn, eps_t, ps = (L[k] for k in ('x_rows', 'x_tok', 'xmid', 'xmid_tok', 'wmg', 'wbr', 'wout', 'nm', 'OT', 'idn', 'eps_t', 'ps'))
    with ExitStack() as es3:
        P.es = es3
        W = dict(idn=idn, eps=eps_t, pT=ps[6])
        W['sq1024'] = P.sb("sq1024", [128, 1024], F32)
        W['ss1'] = P.sb("ss1", [128, 1], F32)
        W['xb'] = P.sb("xb", [128, 1024], BF16)
        nm_t = P.sb("nm_t", [128, 8], F32)
        P.dma('sync', nm_t.ap, nm, [], [nm_t])
        hT = P.sb("hT", [128, 8, TOK], BF16)
        xs = [P.sb("xs%d" % i, [128, 1024], F32) for i in range(2)]
        for s in range(NS):
            b = s % 2
            P.dma('gpsimd', xs[b].ap, x_rows(s), [x_tok], [xs[b]])
            emit_xnorm_T(P, xs[b].ap, xs[b], hT, s, W)
        WB = P.sb("WB", [128, 12, 1024], BF16)
        WO = P.sb("WO", [128, 8, 1024], BF16)
        stg = [P.sb("stg%d" % i, [128, 2, 1024], F32) for i in range(2)]
        wbv = wbr.rearrange("n (mc p) d -> p (n mc) d", p=128)
        wov = wout.rearrange("(dc p) d -> p dc d", p=128)
        for i in range(10):
            st = stg[i % 2]
            if i < 6:
                P.dma('sync', st.ap, wbv[:, 2 * i:2 * i + 2, :], [], [st])
                dst = WB.ap[:, 2 * i:2 * i + 2, :]
                dbuf = WB
            else:
                P.dma('sync', st.ap, wov[:, 2 * (i - 6):2 * (i - 6) + 2, :], [], [st])
                dst = WO.ap[:, 2 * (i - 6):2 * (i - 6) + 2, :]
                dbuf = WO
            V_(P, lambda e: e.tensor_copy(out=dst[:, 0:1, :], in_=st.ap[:, 0:1, :]), [st], [dbuf])
            A_(P, lambda e: e.copy(out=dst[:, 1:2, :], in_=st.ap[:, 1:2, :]), [st], [dbuf])
        wg_s = [P.sb("wg_s0", [128, 8, 3, 128], F32)] * 2
        wg_b = [P.sb("wg_b%d" % i, [128, 8, 3, 128], BF16) for i in range(2)]
        zT = P.sb("zT", [128, 8, 512], BF16)
        Gt = [P.sb("Gt%d" % i, [128, 512], F32) for i in range(2)]
        zacc = P.sb("zacc", [128, 512], F32)
        ztmp = P.sb("ztmp", [128, 512], F32)
        xn = xs
        wgv = wmg.rearrange("(kc p) (n d) -> p kc n d", p=128, n=3)
        pg = ps[0:2]
        py = ps[2:4]
        po = ps[4:6]
        cnt = 0
        for T in range(4):
            ts = slice(T * 512, (T + 1) * 512)
            for dc in range(8):
                b = cnt % 2
                cnt += 1
                for n in range(3):
                    P.dma('sync', wg_s[b].ap[:, :, n, :], wgv[:, :, n, dc * 128:(dc + 1) * 128], [], [wg_s[b]])
                for kc in range(8):
                    if kc % 2 == 0:
                        V_(P, lambda e: e.tensor_scalar(out=wg_b[b].ap[:, kc], in0=wg_s[b].ap[:, kc], scalar1=nm_t.ap[:, kc:kc + 1], scalar2=None, op0=ALU.mult), [wg_s[b], nm_t], [wg_b[b]])
                    else:
                        A_(P, lambda e: e.activation(out=wg_b[b].ap[:, kc], in_=wg_s[b].ap[:, kc], func=ACT.Copy, scale=nm_t.ap[:, kc:kc + 1]), [wg_s[b], nm_t], [wg_b[b]])
                for n in range(3):
                    g_ps = pg[n % 2]
                    y_ps = py[n % 2]
                    G = Gt[n % 2]
                    for kc in range(8):
                        T_(P, lambda e: e.matmul(out=g_ps.ap, lhsT=wg_b[b].ap[:, kc, n, :], rhs=hT.ap[:, kc, ts], start=(kc == 0), stop=(kc == 7)), [wg_b[b], hT], [g_ps])
                    A_(P, lambda e: e.activation(out=G.ap, in_=g_ps.ap, func=ACT.Sigmoid), [g_ps], [G])
                    for mc in range(4):
                        T_(P, lambda e: e.matmul(out=y_ps.ap, lhsT=WB.ap[:, 4 * n + mc, dc * 128:(dc + 1) * 128], rhs=OT.ap[:, 4 * n + mc, ts], start=(mc == 0), stop=(mc == 3)), [WB, OT], [y_ps])
                    if n == 0:
                        V_(P, lambda e: e.tensor_tensor(out=zacc.ap, in0=G.ap, in1=y_ps.ap, op=ALU.mult), [G, y_ps], [zacc])
                    else:
                        V_(P, lambda e: e.tensor_tensor(out=ztmp.ap, in0=G.ap, in1=y_ps.ap, op=ALU.mult), [G, y_ps], [ztmp])
                        if n == 1:
                            V_(P, lambda e: e.tensor_tensor(out=zacc.ap, in0=zacc.ap, in1=ztmp.ap, op=ALU.add), [zacc, ztmp], [zacc])
                        else:
                            V_(P, lambda e: e.tensor_tensor(out=zT.ap[:, dc, :], in0=zacc.ap, in1=ztmp.ap, op=ALU.add), [zacc, ztmp], [zT])
            for si in range(4):
                s = T * 4 + si
                xb_ = xn[s % 2]
                P.dma('gpsimd', xb_.ap, x_rows(s), [x_tok], [xb_])
                for half in range(2):
                    o_ps = po[half]
                    for dc in range(8):
                        T_(P, lambda e: e.matmul(out=o_ps.ap, lhsT=zT.ap[:, dc, si * 128:(si + 1) * 128], rhs=WO.ap[:, dc, half * 512:(half + 1) * 512], start=(dc == 0), stop=(dc == 7)), [zT, WO], [o_ps])
                    V_(P, lambda e: e.tensor_tensor(out=xb_.ap[:, half * 512:(half + 1) * 512], in0=xb_.ap[:, half * 512:(half + 1) * 512], in1=o_ps.ap, op=ALU.add), [xb_, o_ps], [xb_])
                P.dma('sync', xmid[s * 128:(s + 1) * 128, :], xb_.ap, [xb_], [xmid_tok])


def emit_phase3b(P, L):
    xmid, xmid_tok, xout, xout_tok, wup, wdn, nmlp, idn, eps_t, ps = (L[k] for k in ('xmid', 'xmid_tok', 'xout', 'xout_tok', 'wup', 'wdn', 'nmlp', 'idn', 'eps_t', 'ps'))
    with ExitStack() as es3:
        P.es = es3
        W = dict(idn=idn, eps=eps_t, pT=ps[6])
        W['sq1024'] = P.sb("sq1024b", [128, 1024], F32)
        W['ss1'] = P.sb("ss1b", [128, 1], F32)
        W['xb'] = P.sb("xbb", [128, 1024], BF16)
        nm_t = P.sb("nmlp_t", [128, 8], F32)
        P.dma('sync', nm_t.ap, nmlp, [], [nm_t])
        X = P.sb("X", [128, NS, D], F32)
        hT = P.sb("hT2", [128, 8, TOK], BF16)
        P.dma('sync', X.ap, xmid.rearrange("(s p) d -> p s d", p=128), [xmid_tok], [X])
        for s in range(NS):
            emit_xnorm_T(P, X.ap[:, s, :], X, hT, s, W)
        wu_s = [P.sb("wu_s0", [128, 8, 512], F32)] * 2
        wu_b = [P.sb("wu_b%d" % i, [128, 8, 512], BF16) for i in range(2)]
        wd_s = [P.sb("wd_s0", [128, 4, 1024], F32)] * 2
        wd_b = [P.sb("wd_b%d" % i, [128, 4, 1024], BF16) for i in range(2)]
        aT = P.sb("aT", [128, 4, TOK], BF16)
        rl = [P.sb("rl%d" % i, [128, 512], F32) for i in range(2)]
        wuv = wup.rearrange("(kc p) f -> p kc f", p=128)
        wdv = wdn.rearrange("(fc p) d -> p fc d", p=128)
        pu = ps[0:2]
        pd = ps[2:4]

        def load(fg):
            b = fg % 2
            P.dma('sync', wu_s[b].ap, wuv[:, :, fg * 512:(fg + 1) * 512], [], [wu_s[b]])
            P.dma('gpsimd', wd_s[b].ap, wdv[:, fg * 4:(fg + 1) * 4, :], [], [wd_s[b]])

        load(0)
        cnt = 0
        for fg in range(8):
            b = fg % 2
            for kc in range(8):
                if kc % 2 == 0:
                    V_(P, lambda e: e.tensor_scalar(out=wu_b[b].ap[:, kc], in0=wu_s[b].ap[:, kc], scalar1=nm_t.ap[:, kc:kc + 1], scalar2=None, op0=ALU.mult), [wu_s[b], nm_t], [wu_b[b]])
                else:
                    A_(P, lambda e: e.activation(out=wu_b[b].ap[:, kc], in_=wu_s[b].ap[:, kc], func=ACT.Copy, scale=nm_t.ap[:, kc:kc + 1]), [wu_s[b], nm_t], [wu_b[b]])
            V_(P, lambda e: e.tensor_copy(out=wd_b[b].ap[:, 0:2], in_=wd_s[b].ap[:, 0:2]), [wd_s[b]], [wd_b[b]])
            A_(P, lambda e: e.copy(out=wd_b[b].ap[:, 2:4], in_=wd_s[b].ap[:, 2:4]), [wd_s[b]], [wd_b[b]])
            if fg + 1 < 8:
                load(fg + 1)
            for fc in range(4):
                for T in range(4):
                    u_ps = pu[cnt % 2]
                    r_ = rl[cnt % 2]
                    cnt += 1
                    for kc in range(8):
                        T_(P, lambda e: e.matmul(out=u_ps.ap, lhsT=wu_b[b].ap[:, kc, fc * 128:(fc + 1) * 128], rhs=hT.ap[:, kc, T * 512:(T + 1) * 512], start=(kc == 0), stop=(kc == 7)), [wu_b[b], hT], [u_ps])
                    A_(P, lambda e: e.activation(out=r_.ap, in_=u_ps.ap, func=ACT.Relu), [u_ps], [r_])
                    V_(P, lambda e: e.tensor_tensor(out=aT.ap[:, fc, T * 512:(T + 1) * 512], in0=r_.ap, in1=r_.ap, op=ALU.mult), [r_], [aT])
            for s in range(NS):
                for half in range(2):
                    d_ps = pd[half]
                    for fc in range(4):
                        T_(P, lambda e: e.matmul(out=d_ps.ap, lhsT=aT.ap[:, fc, s * 128:(s + 1) * 128], rhs=wd_b[b].ap[:, fc, half * 512:(half + 1) * 512], start=(fc == 0), stop=(fc == 3)), [aT, wd_b[b]], [d_ps])
                    V_(P, lambda e: e.tensor_tensor(out=X.ap[:, s, half * 512:(half + 1) * 512], in0=X.ap[:, s, half * 512:(half + 1) * 512], in1=d_ps.ap, op=ALU.add), [X, d_ps], [X])
        P.dma('sync', xout.rearrange("(s p) d -> p s d", p=128), X.ap, [X], [xout_tok])
        if L.get('xdbg') is not None:
            P.dma('sync', L['xdbg'].rearrange("(s p) d -> p s d", p=128), X.ap, [X], [])


def emit_family_c(P, L):
    qt, qt_tok, load_kT, load_V, w1, w2, posT, g5, bms, bmw, cm, selb, E_d, ov_d = (L[k] for k in
        ('qt', 'qt_tok', 'load_kT', 'load_V', 'w1', 'w2', 'posT', 'g5', 'bms', 'bmw', 'cm', 'selb', 'E_d', 'ov_d'))
    Wk, O_ps, ptr, pm, consts, eps_t, idn, cg_t, otm, sm, store_ot, ps = (L[k] for k in
        ('Wk', 'O_ps', 'ptr', 'pm', 'consts', 'eps_t', 'idn', 'cg_t', 'otm', 'sm', 'store_ot', 'ps'))
    ph = ps[6]
    with ExitStack() as es3:
        P.es = es3
        kcT = P.sb("kcT", [128, 2, 256], BF16)
        vca = P.sb("vca", [128, 2, 2, 129], BF16)
        V_(P, lambda e: e.memset(kcT.ap, 0.0), [], [kcT])
        V_(P, lambda e: e.memset(vca.ap, 0.0), [], [vca])
        V_(P, lambda e: e.memset(vca.ap[:, :, :, 64:65], 1.0), [], [vca])
        for g in range(2):
            P.dma('sync', vca.ap[:, g, :, 65:129], ov_d, [], [vca])
        with ExitStack() as es4:
            P.es = es4
            CKV = P.sb("CKV", [128, 2, S], BF16)
            load_kT(P, CKV, CKV.ap[:, 0, :], 6)
            load_kT(P, CKV, CKV.ap[:, 1, :], 7)
            W1s = P.sb("W1s", [128, 32, 256], F32)
            W1b = P.sb("W1b", [128, 32, 256], BF16)
            W2s = P.sb("W2s", [128, 2, 64], F32)
            W2b = P.sb("W2b", [128, 2, 64], BF16)
            pos_s = P.sb("pos_s", [128, 32], F32)
            pos_b = P.sb("pos_b", [128, 32], BF16)
            g5_t = P.sb("g5_t", [128, 64], F32)
            hb = P.sb("hb", [128, 1], F32)
            u = P.sb("u", [128, 256], F32)
            u2 = P.sb("u2", [128, 256], F32)
            gT = P.sb("gT", [128, 2, 256], BF16)
            kd = P.sb("kd", [128, 128], BF16)
            kf = P.sb("kf", [128, 64], F32)
            V_(P, lambda e: e.memset(gT.ap, 0.0), [], [gT])
            P.dma('sync', g5_t.ap, g5, [], [g5_t])
            for i in range(2):
                for half in range(2):
                    P.dma('sync', W1s.ap[64 * half:64 * half + 64], w1[i].rearrange("(l d) h -> d l h", d=64), [], [W1s])
                    P.dma('sync', pos_s.ap[64 * half:64 * half + 64], posT[i], [], [pos_s])
                P.dma('sync', W2s.ap, w2[i].rearrange("(c p) d -> p c d", p=128), [], [W2s])
                for q in range(4):
                    if q % 2 == 0:
                        V_(P, lambda e: e.tensor_copy(out=W1b.ap[:, q * 8:(q + 1) * 8, :], in_=W1s.ap[:, q * 8:(q + 1) * 8, :]), [W1s], [W1b])
                    else:
                        A_(P, lambda e: e.copy(out=W1b.ap[:, q * 8:(q + 1) * 8, :], in_=W1s.ap[:, q * 8:(q + 1) * 8, :]), [W1s], [W1b])
                V_(P, lambda e: e.tensor_copy(out=W2b.ap, in_=W2s.ap), [W2s], [W2b])
                V_(P, lambda e: e.tensor_copy(out=pos_b.ap, in_=pos_s.ap), [pos_s], [pos_b])
                for g in range(2):
                    r0 = 64 * g
                    src3 = CKV.ap[r0:r0 + 64, i, :].rearrange("p (c s) -> p c s", s=16)
                    for hc in range(2):
                        for l in range(32):
                            T_(P, lambda e: e.matmul(out=pm.ap[:, 0:1], lhsT=W1b.ap[r0:r0 + 64, l, hc * 128:(hc + 1) * 128], rhs=pos_b.ap[r0:r0 + 64, l:l + 1],
                                                     start=(l == 0), stop=(l == 31)), [W1b, pos_b], [pm])
                        A_(P, lambda e: e.copy(out=hb.ap, in_=pm.ap[:, 0:1]), [pm], [hb])
                        for l in range(32):
                            rhs = src3[:, 0:255, l] if l < 16 else src3[:, 1:256, l - 16]
                            T_(P, lambda e: e.matmul(out=ph.ap[:, 0:255], lhsT=W1b.ap[r0:r0 + 64, l, hc * 128:(hc + 1) * 128], rhs=rhs,
                                                     start=(l == 0), stop=(l == 31)), [W1b, CKV], [ph])
                        A_(P, lambda e: e.activation(out=u.ap[:, 0:255], in_=ph.ap[:, 0:255], func=ACT.Identity, bias=hb.ap[:, 0:1]), [ph, hb], [u])
                        V_(P, lambda e: e.tensor_tensor(out=u2.ap[:, 0:255], in0=u.ap[:, 0:255], in1=u.ap[:, 0:255], op=ALU.mult), [u], [u2])
                        V_(P, lambda e: e.tensor_scalar(out=u2.ap[:, 0:255], in0=u2.ap[:, 0:255], scalar1=0.044715, scalar2=1.0, op0=ALU.mult, op1=ALU.add), [u2], [u2])
                        V_(P, lambda e: e.tensor_tensor(out=u2.ap[:, 0:255], in0=u2.ap[:, 0:255], in1=u.ap[:, 0:255], op=ALU.mult), [u2, u], [u2])
                        A_(P, lambda e: e.activation(out=u2.ap[:, 0:255], in_=u2.ap[:, 0:255], func=ACT.Tanh, scale=0.7978845608028654), [u2], [u2])
                        V_(P, lambda e: e.tensor_scalar(out=u2.ap[:, 0:255], in0=u2.ap[:, 0:255], scalar1=1.0, scalar2=0.5, op0=ALU.add, op1=ALU.mult), [u2], [u2])
                        V_(P, lambda e: e.tensor_tensor(out=gT.ap[:, hc, 0:255], in0=u2.ap[:, 0:255], in1=u.ap[:, 0:255], op=ALU.mult), [u2, u], [gT])
                    for t in range(2):
                        cn = 128 if t == 0 else 127
                        for hc in range(2):
                            T_(P, lambda e: e.matmul(out=pm.ap[0:cn, 0:64], lhsT=gT.ap[:, hc, t * 128:t * 128 + cn], rhs=W2b.ap[:, hc, :],
                                                     start=(hc == 0), stop=(hc == 1)), [gT, W2b], [pm])
                        if i == 1:
                            A_(P, lambda e: e.copy(out=vca.ap[0:cn, g, t, 0:64], in_=pm.ap[0:cn, 0:64]), [pm], [vca])
                        else:
                            A_(P, lambda e: e.copy(out=kf.ap[0:cn, :], in_=pm.ap[0:cn, 0:64]), [pm], [kf])
                            A_(P, lambda e: e.activation(out=u.ap[0:cn, 0:64], in_=kf.ap[0:cn, :], func=ACT.Square, accum_out=sm.ap[0:cn, 8:9]), [kf], [u, sm])
                            A_(P, lambda e: e.activation(out=sm.ap[0:cn, 8:9], in_=sm.ap[0:cn, 8:9], func=ACT.Sqrt, scale=1.0 / 64, bias=eps_t.ap[0:cn, 0:1]), [sm, eps_t], [sm])
                            V_(P, lambda e: e.reciprocal(out=sm.ap[0:cn, 8:9], in_=sm.ap[0:cn, 8:9]), [sm], [sm])
                            V_(P, lambda e: e.memset(kd.ap, 0.0), [], [kd])
                            for c in range(2):
                                V_(P, lambda e: e.scalar_tensor_tensor(out=kd.ap[0:cn, c * 64:(c + 1) * 64], in0=kf.ap[0:cn, :], scalar=sm.ap[0:cn, 8:9], in1=g5_t.ap[0:cn, :],
                                                                       op0=ALU.mult, op1=ALU.mult), [kf, sm, g5_t], [kd])
                            pb = ptr.ap.bitcast(BF16)
                            T_(P, lambda e: e.transpose(out=pb[:, 0:128], in_=kd.ap, identity=idn.ap), [kd, idn], [ptr])
                            A_(P, lambda e: e.copy(out=kcT.ap[:, g, t * 128:t * 128 + cn], in_=pb[:, 0:cn]), [ptr], [kcT])
        P.barrier()
        P.es = es3
        from functools import partial
        pipe, Ob, smr, otms = L['pipe'], L['Ob'], L['smr'], L['otms']
        CM = P.sb("CM", [128, 2, NS, 128], F32)
        SELB = P.sb("SELB", [128, NS, 64], F32)
        Et = P.sb("Et", [128, S], BF16)
        P.dma('sync', CM.ap, cm, [], [CM])
        P.dma('sync', SELB.ap, selb, [], [SELB])
        P.dma('sync', Et.ap[0:64], E_d, [], [Et])
        P.dma('sync', Et.ap[64:128], E_d, [], [Et])
        KS = P.sb("KS", [128, S], BF16)
        KW = P.sb("KW", [128, S], BF16)
        VS = P.sb("VS", [128, 32, 65], BF16)
        VW = P.sb("VW", [128, 32, 65], BF16)
        QC = P.sb("QCbd", [128, 2, NS, 2, 128], BF16)
        BMS = P.sb("BMS", [128, 4, 6, 128], F32)
        BMW = P.sb("BMW", [128, 4, 12, 128], F32)
        accs = [P.sb("acc%d" % i, [128, 4, 64], F32) for i in range(2)]
        imps = [P.sb("imp%d" % i, [128, 64], F32) for i in range(2)]
        scs = [P.sb("sc%d" % i, [128, 64], F32) for i in range(2)]
        sc2s = [P.sb("sc2_%d" % i, [128, 64], F32) for i in range(2)]
        m8s = [P.sb("m8_%d" % i, [128, 8], F32) for i in range(2)]
        selnbs = [P.sb("selnb%d" % i, [128, 128], BF16) for i in range(2)]
        selTs = [P.sb("sel2_%d" % i, [128, 256], BF16) for i in range(2)]
        for _b in selTs:
            V_(P, lambda e: e.memset(_b.ap, 0.0), [], [_b])
        V_(P, lambda e: e.memset(QC.ap, 0.0), [], [QC])
        V_(P, lambda e: e.memset(VS.ap[:, :, 64:65], 1.0), [], [VS])
        V_(P, lambda e: e.memset(VW.ap[:, :, 64:65], 1.0), [], [VW])

        def selT_store(selnb, selT):
            pb = ptr.ap.bitcast(BF16)
            T_(P, lambda e: e.transpose(out=pb[:, 0:128], in_=selnb.ap, identity=idn.ap), [selnb, idn], [ptr])
            A_(P, lambda e: e.copy(out=selT.ap[0:64, 0:128], in_=pb[0:64, 0:128]), [ptr], [selT])
            V_(P, lambda e: e.tensor_copy(out=selT.ap[0:64, 128:256], in_=pb[0:64, 0:128]), [ptr], [selT])

        def epi_cmp(h, r, s, O, sm, acc, imp, sc, sc2, m8, selnb, selT, state):
            V_(P, lambda e: e.tensor_tensor(out=sm.ap[:, 4:5], in0=O.ap[:, 64:65], in1=consts.ap[:, 31:32], op=ALU.add), [O, consts], [sm])
            V_(P, lambda e: e.reciprocal(out=sm.ap[:, 4:5], in_=sm.ap[:, 4:5]), [sm], [sm])
            V_(P, lambda e: e.tensor_tensor(out=sm.ap[:, 5:6], in0=sm.ap[:, 4:5], in1=cg_t.ap[:, s, 3 * h:3 * h + 1], op=ALU.mult), [sm, cg_t], [sm])
            V_(P, lambda e: e.tensor_scalar(out=acc.ap[:, r, :], in0=O.ap[:, 0:64], scalar1=sm.ap[:, 5:6], scalar2=None, op0=ALU.mult), [O, sm], [acc])
            if r == 0:
                V_(P, lambda e: e.tensor_scalar(out=imp.ap, in0=O.ap[:, 65:129], scalar1=sm.ap[:, 4:5], scalar2=None, op0=ALU.mult), [O, sm], [imp])
            else:
                V_(P, lambda e: e.scalar_tensor_tensor(out=imp.ap, in0=O.ap[:, 65:129], scalar=sm.ap[:, 4:5], in1=imp.ap, op0=ALU.mult, op1=ALU.add), [O, sm, imp], [imp])
            if r == 3:
                V_(P, lambda e: e.tensor_tensor(out=sc.ap, in0=imp.ap, in1=SELB.ap[:, s, :], op=ALU.add), [imp, SELB], [sc])
                V_(P, lambda e: e.max(out=m8.ap, in_=sc.ap), [sc], [m8])
                V_(P, lambda e: e.match_replace(out=sc2.ap, in_to_replace=m8.ap, in_values=sc.ap, imm_value=-3.0e9), [m8, sc], [sc2])
                V_(P, lambda e: e.max(out=m8.ap, in_=sc2.ap), [sc2], [m8])
                for c in range(2):
                    V_(P, lambda e: e.tensor_scalar(out=selnb.ap[:, c * 64:(c + 1) * 64], in0=sc.ap, scalar1=m8.ap[:, 7:8], scalar2=NEG, op0=ALU.is_lt, op1=ALU.mult), [sc, m8], [selnb])

                def fire():
                    selT_store(selnb, selT)
                    state['selT_ready'] = True
                pipe.defer(2, fire)

        def epi_br(h, r, s, g, br, O, sm, acc, otm_, last):
            V_(P, lambda e: e.reciprocal(out=sm.ap[:, 4:5], in_=O.ap[:, 64:65]), [O], [sm])
            V_(P, lambda e: e.tensor_tensor(out=sm.ap[:, 5:6], in0=sm.ap[:, 4:5], in1=cg_t.ap[:, s, 3 * h + 1 + br:3 * h + 2 + br], op=ALU.mult), [sm, cg_t], [sm])
            V_(P, lambda e: e.scalar_tensor_tensor(out=acc.ap[:, r, :], in0=O.ap[:, 0:64], scalar=sm.ap[:, 5:6], in1=acc.ap[:, r, :], op0=ALU.mult, op1=ALU.add), [O, sm, acc], [acc])
            if last:
                V_(P, lambda e: e.tensor_copy(out=otm_.ap[:, 0:256], in_=acc.ap.rearrange("p r d -> p (r d)")), [acc], [otm_])
                pipe.defer(2, partial(store_ot, s, 8 + 2 * g, 2, otm_))

        def push_cmp(g, s, r, nct, O, done):
            p0 = 64 * (r % 2)
            qap = QC.ap[p0:p0 + 64, r // 2, s, r % 2, :]
            k = Wk['scnt']
            Wk['scnt'] += 1
            Sb = Wk['sps'][k % len(Wk['sps'])]
            PT = Wk['pts'][k % len(Wk['pts'])]
            tmp = Wk['stmps'][Wk['tcnt'] % len(Wk['stmps'])]
            Wk['tcnt'] += 1

            def s_fn():
                for t in range(nct):
                    T_(P, lambda e: e.matmul(out=Sb.ap[:, t * 128:(t + 1) * 128], lhsT=kcT.ap[p0:p0 + 64, g, t * 128:(t + 1) * 128], rhs=qap, start=True, stop=True), [kcT, QC], [Sb])

            def e_fn():
                for t in range(nct):
                    V_(P, lambda e: e.tensor_tensor(out=tmp.ap[:, t * 128:(t + 1) * 128], in0=Sb.ap[:, t * 128:(t + 1) * 128], in1=CM.ap[:, t, s, :], op=ALU.add), [Sb, CM], [tmp])
                A_(P, lambda e: e.activation(out=PT.ap[:, 0:nct * 128], in_=tmp.ap[:, 0:nct * 128], func=ACT.Exp), [tmp], [PT])

            def pv_fn():
                for t in range(nct):
                    T_(P, lambda e: e.matmul(out=O.ap[:, 0:129], lhsT=PT.ap[:, t * 128:(t + 1) * 128], rhs=vca.ap[:, g, t, :], start=(t == 0), stop=(t == nct - 1)), [PT, vca], [O])
                done()

            pipe.push(s_fn, e_fn, pv_fn)

        ucnt = 0
        gs = 0
        for g in range(2):
            load_kT(P, KS, KS.ap, 8 + g)
            load_kT(P, KW, KW.ap, 10 + g)
            load_V(P, VS, VS.ap[:, :, 0:64], 640 + g * 64, 64)
            load_V(P, VW, VW.ap[:, :, 0:64], 768 + g * 64, 64)
            for pr_ in range(2):
                for c in range(2):
                    P.dma('sync', QC.ap[64 * c:64 * c + 64, pr_, :, c, :], qt[8 + 2 * g + pr_][64 * c:64 * c + 64, :].rearrange("p (s q) -> p s q", q=128), [qt_tok], [QC])
            P.dma('sync', BMS.ap, bms[g], [], [BMS])
            P.dma('sync', BMW.ap, bmw[g], [], [BMW])
            for s in range(NS):
                n = nproc(s)
                nct = 1 if 8 * n <= 128 else 2
                i2 = gs % 2
                gs += 1
                acc, imp, sc, sc2, m8, selnb, selT, otm_ = accs[i2], imps[i2], scs[i2], sc2s[i2], m8s[i2], selnbs[i2], selTs[i2], otms[gs % 4]
                state = dict(selT_ready=False)
                for r in range(4):
                    h = 4 * g + r
                    O = Ob[ucnt % 4]
                    done = partial(epi_cmp, h, r, s, O, smr[ucnt % 8], acc, imp, sc, sc2, m8, selnb, selT, state)
                    ucnt += 1
                    push_cmp(g, s, r, nct, O, done)
                for pr_ in range(2):
                    Os = (Ob[(2 * ucnt) % 4], Ob[(2 * ucnt + 1) % 4])
                    d0 = partial(epi_br, 4 * g + 2 * pr_, 2 * pr_, s, g, 1, Os[0], smr[(2 * ucnt) % 8], acc, otm_, False)
                    d1 = partial(epi_br, 4 * g + 2 * pr_ + 1, 2 * pr_ + 1, s, g, 1, Os[1], smr[(2 * ucnt + 1) % 8], acc, otm_, False)
                    ucnt += 1

                    def done(d0=d0, d1=d1):
                        d0()
                        d1()
                    kbs = [(kb, kb - (n - 6)) for kb in range(max(0, n - 6), n)]
                    attn_pair_p(pipe, P, Wk, QC.ap[:, pr_, s, :, :].rearrange("p j q -> p (j q)"), QC,
                                lambda kb: KW.ap[:, kb * 128:(kb + 1) * 128], KW, kbs,
                                lambda kb: VW.ap[:, kb, :], VW, 65, Os,
                                lambda rr: BMW.ap[:, 2 * pr_:2 * pr_ + 2, (s % 2) * 6 + rr, :], BMW, (None, None), on_done=done)
                assert state['selT_ready'], "selection mask must be emitted before the selected-block branch"
                for pr_ in range(2):
                    Os = (Ob[(2 * ucnt) % 4], Ob[(2 * ucnt + 1) % 4])
                    h0 = 4 * g + 2 * pr_
                    d0 = partial(epi_br, h0, 2 * pr_, s, g, 0, Os[0], smr[(2 * ucnt) % 8], acc, otm_, False)
                    d1 = partial(epi_br, h0 + 1, 2 * pr_ + 1, s, g, 0, Os[1], smr[(2 * ucnt + 1) % 8], acc, otm_, pr_ == 1)
                    ucnt += 1

                    def done(d0=d0, d1=d1):
                        d0()
                        d1()
                    attn_pair_p(pipe, P, Wk, QC.ap[:, pr_, s, :, :].rearrange("p j q -> p (j q)"), QC,
                                lambda kb: KS.ap[:, kb * 128:(kb + 1) * 128], KS, near_far(n, 3),
                                lambda kb: VS.ap[:, kb, :], VS, 65, Os,
                                lambda rr: BMS.ap[:, 2 * pr_:2 * pr_ + 2, (s % 2) * 3 + rr, :], BMS,
                                (consts.ap[:, 12 + h0:13 + h0], consts.ap[:, 13 + h0:14 + h0]), sel=(Et, selT.ap, selT), on_done=done)
            pipe.flush()


RANK_SLOT_C = {(0, 0): 0, (0, 1): 3, (1, 0): 1, (1, 1): 2}
PAIRS = [[0, 1], [2, 3], [4, 5], [6, 7]]
NL = 4
SAME_ENGINE_SYNC = True


def build_fused(n_layers=NL):
    nc = bass.Bass("TRN2", target_bir_lowering=False)

    def din(name, shape, dt=F32):
        return nc.dram_tensor(name, list(shape), dt, kind="ExternalInput").ap()

    x = din("x", [TOK, D])
    w_in = din("w_in", [NL, D, 6680])
    wbr = din("wbr", [NL, 3, 512, D])
    wout = din("wout", [NL, D, D])
    wup = din("wup", [NL, D, 4096])
    wdn = din("wdn", [NL, 4096, D])
    nm = din("nm", [NL, 128, 8])
    nmlp = din("nmlp", [NL, 128, 8])
    gains = din("gains", [NL, 128, 8, 64])
    w1 = din("w1", [NL, 2, 2048, 256])
    w2 = din("w2", [NL, 2, 256, 64])
    posT = din("posT", [NL, 2, 64, 32])
    dl = din("dl", [NL, 128, 4, 64])
    lamc = din("lamc", [NL, 128, 2])
    subln = din("subln", [NL, 128, 128])
    sinks = din("sinks", [NL, 128, 8])
    cfar = din("cfar", [128, 20])
    bma = din("bma", [4, 128, 6, 128])
    bmb = din("bmb", [2, 128, 4, 6, 128])
    bms = din("bms", [2, 128, 4, 6, 128])
    bmw = din("bmw", [2, 128, 4, 12, 128])
    cm = din("cm", [128, 2, NS, 128])
    selb = din("selb", [128, NS, 64])
    E_d = din("E", [64, S], BF16)
    ov_d = din("ov", [128, 2, 64], BF16)
    idn_d = din("idn", [128, 128], BF16)
    xo = nc.dram_tensor("xo", [TOK, D], F32, kind="ExternalOutput").ap()
    xdbg_all = nc.dram_tensor("xdbg", [NL, TOK, D], F32, kind="ExternalOutput").ap() if _DBG else None
    qt_d = nc.dram_tensor("qt_d", [12, 128, TOK], BF16).ap()
    cgs_d = nc.dram_tensor("cgs_d", [TOK, 24], F32).ap()
    xmid = nc.dram_tensor("xmid", [TOK, D], F32).ap()
    xbuf = [nc.dram_tensor("xbuf%d" % i, [TOK, D], F32).ap() for i in range(2)]
    exK = [[nc.dram_tensor("exK%d_%d" % (i, q), [4 * 128, TOK], BF16).ap() for q in range(3)] for i in range(2)]
    exV = [[nc.dram_tensor("exV%d_%d" % (i, u), [TOK // 2, 896], BF16).ap() for u in range(2)] for i in range(2)]
    exKo = [[nc.dram_tensor("exKo%d_%d" % (i, q), [2 * 4 * 128, TOK], BF16).ap() for q in range(3)] for i in range(2)]
    exVo = [[nc.dram_tensor("exVo%d_%d" % (i, u), [TOK, 896], BF16).ap() for u in range(2)] for i in range(2)]

    with ExitStack() as es:
        P = Prog(nc, es, same_engine_sync=SAME_ENGINE_SYNC)
        idn = P.sb("idn_t", [128, 128], BF16)
        eps_t = P.sb("eps_t", [128, 1], F32)
        ps = [P.ps("ps%d" % i) for i in range(8)]
        P.dma('sync', idn.ap, idn_d, [], [idn])
        V_(P, lambda e: e.memset(eps_t.ap, EPS), [], [eps_t])
        toks = dict(qt=Tok("qt"), cgs=Tok("cgs"), xmid=Tok("xmid"), xb=[Tok("xb0"), Tok("xb1")], xo=Tok("xo"), x=Tok("x"),
                    exK=[[Tok("exK") for q in range(3)] for i in range(2)], exV=[[Tok("exV") for u in range(2)] for i in range(2)],
                    exKo=[[Tok("exKo") for q in range(3)] for i in range(2)], exVo=[[Tok("exVo") for u in range(2)] for i in range(2)])
        for l in range(n_layers):
            par = l % 2
            if l == 0:
                xsrc, xsrc_tok = x, toks['x']
            else:
                xsrc, xsrc_tok = xbuf[(l - 1) % 2], toks['xb'][(l - 1) % 2]
            if l == n_layers - 1:
                xdst, xdst_tok = xo, toks['xo']
            else:
                xdst, xdst_tok = xbuf[l % 2], toks['xb'][l % 2]
            Ko, Vo = exKo[par], exVo[par]
            Ko_tok, Vo_tok = toks['exKo'][par], toks['exVo'][par]

            def load_kT(P, buf, dst_ap, tile, Ko=Ko, Ko_tok=Ko_tok):
                d4 = dst_ap.rearrange("p (m c t) -> p m c t", c=4, t=128)
                kq, kr = divmod(tile, 4)
                for rank in range(2):
                    src = Ko[kq][rank * 512 + kr * 128:rank * 512 + (kr + 1) * 128, :].rearrange("p (m o t) -> p m o t", o=2, t=128)
                    for odd in range(2):
                        P.dma('sync', d4[:, :, RANK_SLOT_C[(rank, odd)], :], src[:, :, odd, :], [Ko_tok[kq]], [buf])

            def load_V(P, buf, dst_ap3, col0, ncols, Vo=Vo, Vo_tok=Vo_tok):
                d4 = dst_ap3.rearrange("p (m c) v -> p m c v", c=4)
                for u in range(2):
                    for rank in range(2):
                        src = Vo[u][rank * 1024:(rank + 1) * 1024, col0:col0 + ncols].rearrange("(m o p) v -> p m o v", o=2, p=128)
                        for odd in range(2):
                            P.dma('sync', d4[:, 4 * u:4 * u + 4, RANK_SLOT_C[(rank, odd)], :], src[:, :, odd, :], [Vo_tok[u]], [buf])

            c = dict(idn=idn, eps_t=eps_t, ps=ps,
                     x_rows=(lambda s, xsrc=xsrc: xsrc[s * 128:(s + 1) * 128, :]), x_tok=xsrc_tok,
                     wv=w_in[l].rearrange("(kc p) c -> p kc c", p=128), nm=nm[l], gains=gains[l],
                     qt_d=qt_d, qt_tok=toks['qt'], exK3=[a.rearrange("(k p) t -> k p t", p=128) for a in exK[par]], exK_tok=toks['exK'][par],
                     exV=exV[par], exV_tok=toks['exV'][par], cgs_d=cgs_d, cgs_tok=toks['cgs'],
                     wmg=w_in[l][:, C1:6680], wbr=wbr[l], wout=wout[l], wup=wup[l], wdn=wdn[l], nmlp=nmlp[l],
                     w1=w1[l], w2=w2[l], posT=posT[l], g5=gains[l][:, 5, :], dl=dl[l], lamc=lamc[l], subln=subln[l], sinks=sinks[l],
                     cfar=cfar, bma=bma, bmb=bmb, bms=bms, bmw=bmw, cm=cm, selb=selb, E_d=E_d, ov_d=ov_d,
                     xdbg=(xdbg_all[l] if _DBG else None), xmid=xmid, xmid_tok=toks['xmid'], xout=xdst, xout_tok=xdst_tok, load_kT=load_kT, load_V=load_V)
            emit_phase1(P, c)
            P.barrier()
            for q in range(3):
                P.collective(exK[par][q], Ko[q], [toks['exK'][par][q]], [Ko_tok[q]], PAIRS)
            for u in range(2):
                P.collective(exV[par][u], Vo[u], [toks['exV'][par][u]], [Vo_tok[u]], PAIRS)
            emit_phase23(P, c)
        P.es = es
        P.finish()
    return nc


BF = ml_dtypes.bfloat16


def _bucket(dist):
    n = np.maximum(dist, 0)
    nf = np.maximum(n, 1).astype(np.float32)
    large = 16 + (np.log(nf / np.float32(16)) / np.float32(np.log(8.0)) * np.float32(16)).astype(np.int32)
    large = np.minimum(large, 31)
    return np.where(n < 16, n, large)


def _bm_tile(rel_bias, col, delta, window):
    k = np.arange(128)[:, None]
    q = np.arange(128)[None, :]
    dist = delta * 128 + q - k
    vis = dist >= 0
    if window is not None:
        vis = vis & (dist < window)
    vals = rel_bias[_bucket(dist), col].astype(np.float32)
    return np.where(vis, vals, np.float32(NEG)).astype(np.float32)


def _bm_table(rel_bias, j, cols, nnear, window):
    out = np.zeros((len(cols), 128, 2 * nnear, 128), np.float32)
    for par in range(2):
        for r in range(nnear):
            delta = qi_of(j, par) - nproc(par) + nnear - r
            for ci, col in enumerate(cols):
                out[ci, :, par * nnear + r, :] = _bm_tile(rel_bias, col, delta, window)
    return out


def _core_tables(rel_bias, j):
    t = {}
    t['bma'] = _bm_table(rel_bias, j, list(range(0, 4)), 3, None)
    bmb = _bm_table(rel_bias, j, list(range(4, 12)), 3, 128)
    t['bmb'] = np.ascontiguousarray(bmb.reshape(2, 4, 128, 6, 128).transpose(0, 2, 1, 3, 4))
    bms = _bm_table(rel_bias, j, list(range(12, 20)), 3, None)
    t['bms'] = np.ascontiguousarray(bms.reshape(2, 4, 128, 6, 128).transpose(0, 2, 1, 3, 4))
    bmw = _bm_table(rel_bias, j, list(range(12, 20)), 6, 512)
    t['bmw'] = np.ascontiguousarray(bmw.reshape(2, 4, 128, 12, 128).transpose(0, 2, 1, 3, 4))
    cm = np.zeros((128, 2, NS, 128), np.float32)
    selb = np.zeros((128, NS, 64), np.float32)
    for s in range(NS):
        qi = qi_of(j, s)
        tq = 128 * qi + np.arange(128)
        for tl in range(2):
            c = tl * 128 + np.arange(128)
            vis = (16 * c[:, None] + 31 <= tq[None, :]) & (c[:, None] < 255)
            cm[:, tl, s, :] = np.where(vis, 0.0, NEG)
        jb = np.arange(64)[None, :]
        valid = 64 * jb <= tq[:, None]
        cur = tq[:, None] // 64
        forced = (jb == 0) | (jb == cur) | (jb == cur - 1)
        selb[:, s, :] = np.where(valid, np.where(forced, 1.0e4, 0.0), -1.0e9)
    t['cm'] = cm
    t['selb'] = selb
    return t


_CONST = {}


def _consts():
    if not _CONST:
        tt = np.arange(S)
        _CONST['E'] = (np.arange(64)[:, None] == (tt[None, :] // 64)).astype(BF)
        c = (np.arange(2)[None, :, None] * 128 + np.arange(128)[:, None, None])
        jb = np.arange(64)[None, None, :]
        ov = (16 * c < 64 * jb + 64) & (16 * c + 32 > 64 * jb) & (c < 255)
        _CONST['ov'] = ov.astype(BF)
        _CONST['idn'] = np.eye(128).astype(BF)
    return _CONST


_PROGS = {}


def _rep(v):
    v = np.asarray(v, np.float32)
    return np.ascontiguousarray(np.broadcast_to(v[None], (128,) + v.shape))


def _repl(v):
    v = np.asarray(v, np.float32)
    return np.ascontiguousarray(np.broadcast_to(v[:, None], (v.shape[0], 128) + v.shape[1:]))


def kernel(x, w_in, qk_gain, diff_lambda, diff_subln, sinks, cmp_pos, cmp_w1, cmp_w2,
           w_branch, w_out, norm_mix, norm_mlp, w_up, w_down, rel_bias, _n_layers=NL):
    import math
    x = np.asarray(x, np.float32)
    rel_bias = np.asarray(rel_bias, np.float32)
    if ('fused', _n_layers) not in _PROGS:
        _PROGS[('fused', _n_layers)] = build_fused(_n_layers)
    cst = _consts()
    cores = [(b, j) for b in range(4) for j in range(2)]
    tabs = [_core_tables(rel_bias, j) for j in range(2)]
    f32 = lambda a: np.ascontiguousarray(np.asarray(a, np.float32))
    lamc = np.array([[0.8 - 0.6 * math.exp(-0.3 * l), 1.0 - (0.8 - 0.6 * math.exp(-0.3 * l))] for l in range(NL)], np.float32)
    shared = dict(
        w_in=f32(w_in), wbr=f32(w_branch), wout=f32(w_out), wup=f32(w_up), wdn=f32(w_down),
        nm=np.ascontiguousarray(f32(norm_mix).reshape(NL, 8, 128).transpose(0, 2, 1)),
        nmlp=np.ascontiguousarray(f32(norm_mlp).reshape(NL, 8, 128).transpose(0, 2, 1)),
        gains=_repl(qk_gain), w1=f32(cmp_w1), w2=f32(cmp_w2),
        posT=np.ascontiguousarray(f32(cmp_pos).transpose(0, 1, 3, 2)),
        dl=_repl(diff_lambda), lamc=_repl(lamc), subln=_repl(diff_subln), sinks=_repl(sinks),
        cfar=_rep(rel_bias[31]), E=cst['E'], ov=cst['ov'], idn=cst['idn'])
    in_maps = []
    for (b, j) in cores:
        d = dict(shared)
        d['x'] = np.concatenate([x[b, qi_of(j, s) * 128:(qi_of(j, s) + 1) * 128] for s in range(NS)], 0)
        d.update(tabs[j])
        in_maps.append(d)
    res = run_bass_kernel_spmd(_PROGS[('fused', _n_layers)], in_maps, core_ids=list(range(8))).results
    if _DBG:
        _DBG_OUT['res'] = res
    out = np.zeros((4, S, D), np.float32)
    for i, (b, j) in enumerate(cores):
        xo = np.asarray(res[i]['xo'])
        for s in range(NS):
            qi = qi_of(j, s)
            out[b, qi * 128:(qi + 1) * 128] = xo[s * 128:(s + 1) * 128]
    return out
```

```python
import numpy as np
import ml_dtypes
from contextlib import ExitStack
import concourse.bass as bass
import concourse.mybir as mybir
from concourse.bass_utils import run_bass_kernel_spmd

F32 = mybir.dt.float32
BF16 = mybir.dt.bfloat16
ALU = mybir.AluOpType
ACT = mybir.ActivationFunctionType
AX = mybir.AxisListType

ENGINES = ('tensor', 'vector', 'scalar', 'gpsimd', 'sync')
NDMA_SEMS = 24
SEM_EPOCH = 30000


class Buf:
    __slots__ = ('ap', 'w', 'r', 'name')

    def __init__(self, ap, name):
        self.ap = ap
        self.w = None
        self.r = {}
        self.name = name


class Tok(Buf):
    def __init__(self, name=''):
        Buf.__init__(self, None, name)


class _Rec:
    def __init__(self):
        self.name = None

    def __getattr__(self, name):
        def f(*args, **kwargs):
            self.name = name
            self.args = args
            self.kwargs = kwargs
            return self
        return f


class Prog:
    def __init__(self, nc, es, same_engine_sync=True):
        self.nc = nc
        self.es = es
        self.sem_es = es
        self.streams = {e: [] for e in ENGINES}
        self.cur = {}
        self.waited = {e: {} for e in ENGINES}
        self.nsem = 0
        self.dma_sems = []
        self.dma_pools = {}
        self.dma_rrs = {}
        self.same_engine_sync = same_engine_sync
        self.n_ops = 0

    def new_sem(self, name):
        s = self.sem_es.enter_context(self.nc.semaphore(name))
        sid = self.nsem
        self.nsem += 1
        return s, sid

    def sb(self, name, shape, dtype):
        self.n_sb = getattr(self, 'n_sb', 0) + 1
        t = self.es.enter_context(self.nc.sbuf_tensor("%s_u%d" % (name, self.n_sb), list(shape), dtype))
        return Buf(t.ap(), name)

    def ps(self, name):
        t = self.es.enter_context(self.nc.psum_tensor(name, [128, 512], F32))
        return Buf(t.ap(), name)

    def _event(self, e):
        c = self.cur.get(e)
        if c is None or c[1] >= SEM_EPOCH:
            s, sid = self.new_sem("s_%s_%d" % (e, self.nsem))
            c = [s, 0, sid]
            self.cur[e] = c
        c[1] += 1
        return (c[0], c[1], c[2], e)

    def _wait(self, e, ev):
        sem, val, sid, src = ev
        w = self.waited[e]
        if w.get(sid, 0) >= val:
            return
        w[sid] = val
        self.streams[e].append(('w', sem, val))

    def _deps(self, e, reads, writes):
        for t in reads:
            if t.w is not None:
                self._dep1(e, t.w)
        for t in writes:
            if t.w is not None:
                self._dep1(e, t.w)
            for ev in t.r.values():
                self._dep1(e, ev)

    def _dep1(self, e, ev):
        if ev[3] == e and (e == 'tensor' or not self.same_engine_sync):
            return
        self._wait(e, ev)

    def _mark(self, ev, reads, writes):
        for t in reads:
            t.r[ev[2]] = ev
        for t in writes:
            t.w = ev
            t.r = {}

    def op(self, e, fn, reads=(), writes=()):
        rec = _Rec()
        fn(rec)
        self._deps(e, reads, writes)
        ev = self._event(e)
        self.streams[e].append(('i', rec, ev[0]))
        self._mark(ev, reads, writes)
        self.n_ops += 1
        return ev

    def dma(self, e, out, in_, reads=(), writes=()):
        pool = self.dma_pools.setdefault(e, [])
        if not pool:
            for i in range(NDMA_SEMS if e == 'sync' else 8):
                s, sid = self.new_sem("s_dma_%s_%d" % (e, i))
                ent = [s, 0, sid]
                pool.append(ent)
                self.dma_sems.append(ent)
        self.dma_rrs[e] = self.dma_rrs.get(e, 0) + 1
        ds = pool[self.dma_rrs[e] % len(pool)]
        if ds[1] > 0:
            self._wait(e, (ds[0], ds[1], ds[2], 'dma'))
        self._deps(e, reads, writes)
        ds[1] += 16
        ev = (ds[0], ds[1], ds[2], 'dma')
        self.streams[e].append(('d', out, in_, ds[0]))
        self._mark(ev, reads, writes)
        self.n_ops += 1
        return ev

    def collective(self, in_ap, out_ap, reads, writes, groups):
        e = 'gpsimd'
        self._deps(e, reads, writes)
        sem, sid = self.new_sem("s_cc_%d" % self.nsem)
        ev = (sem, 1, sid, 'cc')
        self.streams[e].append(('c', in_ap, out_ap, sem, groups))
        self._mark(ev, reads, writes)
        self.n_ops += 1
        return ev

    def barrier(self):
        evs = []
        for e, c in self.cur.items():
            evs.append((c[0], c[1], c[2], e))
        for ds in self.dma_sems:
            if ds[1] > 0:
                evs.append((ds[0], ds[1], ds[2], 'dma'))
        for e in ENGINES:
            for ev in evs:
                if ev[3] != e:
                    self._wait(e, ev)

    def finish(self):
        for ds in self.dma_sems:
            if ds[1] > 0:
                self._wait('sync', (ds[0], ds[1], ds[2], 'dma'))
        streams = self.streams

        def replay(eng, name):
            for it in streams[name]:
                if it[0] == 'w':
                    eng.wait_ge(it[1], it[2])
                elif it[0] == 'i':
                    getattr(eng, it[1].name)(*it[1].args, **it[1].kwargs).then_inc(it[2], 1)
                elif it[0] == 'c':
                    eng.collective_compute("AllGather", ALU.bypass, replica_groups=it[4], ins=[it[1]], outs=[it[2]]).then_inc(it[3])
                else:
                    eng.dma_start(out=it[1], in_=it[2]).then_inc(it[3], 16)

        with self.nc.Block() as block:
            @block.sync
            def _(eng):
                replay(eng, 'sync')

            @block.tensor
            def _(eng):
                replay(eng, 'tensor')

            @block.vector
            def _(eng):
                replay(eng, 'vector')

            @block.scalar
            def _(eng):
                replay(eng, 'scalar')

            @block.gpsimd
            def _(eng):
                replay(eng, 'gpsimd')


D = 1024
S = 4096
NS = 16
TOK = NS * 128
EPS = 1e-6
NEG = -30000.0
C1 = 3608


def qi_of(j, s):
    m, odd = divmod(s, 2)
    if j == 0:
        return 4 * m + (3 if odd else 0)
    return 4 * m + (2 if odd else 1)


def nproc(s):
    m, odd = divmod(s, 2)
    return 4 * m + (4 if odd else 2)


def V_(P, fn, reads, writes):
    return P.op('vector', fn, reads, writes)


def A_(P, fn, reads, writes):
    return P.op('scalar', fn, reads, writes)


def T_(P, fn, reads, writes):
    return P.op('tensor', fn, reads, writes)


def G_(P, fn, reads, writes):
    return P.op('gpsimd', fn, reads, writes)


def emit_rstd(P, ss, n, eps_t):
    A_(P, lambda e: e.activation(out=ss.ap, in_=ss.ap, func=ACT.Sqrt, scale=1.0 / n, bias=eps_t.ap[:, 0:1]), [ss, eps_t], [ss])
    V_(P, lambda e: e.reciprocal(out=ss.ap, in_=ss.ap), [ss], [ss])


def emit_xnorm_T(P, xsrc_ap, xsrc_tok, hT, s, W):
    sq, ss, xb, pT, idn, eps_t = W['sq1024'], W['ss1'], W['xb'], W['pT'], W['idn'], W['eps']
    A_(P, lambda e: e.activation(out=sq.ap, in_=xsrc_ap, func=ACT.Square, accum_out=ss.ap[:, 0:1]), [xsrc_tok], [sq, ss])
    emit_rstd(P, ss, 1024, eps_t)
    V_(P, lambda e: e.tensor_scalar(out=xb.ap, in0=xsrc_ap, scalar1=ss.ap[:, 0:1], scalar2=None, op0=ALU.mult), [xsrc_tok, ss], [xb])
    pb = pT.ap.bitcast(BF16)
    for kc in range(8):
        T_(P, lambda e, kc=kc: e.transpose(out=pb[:, kc * 128:(kc + 1) * 128], in_=xb.ap[:, kc * 128:(kc + 1) * 128], identity=idn.ap), [xb, idn], [pT])
    A_(P, lambda e: e.copy(out=hT.ap[:, :, s * 128:(s + 1) * 128], in_=pb.rearrange("p (k t) -> p k t", k=8)), [pT], [hT])


PH1_GROUPS = [
    (0, 512, [(0, 512, 'n', dict(gi=0, q=True, dup=False, fm0=0))]),
    (512, 512, [(0, 512, 'n', dict(gi=1, q=False, dup=False, fm0=12))]),
    (1024, 512, [(0, 512, 'v', dict(tmcol=0))]),
    (1536, 512, [(0, 512, 'n', dict(gi=2, q=True, dup=False, fm0=4))]),
    (2048, 256, [(0, 128, 'n', dict(gi=3, q=False, dup=True, fm0=16)), (128, 128, 'v', dict(tmcol=512))]),
    (2304, 512, [(0, 512, 'n', dict(gi=4, q=True, dup=False, fm0=8))]),
    (2816, 512, [(0, 128, 'raw', dict(fm0=18)), (128, 128, 'raw', dict(fm0=19)),
                 (256, 128, 'n', dict(gi=6, q=False, dup=True, fm0=20)), (384, 128, 'v', dict(tmcol=640))]),
    (3328, 280, [(0, 128, 'n', dict(gi=7, q=False, dup=True, fm0=22)), (128, 128, 'v', dict(tmcol=768)),
                 (256, 24, 'cg', dict())]),
]


def emit_phase1(P, cx):
    dbg = False
    idn, eps_t, ps = cx['idn'], cx['eps_t'], cx['ps']
    wv, nm, gains = cx['wv'], cx['nm'], cx['gains']
    with ExitStack() as es1:
        P.es = es1
        W = dict(idn=idn, eps=eps_t, pT=ps[6])
        W['sq1024'] = P.sb("sq1024", [128, 1024], F32)
        W['ss1'] = P.sb("ss1", [128, 1], F32)
        W['xb'] = P.sb("xb", [128, 1024], BF16)
        nm_t = P.sb("nm_t", [128, 8], F32)
        g_t = P.sb("g_t", [128, 8, 64], F32)
        hT = P.sb("hT", [128, 8, TOK], BF16)
        xs = [P.sb("xs%d" % i, [128, 1024], F32) for i in range(2)]
        wst = [P.sb("wst%d" % i, [128, 8, 512], F32) for i in range(2)]
        wbf = [P.sb("wbf%d" % i, [128, 8, 512], BF16) for i in range(2)]
        fms = [P.sb("fms%d" % i, [128, 4, TOK], BF16) for i in range(2)]
        tms = P.sb("tms", [128, NS, 896], BF16)
        cg_t = P.sb("cg_t1", [128, NS, 24], F32)
        pj = ps[0:2]
        pq = ps[2:4]
        Tt = [P.sb("Tt%d" % i, [128, 512], F32) for i in range(4)]
        SQs = [P.sb("SQ%d" % i, [128, 512], F32) for i in range(4)]
        SSs = [P.sb("SS%d" % i, [128, 8], F32) for i in range(4)]
        TB = [P.sb("TB%d" % i, [128, 512], BF16) for i in range(4)]

        P.dma('sync', nm_t.ap, nm, [], [nm_t])
        P.dma('sync', g_t.ap, gains, [], [g_t])

        def load_w(gidx):
            c0, wd, _ = PH1_GROUPS[gidx]
            b = gidx % 2
            P.dma('sync', wst[b].ap[:, :, 0:wd], wv[:, :, c0:c0 + wd], [], [wst[b]])

        def conv_w(gidx):
            c0, wd, _ = PH1_GROUPS[gidx]
            b = gidx % 2
            for kc in range(8):
                if kc % 2 == 0:
                    V_(P, lambda e, kc=kc: e.tensor_scalar(out=wbf[b].ap[:, kc, 0:wd], in0=wst[b].ap[:, kc, 0:wd], scalar1=nm_t.ap[:, kc:kc + 1],
                                                           scalar2=None, op0=ALU.mult), [wst[b], nm_t], [wbf[b]])
                else:
                    A_(P, lambda e, kc=kc: e.activation(out=wbf[b].ap[:, kc, 0:wd], in_=wst[b].ap[:, kc, 0:wd], func=ACT.Copy,
                                                        scale=nm_t.ap[:, kc:kc + 1]), [wst[b], nm_t], [wbf[b]])

        load_w(0)
        for s in range(NS):
            b = s % 2
            P.dma('gpsimd', xs[b].ap, cx['x_rows'](s), [cx['x_tok']], [xs[b]])
            emit_xnorm_T(P, xs[b].ap, xs[b], hT, s, W)
        items = [(gidx, s_) for gidx in range(len(PH1_GROUPS)) for s_ in range(NS)]
        ginfo = {}
        for gidx, (c0, wd, segs) in enumerate(PH1_GROUPS):
            fpos = {}
            _p = 0
            for (off, sw, kind, pr) in segs:
                if kind in ('n', 'raw'):
                    fpos[pr['fm0']] = _p
                    _p += 2 if pr.get('dup') else (1 if kind == 'raw' else sw // 128)
            ginfo[gidx] = (fpos, _p, any(sg[2] == 'n' for sg in segs), any(sg[3].get('q') for sg in segs))

        def stageA(i):
            gidx, s = items[i]
            c0, wd, segs = PH1_GROUPS[gidx]
            if s == 0:
                if gidx + 1 < len(PH1_GROUPS):
                    load_w(gidx + 1)
                conv_w(gidx)
            wb = wbf[gidx % 2]
            pp, T = pj[i % 2], Tt[i % 4]
            for kc in range(8):
                T_(P, lambda e: e.matmul(out=pp.ap[:, 0:wd], lhsT=hT.ap[:, kc, s * 128:(s + 1) * 128], rhs=wb.ap[:, kc, 0:wd],
                                         start=(kc == 0), stop=(kc == 7)), [hT, wb], [pp])
            A_(P, lambda e: e.copy(out=T.ap[:, 0:wd], in_=pp.ap[:, 0:wd]), [pp], [T])

        def stageB1(i):
            gidx, s = items[i]
            c0, wd, segs = PH1_GROUPS[gidx]
            fpos, ntot, has_n, isq = ginfo[gidx]
            T, SQ, SS = Tt[i % 4], SQs[i % 4], SSs[i % 4]
            nh = wd // 64
            if has_n:
                nw = nh * 64
                V_(P, lambda e: e.tensor_tensor(out=SQ.ap[:, 0:nw], in0=T.ap[:, 0:nw], in1=T.ap[:, 0:nw], op=ALU.mult), [T], [SQ])
                V_(P, lambda e: e.tensor_reduce(out=SS.ap[:, 0:nh], in_=SQ.ap[:, 0:nw].rearrange("p (h d) -> p h d", d=64), axis=AX.X, op=ALU.add), [SQ], [SS])
                A_(P, lambda e: e.activation(out=SS.ap, in_=SS.ap, func=ACT.Sqrt, scale=1.0 / 64, bias=eps_t.ap[:, 0:1]), [SS, eps_t], [SS])

        def stageB(i):
            gidx, s = items[i]
            c0, wd, segs = PH1_GROUPS[gidx]
            fpos, ntot, has_n, isq = ginfo[gidx]
            T, tb, SQ, SS = Tt[i % 4], TB[i % 4], SQs[i % 4], SSs[i % 4]
            nh = wd // 64
            if has_n:
                V_(P, lambda e: e.reciprocal(out=SS.ap, in_=SS.ap), [SS], [SS])
            for (off, sw, kind, pr) in segs:
                if kind == 'n':
                    h0, hn = off // 64, sw // 64
                    t0 = fpos[pr['fm0']] * 128
                    t3 = T.ap[:, off:off + sw].rearrange("p (h d) -> p h d", d=64)
                    V_(P, lambda e: e.tensor_tensor(out=t3, in0=t3, in1=SS.ap[:, h0:h0 + hn].unsqueeze(2).broadcast_to([128, hn, 64]), op=ALU.mult), [T, SS], [T])
                    gb = g_t.ap[:, pr['gi'], :].unsqueeze(1).broadcast_to([128, hn, 64])
                    if pr['dup']:
                        o4 = tb.ap[:, t0:t0 + 256].rearrange("p (g c d) -> p g c d", g=2, c=2)
                        for c in range(2):
                            V_(P, lambda e: e.tensor_tensor(out=o4[:, :, c, :], in0=t3, in1=gb, op=ALU.mult), [T, g_t], [tb])
                    else:
                        o3 = tb.ap[:, t0:t0 + sw].rearrange("p (h d) -> p h d", d=64)
                        V_(P, lambda e: e.tensor_tensor(out=o3, in0=t3, in1=gb, op=ALU.mult), [T, g_t], [tb])
                elif kind == 'raw':
                    t0 = fpos[pr['fm0']] * 128
                    V_(P, lambda e: e.tensor_copy(out=tb.ap[:, t0:t0 + sw], in_=T.ap[:, off:off + sw]), [T], [tb])
                elif kind == 'v':
                    tc_ = pr['tmcol']
                    V_(P, lambda e: e.tensor_copy(out=tms.ap[:, s, tc_:tc_ + sw], in_=T.ap[:, off:off + sw]), [T], [tms])
                else:
                    A_(P, lambda e: e.activation(out=cg_t.ap[:, s, :], in_=T.ap[:, off:off + sw], func=ACT.Sigmoid), [T], [cg_t])

        def stageC(i):
            gidx, s = items[i]
            c0, wd, segs = PH1_GROUPS[gidx]
            fpos, ntot, has_n, isq = ginfo[gidx]
            fb = fms[gidx % 2]
            if ntot > 0:
                tb, ptr = TB[i % 4], pq[i % 2]
                pb = ptr.ap.bitcast(BF16)
                for k in range(ntot):
                    T_(P, lambda e: e.transpose(out=pb[:, k * 128:(k + 1) * 128], in_=tb.ap[:, k * 128:(k + 1) * 128], identity=idn.ap), [tb, idn], [ptr])
                dst = fb.ap[:, 0:ntot, s * 128:(s + 1) * 128]
                src = pb[:, 0:ntot * 128].rearrange("p (k t) -> p k t", k=ntot)
                if isq:
                    A_(P, lambda e: e.activation(out=dst, in_=src, func=ACT.Copy, scale=0.125), [ptr], [fb])
                else:
                    V_(P, lambda e: e.tensor_copy(out=dst, in_=src), [ptr], [fb])
            if s == NS - 1:
                for (off, sw, kind, pr) in segs:
                    if kind in ('n', 'raw'):
                        ntile = 2 if pr.get('dup') else (1 if kind == 'raw' else sw // 128)
                        f0 = pr['fm0']
                        fi = fpos[f0]
                        if f0 < 12:
                            P.dma('sync', cx['qt_d'][f0:f0 + ntile].rearrange("k p t -> p k t"), fb.ap[:, fi:fi + ntile, :], [fb], [cx['qt_tok']])
                        else:
                            kq, kr = divmod(f0 - 12, 4)
                            P.dma('sync', cx['exK3'][kq][kr:kr + ntile].rearrange("k p t -> p k t"), fb.ap[:, fi:fi + ntile, :], [fb], [cx['exK_tok'][kq]])

        nit = len(items)
        for step in range(nit + 3):
            if step < nit:
                stageA(step)
            if 0 <= step - 1 < nit:
                stageB1(step - 1)
            if 0 <= step - 2 < nit:
                stageB(step - 2)
            if 0 <= step - 3 < nit:
                stageC(step - 3)
        for u in range(2):
            P.dma('sync', cx['exV'][u].rearrange("(s p) c -> p s c", p=128), tms.ap[:, 8 * u:8 * u + 8, :], [tms], [cx['exV_tok'][u]])
        P.dma('sync', cx['cgs_d'].rearrange("(s p) c -> p s c", p=128), cg_t.ap, [cg_t], [cx['cgs_tok']])


def attn_unit(P, Wk, q_ap, q_buf, kt_ap_fn, kt_buf, kbs, v_ap_fn, v_buf, ncols, O, bm_ap_fn, bm_buf, cfar_ap, sel=None):
    far = [(kb, r) for kb, r in kbs if r is None]
    near = [(kb, r) for kb, r in kbs if r is not None]
    chunks = [far[i:i + 4] for i in range(0, len(far), 4)] + [near[i:i + 4] for i in range(0, len(near), 4)]
    total = len(kbs)
    done = 0
    for ch in chunks:
        Sb = Wk['sps'][Wk['scnt'] % 2]
        PT = Wk['pts'][Wk['scnt'] % 2]
        Wk['scnt'] += 1
        n = len(ch)
        for i, (kb, r) in enumerate(ch):
            T_(P, lambda e: e.matmul(out=Sb.ap[:, i * 128:(i + 1) * 128], lhsT=kt_ap_fn(kb), rhs=q_ap, start=True, stop=(sel is None)),
               [kt_buf, q_buf], [Sb])
            if sel is not None:
                E, selT_ap, selT_buf, ep0 = sel
                T_(P, lambda e: e.matmul(out=Sb.ap[:, i * 128:(i + 1) * 128], lhsT=E.ap[ep0:ep0 + 64, kb * 128:(kb + 1) * 128], rhs=selT_ap,
                                         start=False, stop=True), [E, selT_buf], [Sb])
        if ch[0][1] is None:
            A_(P, lambda e: e.activation(out=PT.ap[:, 0:n * 128], in_=Sb.ap[:, 0:n * 128], func=ACT.Exp, bias=cfar_ap), [Sb, Wk['consts']], [PT])
        else:
            tmp = Wk['stmp']
            for i, (kb, r) in enumerate(ch):
                V_(P, lambda e: e.tensor_tensor(out=tmp.ap[:, i * 128:(i + 1) * 128], in0=Sb.ap[:, i * 128:(i + 1) * 128], in1=bm_ap_fn(r), op=ALU.add),
                   [Sb, bm_buf], [tmp])
            A_(P, lambda e: e.activation(out=PT.ap[:, 0:n * 128], in_=tmp.ap[:, 0:n * 128], func=ACT.Exp), [tmp], [PT])
        for i, (kb, r) in enumerate(ch):
            T_(P, lambda e: e.matmul(out=O.ap[:, 0:ncols], lhsT=PT.ap[:, i * 128:(i + 1) * 128], rhs=v_ap_fn(kb), start=(done == 0), stop=(done == total - 1)),
               [PT, v_buf], [O])
            done += 1


class Pipe:
    def __init__(self):
        self.pending = None
        self.deferred = []

    def push(self, s_fn, e_fn, pv_fn):
        s_fn()
        e_fn()
        if self.pending is not None:
            self.pending()
        self.pending = pv_fn
        self._tick()

    def _tick(self):
        cur, self.deferred = self.deferred, []
        for item in cur:
            item[0] -= 1
            if item[0] <= 0:
                item[1]()
            else:
                self.deferred.append(item)

    def defer(self, n, fn):
        self.deferred.append([n, fn])

    def flush(self):
        if self.pending is not None:
            self.pending()
            self.pending = None
        while self.deferred:
            self._tick()


def _push_chunk(pipe, P, Wk, ch, base, total, q_ap, q_buf, kt_aps, kt_buf, v_aps, v_buf, ncols, O, bm_aps, bm_buf, cfar_ap, sel, on_done):
    n = len(ch)
    k = Wk['scnt']
    Wk['scnt'] += 1
    Sb = Wk['sps'][k % len(Wk['sps'])]
    PT = Wk['pts'][k % len(Wk['pts'])]
    is_far = ch[0][1] is None

    def s_fn():
        for i, (kb, r) in enumerate(ch):
            T_(P, lambda e: e.matmul(out=Sb.ap[:, i * 128:(i + 1) * 128], lhsT=kt_aps[i], rhs=q_ap, start=True, stop=(sel is None)), [kt_buf, q_buf], [Sb])
            if sel is not None:
                E, selT_ap, selT_buf, ep0 = sel
                T_(P, lambda e: e.matmul(out=Sb.ap[:, i * 128:(i + 1) * 128], lhsT=E.ap[ep0:ep0 + 64, kb * 128:(kb + 1) * 128], rhs=selT_ap,
                                         start=False, stop=True), [E, selT_buf], [Sb])

    def e_fn():
        if is_far:
            A_(P, lambda e: e.activation(out=PT.ap[:, 0:n * 128], in_=Sb.ap[:, 0:n * 128], func=ACT.Exp, bias=cfar_ap), [Sb, Wk['consts']], [PT])
        else:
            tmp = Wk['stmps'][Wk['tcnt'] % len(Wk['stmps'])]
            Wk['tcnt'] += 1
            for i in range(n):
                V_(P, lambda e: e.tensor_tensor(out=tmp.ap[:, i * 128:(i + 1) * 128], in0=Sb.ap[:, i * 128:(i + 1) * 128], in1=bm_aps[i], op=ALU.add), [Sb, bm_buf], [tmp])
            A_(P, lambda e: e.activation(out=PT.ap[:, 0:n * 128], in_=tmp.ap[:, 0:n * 128], func=ACT.Exp), [tmp], [PT])

    def pv_fn():
        for i in range(n):
            T_(P, lambda e: e.matmul(out=O.ap[:, 0:ncols], lhsT=PT.ap[:, i * 128:(i + 1) * 128], rhs=v_aps[i], start=(base + i == 0), stop=(base + i == total - 1)),
               [PT, v_buf], [O])
        if on_done is not None:
            on_done()

    pipe.push(s_fn, e_fn, pv_fn)


def attn_unit_p(pipe, P, Wk, q_ap, q_buf, kt_ap_fn, kt_buf, kbs, v_ap_fn, v_buf, ncols, O, bm_ap_fn, bm_buf, cfar_ap, sel=None, on_done=None):
    far = [(kb, r) for kb, r in kbs if r is None]
    near = [(kb, r) for kb, r in kbs if r is not None]
    chunks = [far[i:i + 4] for i in range(0, len(far), 4)] + [near[i:i + 4] for i in range(0, len(near), 4)]
    base = 0
    for ci, ch in enumerate(chunks):
        _push_chunk(pipe, P, Wk, ch, base, len(kbs), q_ap, q_buf, [kt_ap_fn(kb) for kb, r in ch], kt_buf, [v_ap_fn(kb) for kb, r in ch], v_buf, ncols, O,
                    [bm_ap_fn(r) if r is not None else None for kb, r in ch], bm_buf, cfar_ap, sel, on_done if ci == len(chunks) - 1 else None)
        base += len(ch)


def _push_chunk2(pipe, P, Wk, ch, base, total, qbd_ap, q_buf, kt_aps, kt_buf, v_aps, v_buf, ncols, Os, bm_aps, bm_buf, cfar_aps, sel, on_done):
    n = len(ch)
    k = Wk['scnt']
    Wk['scnt'] += 1
    Sb = Wk['sps'][k % len(Wk['sps'])]
    PT = Wk['pts'][k % len(Wk['pts'])]
    is_far = ch[0][1] is None

    def s_fn():
        for i, (kb, r) in enumerate(ch):
            T_(P, lambda e: e.matmul(out=Sb.ap[:, i * 256:(i + 1) * 256], lhsT=kt_aps[i], rhs=qbd_ap, start=True, stop=(sel is None)), [kt_buf, q_buf], [Sb])
            if sel is not None:
                E, sel2_ap, sel2_buf = sel
                T_(P, lambda e: e.matmul(out=Sb.ap[:, i * 256:(i + 1) * 256], lhsT=E.ap[:, kb * 128:(kb + 1) * 128], rhs=sel2_ap,
                                         start=False, stop=True), [E, sel2_buf], [Sb])

    def e_fn():
        if is_far:
            if cfar_aps[0] is cfar_aps[1]:
                A_(P, lambda e: e.activation(out=PT.ap[:, 0:n * 256], in_=Sb.ap[:, 0:n * 256], func=ACT.Exp, bias=cfar_aps[0]), [Sb, Wk['consts']], [PT])
            else:
                for j in range(2):
                    A_(P, lambda e: e.activation(out=PT.ap[:, 0:n * 256].rearrange("p (i j q) -> p i j q", j=2, q=128)[:, :, j, :],
                                                 in_=Sb.ap[:, 0:n * 256].rearrange("p (i j q) -> p i j q", j=2, q=128)[:, :, j, :],
                                                 func=ACT.Exp, bias=cfar_aps[j]), [Sb, Wk['consts']], [PT])
        else:
            tmp = Wk['stmps'][Wk['tcnt'] % len(Wk['stmps'])]
            Wk['tcnt'] += 1
            for i in range(n):
                V_(P, lambda e: e.tensor_tensor(out=tmp.ap[:, i * 256:(i + 1) * 256].rearrange("p (j q) -> p j q", j=2),
                                                in0=Sb.ap[:, i * 256:(i + 1) * 256].rearrange("p (j q) -> p j q", j=2), in1=bm_aps[i], op=ALU.add), [Sb, bm_buf], [tmp])
            A_(P, lambda e: e.activation(out=PT.ap[:, 0:n * 256], in_=tmp.ap[:, 0:n * 256], func=ACT.Exp), [tmp], [PT])

    def pv_fn():
        for i in range(n):
            for j in range(2):
                T_(P, lambda e: e.matmul(out=Os[j].ap[:, 0:ncols], lhsT=PT.ap[:, i * 256 + j * 128:i * 256 + (j + 1) * 128], rhs=v_aps[i],
                                         start=(base + i == 0), stop=(base + i == total - 1)), [PT, v_buf], [Os[j]])
        if on_done is not None:
            on_done()

    pipe.push(s_fn, e_fn, pv_fn)


def attn_pair_p(pipe, P, Wk, qbd_ap, q_buf, kt_ap_fn, kt_buf, kbs, v_ap_fn, v_buf, ncols, Os, bm_ap_fn, bm_buf, cfar_aps, sel=None, on_done=None):
    far = [(kb, r) for kb, r in kbs if r is None]
    near = [(kb, r) for kb, r in kbs if r is not None]
    chunks = [far[i:i + 2] for i in range(0, len(far), 2)] + [near[i:i + 2] for i in range(0, len(near), 2)]
    base = 0
    for ci, ch in enumerate(chunks):
        _push_chunk2(pipe, P, Wk, ch, base, len(kbs), qbd_ap, q_buf, [kt_ap_fn(kb) for kb, r in ch], kt_buf, [v_ap_fn(kb) for kb, r in ch], v_buf, ncols, Os,
                     [bm_ap_fn(r) if r is not None else None for kb, r in ch], bm_buf, cfar_aps, sel, on_done if ci == len(chunks) - 1 else None)
        base += len(ch)


def near_far(n, nnear):
    kbs = []
    for kb in range(n):
        r = kb - (n - nnear)
        kbs.append((kb, r if r >= 0 else None))
    return kbs


_SKIP = set()
_DBG = False
_DBG_OUT = {}


def emit_phase23(P, c):
    (x_rows, x_tok, qt, qt_tok, cgs, cgs_tok, wmg, wbr, wout, wup, wdn, nm, nmlp, w1, w2, posT, g5, dl, lamc, subln, sinks, cfar,
     bma, bmb, bms, bmw, cm, selb, E_d, ov_d, xmid, xmid_tok, xout, xout_tok) = (c[k] for k in (
        'x_rows', 'x_tok', 'qt_d', 'qt_tok', 'cgs_d', 'cgs_tok', 'wmg', 'wbr', 'wout', 'wup', 'wdn', 'nm', 'nmlp', 'w1', 'w2', 'posT', 'g5', 'dl',
        'lamc', 'subln', 'sinks', 'cfar', 'bma', 'bmb', 'bms', 'bmw', 'cm', 'selb', 'E_d', 'ov_d', 'xmid', 'xmid_tok', 'xout', 'xout_tok'))
    idn, eps_t, ps = c['idn'], c['eps_t'], c['ps']
    load_kT, load_V = c['load_kT'], c['load_V']
    xdbg = c.get('xdbg')
    with ExitStack() as es:
        P.es = es
        consts = P.sb("consts", [128, 64], F32)
        slg = P.sb("slg", [128, 128], F32)
        cg_t = P.sb("cg_t", [128, NS, 24], F32)
        Wk = dict(sps=[ps[0], ps[1], ps[7]], scnt=0, tcnt=0, consts=consts)
        O_ps = ps[2:4]
        Ob = [ps[2], ps[3], ps[5], ps[6]]
        ptr = ps[4]
        pm = ps[5]
        P.dma('sync', consts.ap[:, 0:20], cfar, [], [consts])
        P.dma('sync', consts.ap[:, 20:28], sinks, [], [consts])
        P.dma('sync', consts.ap[:, 29:31], lamc, [], [consts])
        P.dma('sync', slg.ap, subln, [], [slg])
        P.dma('sync', cg_t.ap, cgs.rearrange("(s p) c -> p s c", p=128), [cgs_tok], [cg_t])
        V_(P, lambda e: e.memset(consts.ap[:, 31:32], 1e-30), [], [consts])
        A_(P, lambda e: e.activation(out=consts.ap[:, 20:28], in_=consts.ap[:, 20:28], func=ACT.Exp), [consts], [consts])
        V_(P, lambda e: e.tensor_scalar(out=slg.ap, in0=slg.ap, scalar1=consts.ap[:, 30:31], scalar2=None, op0=ALU.mult), [slg, consts], [slg])

        with ExitStack() as es2:
            P.es = es2
            OT = P.sb("OT", [128, 12, TOK], BF16)
            Wk['pts'] = [P.sb("pt%d" % i, [128, 512], BF16) for i in range(3)]
            Wk['stmps'] = [P.sb("stmp%d" % i, [128, 512], F32) for i in range(2)]
            Wk['stmp'] = Wk['stmps'][0]
            otms = [P.sb("otm%d" % i, [128, 256], BF16) for i in range(4)]
            otm = otms[0]
            smr = [P.sb("smr%d" % i, [128, 8], F32) for i in range(8)]
            a0s = [P.sb("a0_%d" % i, [128, 128], F32) for i in range(4)]
            oos = [P.sb("oo_%d" % i, [128, 128], F32) for i in range(4)]
            pipe = Pipe()
            sm = P.sb("sm", [128, 16], F32)
            dl_t = P.sb("dl_t", [128, 4, 64], F32)
            P.dma('sync', dl_t.ap, dl, [], [dl_t])
            d4 = dl_t.ap.rearrange("p (a b) d -> p a b d", b=2)
            lt = P.sb("lt", [128, 2, 64], F32)
            V_(P, lambda e: e.tensor_tensor(out=lt.ap, in0=d4[:, :, 0, :], in1=d4[:, :, 1, :], op=ALU.mult), [dl_t], [lt])
            V_(P, lambda e: e.tensor_reduce(out=sm.ap[:, 0:2], in_=lt.ap, axis=AX.X, op=ALU.add), [lt], [sm])
            A_(P, lambda e: e.activation(out=sm.ap[:, 0:2], in_=sm.ap[:, 0:2], func=ACT.Exp), [sm], [sm])
            V_(P, lambda e: e.tensor_tensor(out=sm.ap[:, 2:3], in0=sm.ap[:, 1:2], in1=sm.ap[:, 0:1], op=ALU.subtract), [sm], [sm])
            V_(P, lambda e: e.tensor_tensor(out=consts.ap[:, 28:29], in0=sm.ap[:, 2:3], in1=consts.ap[:, 29:30], op=ALU.subtract), [sm, consts], [consts])

            def store_ot(s, t0, ntile, otm=otm):
                pb = ptr.ap.bitcast(BF16)
                for i in range(ntile):
                    T_(P, lambda e: e.transpose(out=pb[:, i * 128:(i + 1) * 128], in_=otm.ap[:, i * 128:(i + 1) * 128], identity=idn.ap), [otm, idn], [ptr])
                A_(P, lambda e: e.copy(out=OT.ap[:, t0:t0 + ntile, s * 128:(s + 1) * 128], in_=pb[:, 0:ntile * 128].rearrange("p (k t) -> p k t", k=ntile)), [ptr], [OT])

            from functools import partial
            with ExitStack() as es3:
                P.es = es3
                KA = P.sb("KA", [128, S], BF16)
                VA = P.sb("VA", [128, 32, 129], BF16)
                QA = P.sb("QAbd", [128, NS, 2, 128], BF16)
                BMA = P.sb("BMA", [128, 6, 128], F32)
                V_(P, lambda e: e.memset(VA.ap[:, :, 128:129], 1.0), [], [VA])
                V_(P, lambda e: e.memset(QA.ap, 0.0), [], [QA])

                def epi_A(h, s, O0, O1, sm, a0, oo, otm_):
                    def st1():
                        V_(P, lambda e: e.reciprocal(out=sm.ap[:, 4:5], in_=O0.ap[:, 128:129]), [O0], [sm])
                        V_(P, lambda e: e.reciprocal(out=sm.ap[:, 5:6], in_=O1.ap[:, 128:129]), [O1], [sm])
                        V_(P, lambda e: e.tensor_tensor(out=sm.ap[:, 6:7], in0=sm.ap[:, 5:6], in1=consts.ap[:, 28:29], op=ALU.mult), [sm, consts], [sm])
                        V_(P, lambda e: e.tensor_scalar(out=a0.ap, in0=O0.ap[:, 0:128], scalar1=sm.ap[:, 4:5], scalar2=None, op0=ALU.mult), [O0, sm], [a0])
                        V_(P, lambda e: e.scalar_tensor_tensor(out=oo.ap, in0=O1.ap[:, 0:128], scalar=sm.ap[:, 6:7], in1=a0.ap, op0=ALU.mult, op1=ALU.add), [O1, sm, a0], [oo])
                        pipe.defer(1, st2)

                    def st2():
                        A_(P, lambda e: e.activation(out=a0.ap, in_=oo.ap, func=ACT.Square, accum_out=sm.ap[:, 7:8]), [oo], [a0, sm])
                        A_(P, lambda e: e.activation(out=sm.ap[:, 7:8], in_=sm.ap[:, 7:8], func=ACT.Sqrt, scale=1.0 / 128, bias=eps_t.ap[:, 0:1]), [sm, eps_t], [sm])
                        pipe.defer(1, st3)

                    def st3():
                        V_(P, lambda e: e.reciprocal(out=sm.ap[:, 7:8], in_=sm.ap[:, 7:8]), [sm], [sm])
                        V_(P, lambda e: e.scalar_tensor_tensor(out=otm_.ap[:, 0:128], in0=oo.ap, scalar=sm.ap[:, 7:8], in1=slg.ap, op0=ALU.mult, op1=ALU.mult), [oo, sm, slg], [otm_])
                        pipe.defer(1, partial(store_ot, s, h, 1, otm_))

                    pipe.defer(1, st1)

                ucnt = 0
                for h in range(4):
                    load_kT(P, KA, KA.ap, h)
                    load_V(P, VA, VA.ap[:, :, 0:128], h * 128, 128)
                    for c in range(2):
                        P.dma('sync', QA.ap[64 * c:64 * c + 64, :, c, :], qt[h][64 * c:64 * c + 64, :].rearrange("p (s q) -> p s q", q=128), [qt_tok], [QA])
                    P.dma('sync', BMA.ap, bma[h], [], [BMA])
                    for s in range(NS):
                        n = nproc(s)
                        kbs = near_far(n, 3)
                        O0, O1 = Ob[(2 * ucnt) % 4], Ob[(2 * ucnt + 1) % 4]
                        done = partial(epi_A, h, s, O0, O1, smr[ucnt % 8], a0s[ucnt % 4], oos[ucnt % 4], otms[ucnt % 4])
                        ucnt += 1
                        cf = consts.ap[:, h:h + 1]
                        attn_pair_p(pipe, P, Wk, QA.ap[:, s, :, :].rearrange("p j q -> p (j q)"), QA,
                                    lambda kb: KA.ap[:, kb * 128:(kb + 1) * 128], KA, kbs,
                                    lambda kb: VA.ap[:, kb, :], VA, 129, (O0, O1),
                                    lambda r: BMA.ap[:, (s % 2) * 3 + r, :].unsqueeze(1).broadcast_to([128, 2, 128]), BMA, (cf, cf), on_done=done)
                    pipe.flush()
            P.barrier()
            with ExitStack() as es3:
                P.es = es3
                KB = P.sb("KB", [128, S], BF16)
                VB = P.sb("VB", [128, 32, 65], BF16)
                QB = P.sb("QBbd", [128, 2, NS, 2, 128], BF16)
                BMB = P.sb("BMB", [128, 4, 6, 128], F32)
                V_(P, lambda e: e.memset(VB.ap[:, :, 64:65], 1.0), [], [VB])
                V_(P, lambda e: e.memset(QB.ap, 0.0), [], [QB])

                def epi_B(h, r, s, g, O, sm, otm_):
                    def st1():
                        V_(P, lambda e: e.tensor_tensor(out=sm.ap[:, 4:5], in0=O.ap[:, 64:65], in1=consts.ap[:, 20 + h:21 + h], op=ALU.add), [O, consts], [sm])
                        V_(P, lambda e: e.reciprocal(out=sm.ap[:, 4:5], in_=sm.ap[:, 4:5]), [sm], [sm])
                        V_(P, lambda e: e.tensor_scalar(out=otm_.ap[:, r * 64:(r + 1) * 64], in0=O.ap[:, 0:64], scalar1=sm.ap[:, 4:5], scalar2=None, op0=ALU.mult), [O, sm], [otm_])
                        if r == 3:
                            pipe.defer(1, partial(store_ot, s, 4 + 2 * g, 2, otm_))
                    pipe.defer(1, st1)

                ucnt = 0
                for g in range(2):
                    load_kT(P, KB, KB.ap, 4 + g)
                    load_V(P, VB, VB.ap[:, :, 0:64], 512 + g * 64, 64)
                    for pr_ in range(2):
                        for c in range(2):
                            P.dma('sync', QB.ap[64 * c:64 * c + 64, pr_, :, c, :], qt[4 + 2 * g + pr_][64 * c:64 * c + 64, :].rearrange("p (s q) -> p s q", q=128), [qt_tok], [QB])
                    P.dma('sync', BMB.ap, bmb[g], [], [BMB])
                    for s in range(NS):
                        n = nproc(s)
                        kbs = [(kb, kb - (n - 3)) for kb in range(max(0, n - 3), n)]
                        otm_ = otms[s % 4]
                        for pr_ in range(2):
                            Os = (Ob[(2 * ucnt) % 4], Ob[(2 * ucnt + 1) % 4])
                            d0 = partial(epi_B, 4 * g + 2 * pr_, 2 * pr_, s, g, Os[0], smr[(2 * ucnt) % 8], otm_)
                            d1 = partial(epi_B, 4 * g + 2 * pr_ + 1, 2 * pr_ + 1, s, g, Os[1], smr[(2 * ucnt + 1) % 8], otm_)
                            ucnt += 1

                            def done(d0=d0, d1=d1):
                                d0()
                                d1()
                            attn_pair_p(pipe, P, Wk, QB.ap[:, pr_, s, :, :].rearrange("p j q -> p (j q)"), QB,
                                        lambda kb: KB.ap[:, kb * 128:(kb + 1) * 128], KB, kbs,
                                        lambda kb: VB.ap[:, kb, :], VB, 65, Os,
                                        lambda rr: BMB.ap[:, 2 * pr_:2 * pr_ + 2, (s % 2) * 3 + rr, :], BMB, (None, None), on_done=done)
                    pipe.flush()
            P.barrier()
            emit_family_c(P, locals())
            P.barrier()
            P.es = es2
            emit_phase3a(P, locals())
            P.barrier()
        P.es = es
        emit_phase3b(P, locals())
        P.barrier()


def emit_phase3a(P, L):
    x_rows, x_tok, xmid, xmid_tok, wmg, wbr, wout, nm, OT, idn, eps_t, ps = (L[k] for k in ('x_rows', 'x_tok', 'xmid', 'xmid_tok', 'wmg', 'wbr', 'wout', 'nm', 'OT', 'idn', 'eps_t', 'ps'))
    with ExitStack() as es3:
        P.es = es3
        W = dict(idn=idn, eps=eps_t, pT=ps[6])
        W['sq1024'] = P.sb("sq1024", [128, 1024], F32)
        W['ss1'] = P.sb("ss1", [128, 1], F32)
        W['xb'] = P.sb("xb", [128, 1024], BF16)
        nm_t = P.sb("nm_t", [128, 8], F32)
        P.dma('sync', nm_t.ap, nm, [], [nm_t])
        hT = P.sb("hT", [128, 8, TOK], BF16)
        xs = [P.sb("xs%d" % i, [128, 1024], F32) for i in range(2)]
        for s in range(NS):
            b = s % 2
            P.dma('gpsimd', xs[b].ap, x_rows(s), [x_tok], [xs[b]])
            emit_xnorm_T(P, xs[b].ap, xs[b], hT, s, W)
        WB = P.sb("WB", [128, 12, 1024], BF16)
        WO = P.sb("WO", [128, 8, 1024], BF16)
        stg = [P.sb("stg%d" % i, [128, 2, 1024], F32) for i in range(2)]
        wbv = wbr.rearrange("n (mc p) d -> p (n mc) d", p=128)
        wov = wout.rearrange("(dc p) d -> p dc d", p=128)
        for i in range(10):
            st = stg[i % 2]
            if i < 6:
                P.dma('sync', st.ap, wbv[:, 2 * i:2 * i + 2, :], [], [st])
                dst = WB.ap[:, 2 * i:2 * i + 2, :]
                dbuf = WB
            else:
                P.dma('sync', st.ap, wov[:, 2 * (i - 6):2 * (i - 6) + 2, :], [], [st])
                dst = WO.ap[:, 2 * (i - 6):2 * (i - 6) + 2, :]
                dbuf = WO
            V_(P, lambda e: e.tensor_copy(out=dst[:, 0:1, :], in_=st.ap[:, 0:1, :]), [st], [dbuf])
            A_(P, lambda e: e.copy(out=dst[:, 1:2, :], in_=st.ap[:, 1:2, :]), [st], [dbuf])
        wg_s = [P.sb("wg_s0", [128, 8, 3, 128], F32)] * 2
        wg_b = [P.sb("wg_b%d" % i, [128, 8, 3, 128], BF16) for i in range(2)]
        zT = P.sb("zT", [128, 8, 512], BF16)
        Gt = [P.sb("Gt%d" % i, [128, 512], F32) for i in range(2)]
        zacc = P.sb("zacc", [128, 512], F32)
        ztmp = P.sb("ztmp", [128, 512], F32)
        xn = xs
        wgv = wmg.rearrange("(kc p) (n d) -> p kc n d", p=128, n=3)
        pg = ps[0:2]
        py = ps[2:4]
        po = ps[4:6]
        cnt = 0
        for T in range(4):
            ts = slice(T * 512, (T + 1) * 512)
            for dc in range(8):
                b = cnt % 2
                cnt += 1
                for n in range(3):
                    P.dma('sync', wg_s[b].ap[:, :, n, :], wgv[:, :, n, dc * 128:(dc + 1) * 128], [], [wg_s[b]])
                for kc in range(8):
                    if kc % 2 == 0:
                        V_(P, lambda e: e.tensor_scalar(out=wg_b[b].ap[:, kc], in0=wg_s[b].ap[:, kc], scalar1=nm_t.ap[:, kc:kc + 1], scalar2=None, op0=ALU.mult), [wg_s[b], nm_t], [wg_b[b]])
                    else:
                        A_(P, lambda e: e.activation(out=wg_b[b].ap[:, kc], in_=wg_s[b].ap[:, kc], func=ACT.Copy, scale=nm_t.ap[:, kc:kc + 1]), [wg_s[b], nm_t], [wg_b[b]])
                for n in range(3):
                    g_ps = pg[n % 2]
                    y_ps = py[n % 2]
                    G = Gt[n % 2]
                    for kc in range(8):
                        T_(P, lambda e: e.matmul(out=g_ps.ap, lhsT=wg_b[b].ap[:, kc, n, :], rhs=hT.ap[:, kc, ts], start=(kc == 0), stop=(kc == 7)), [wg_b[b], hT], [g_ps])
                    A_(P, lambda e: e.activation(out=G.ap, in_=g_ps.ap, func=ACT.Sigmoid), [g_ps], [G])
                    for mc in range(4):
                        T_(P, lambda e: e.matmul(out=y_ps.ap, lhsT=WB.ap[:, 4 * n + mc, dc * 128:(dc + 1) * 128], rhs=OT.ap[:, 4 * n + mc, ts], start=(mc == 0), stop=(mc == 3)), [WB, OT], [y_ps])
                    if n == 0:
                        V_(P, lambda e: e.tensor_tensor(out=zacc.ap, in0=G.ap, in1=y_ps.ap, op=ALU.mult), [G, y_ps], [zacc])
                    else:
                        V_(P, lambda e: e.tensor_tensor(out=ztmp.ap, in0=G.ap, in1=y_ps.ap, op=ALU.mult), [G, y_ps], [ztmp])
                        if n == 1:
                            V_(P, lambda e: e.tensor_tensor(out=zacc.ap, in0=zacc.ap, in1=ztmp.ap, op=ALU.add), [zacc, ztmp], [zacc])
                        else:
                            V_(P, lambda e: e.tensor_tensor(out=zT.ap[:, dc, :], in0=zacc.ap, in1=ztmp.ap, op=ALU.add), [zacc, ztmp], [zT])
            for si in range(4):
                s = T * 4 + si
                xb_ = xn[s % 2]
                P.dma('gpsimd', xb_.ap, x_rows(s), [x_tok], [xb_])
                for half in range(2):
                    o_ps = po[half]
                    for dc in range(8):
                        T_(P, lambda e: e.matmul(out=o_ps.ap, lhsT=zT.ap[:, dc, si * 128:(si + 1) * 128], rhs=WO.ap[:, dc, half * 512:(half + 1) * 512], start=(dc == 0), stop=(dc == 7)), [zT, WO], [o_ps])
                    V_(P, lambda e: e.tensor_tensor(out=xb_.ap[:, half * 512:(half + 1) * 512], in0=xb_.ap[:, half * 512:(half + 1) * 512], in1=o_ps.ap, op=ALU.add), [xb_, o_ps], [xb_])
                P.dma('sync', xmid[s * 128:(s + 1) * 128, :], xb_.ap, [xb_], [xmid_tok])


def emit_phase3b(P, L):
    xmid, xmid_tok, xout, xout_tok, wup, wdn, nmlp, idn, eps_t, ps = (L[k] for k in ('xmid', 'xmid_tok', 'xout', 'xout_tok', 'wup', 'wdn', 'nmlp', 'idn', 'eps_t', 'ps'))
    with ExitStack() as es3:
        P.es = es3
        W = dict(idn=idn, eps=eps_t, pT=ps[6])
        W['sq1024'] = P.sb("sq1024b", [128, 1024], F32)
        W['ss1'] = P.sb("ss1b", [128, 1], F32)
        W['xb'] = P.sb("xbb", [128, 1024], BF16)
        nm_t = P.sb("nmlp_t", [128, 8], F32)
        P.dma('sync', nm_t.ap, nmlp, [], [nm_t])
        X = P.sb("X", [128, NS, D], F32)
        hT = P.sb("hT2", [128, 8, TOK], BF16)
        P.dma('sync', X.ap, xmid.rearrange("(s p) d -> p s d", p=128), [xmid_tok], [X])
        for s in range(NS):
            emit_xnorm_T(P, X.ap[:, s, :], X, hT, s, W)
        wu_s = [P.sb("wu_s0", [128, 8, 512], F32)] * 2
        wu_b = [P.sb("wu_b%d" % i, [128, 8, 512], BF16) for i in range(2)]
        wd_s = [P.sb("wd_s0", [128, 4, 1024], F32)] * 2
        wd_b = [P.sb("wd_b%d" % i, [128, 4, 1024], BF16) for i in range(2)]
        aT = P.sb("aT", [128, 4, TOK], BF16)
        rl = [P.sb("rl%d" % i, [128, 512], F32) for i in range(2)]
        wuv = wup.rearrange("(kc p) f -> p kc f", p=128)
        wdv = wdn.rearrange("(fc p) d -> p fc d", p=128)
        pu = ps[0:2]
        pd = ps[2:4]

        def load(fg):
            b = fg % 2
            P.dma('sync', wu_s[b].ap, wuv[:, :, fg * 512:(fg + 1) * 512], [], [wu_s[b]])
            P.dma('gpsimd', wd_s[b].ap, wdv[:, fg * 4:(fg + 1) * 4, :], [], [wd_s[b]])

        load(0)
        cnt = 0
        for fg in range(8):
            b = fg % 2
            for kc in range(8):
                if kc % 2 == 0:
                    V_(P, lambda e: e.tensor_scalar(out=wu_b[b].ap[:, kc], in0=wu_s[b].ap[:, kc], scalar1=nm_t.ap[:, kc:kc + 1], scalar2=None, op0=ALU.mult), [wu_s[b], nm_t], [wu_b[b]])
                else:
                    A_(P, lambda e: e.activation(out=wu_b[b].ap[:, kc], in_=wu_s[b].ap[:, kc], func=ACT.Copy, scale=nm_t.ap[:, kc:kc + 1]), [wu_s[b], nm_t], [wu_b[b]])
            V_(P, lambda e: e.tensor_copy(out=wd_b[b].ap[:, 0:2], in_=wd_s[b].ap[:, 0:2]), [wd_s[b]], [wd_b[b]])
            A_(P, lambda e: e.copy(out=wd_b[b].ap[:, 2:4], in_=wd_s[b].ap[:, 2:4]), [wd_s[b]], [wd_b[b]])
            if fg + 1 < 8:
                load(fg + 1)
            for fc in range(4):
                for T in range(4):
                    u_ps = pu[cnt % 2]
                    r_ = rl[cnt % 2]
                    cnt += 1
                    for kc in range(8):
                        T_(P, lambda e: e.matmul(out=u_ps.ap, lhsT=wu_b[b].ap[:, kc, fc * 128:(fc + 1) * 128], rhs=hT.ap[:, kc, T * 512:(T + 1) * 512], start=(kc == 0), stop=(kc == 7)), [wu_b[b], hT], [u_ps])
                    A_(P, lambda e: e.activation(out=r_.ap, in_=u_ps.ap, func=ACT.Relu), [u_ps], [r_])
                    V_(P, lambda e: e.tensor_tensor(out=aT.ap[:, fc, T * 512:(T + 1) * 512], in0=r_.ap, in1=r_.ap, op=ALU.mult), [r_], [aT])
            for s in range(NS):
                for half in range(2):
                    d_ps = pd[half]
                    for fc in range(4):
                        T_(P, lambda e: e.matmul(out=d_ps.ap, lhsT=aT.ap[:, fc, s * 128:(s + 1) * 128], rhs=wd_b[b].ap[:, fc, half * 512:(half + 1) * 512], start=(fc == 0), stop=(fc == 3)), [aT, wd_b[b]], [d_ps])
                    V_(P, lambda e: e.tensor_tensor(out=X.ap[:, s, half * 512:(half + 1) * 512], in0=X.ap[:, s, half * 512:(half + 1) * 512], in1=d_ps.ap, op=ALU.add), [X, d_ps], [X])
        P.dma('sync', xout.rearrange("(s p) d -> p s d", p=128), X.ap, [X], [xout_tok])
        if L.get('xdbg') is not None:
            P.dma('sync', L['xdbg'].rearrange("(s p) d -> p s d", p=128), X.ap, [X], [])


def emit_family_c(P, L):
    qt, qt_tok, load_kT, load_V, w1, w2, posT, g5, bms, bmw, cm, selb, E_d, ov_d = (L[k] for k in
        ('qt', 'qt_tok', 'load_kT', 'load_V', 'w1', 'w2', 'posT', 'g5', 'bms', 'bmw', 'cm', 'selb', 'E_d', 'ov_d'))
    Wk, O_ps, ptr, pm, consts, eps_t, idn, cg_t, otm, sm, store_ot, ps = (L[k] for k in
        ('Wk', 'O_ps', 'ptr', 'pm', 'consts', 'eps_t', 'idn', 'cg_t', 'otm', 'sm', 'store_ot', 'ps'))
    ph = ps[6]
    with ExitStack() as es3:
        P.es = es3
        kcT = P.sb("kcT", [128, 2, 256], BF16)
        vca = P.sb("vca", [128, 2, 2, 129], BF16)
        V_(P, lambda e: e.memset(kcT.ap, 0.0), [], [kcT])
        V_(P, lambda e: e.memset(vca.ap, 0.0), [], [vca])
        V_(P, lambda e: e.memset(vca.ap[:, :, :, 64:65], 1.0), [], [vca])
        for g in range(2):
            P.dma('sync', vca.ap[:, g, :, 65:129], ov_d, [], [vca])
        with ExitStack() as es4:
            P.es = es4
            CKV = P.sb("CKV", [128, 2, S], BF16)
            load_kT(P, CKV, CKV.ap[:, 0, :], 6)
            load_kT(P, CKV, CKV.ap[:, 1, :], 7)
            W1s = P.sb("W1s", [128, 32, 256], F32)
            W1b = P.sb("W1b", [128, 32, 256], BF16)
            W2s = P.sb("W2s", [128, 2, 64], F32)
            W2b = P.sb("W2b", [128, 2, 64], BF16)
            pos_s = P.sb("pos_s", [128, 32], F32)
            pos_b = P.sb("pos_b", [128, 32], BF16)
            g5_t = P.sb("g5_t", [128, 64], F32)
            hb = P.sb("hb", [128, 1], F32)
            u = P.sb("u", [128, 256], F32)
            u2 = P.sb("u2", [128, 256], F32)
            gT = P.sb("gT", [128, 2, 256], BF16)
            kd = P.sb("kd", [128, 128], BF16)
            kf = P.sb("kf", [128, 64], F32)
            V_(P, lambda e: e.memset(gT.ap, 0.0), [], [gT])
            P.dma('sync', g5_t.ap, g5, [], [g5_t])
            for i in range(2):
                for half in range(2):
                    P.dma('sync', W1s.ap[64 * half:64 * half + 64], w1[i].rearrange("(l d) h -> d l h", d=64), [], [W1s])
                    P.dma('sync', pos_s.ap[64 * half:64 * half + 64], posT[i], [], [pos_s])
                P.dma('sync', W2s.ap, w2[i].rearrange("(c p) d -> p c d", p=128), [], [W2s])
                for q in range(4):
                    if q % 2 == 0:
                        V_(P, lambda e: e.tensor_copy(out=W1b.ap[:, q * 8:(q + 1) * 8, :], in_=W1s.ap[:, q * 8:(q + 1) * 8, :]), [W1s], [W1b])
                    else:
                        A_(P, lambda e: e.copy(out=W1b.ap[:, q * 8:(q + 1) * 8, :], in_=W1s.ap[:, q * 8:(q + 1) * 8, :]), [W1s], [W1b])
                V_(P, lambda e: e.tensor_copy(out=W2b.ap, in_=W2s.ap), [W2s], [W2b])
                V_(P, lambda e: e.tensor_copy(out=pos_b.ap, in_=pos_s.ap), [pos_s], [pos_b])
                for g in range(2):
                    r0 = 64 * g
                    src3 = CKV.ap[r0:r0 + 64, i, :].rearrange("p (c s) -> p c s", s=16)
                    for hc in range(2):
                        for l in range(32):
                            T_(P, lambda e: e.matmul(out=pm.ap[:, 0:1], lhsT=W1b.ap[r0:r0 + 64, l, hc * 128:(hc + 1) * 128], rhs=pos_b.ap[r0:r0 + 64, l:l + 1],
                                                     start=(l == 0), stop=(l == 31)), [W1b, pos_b], [pm])
                        A_(P, lambda e: e.copy(out=hb.ap, in_=pm.ap[:, 0:1]), [pm], [hb])
                        for l in range(32):
                            rhs = src3[:, 0:255, l] if l < 16 else src3[:, 1:256, l - 16]
                            T_(P, lambda e: e.matmul(out=ph.ap[:, 0:255], lhsT=W1b.ap[r0:r0 + 64, l, hc * 128:(hc + 1) * 128], rhs=rhs,
                                                     start=(l == 0), stop=(l == 31)), [W1b, CKV], [ph])
                        A_(P, lambda e: e.activation(out=u.ap[:, 0:255], in_=ph.ap[:, 0:255], func=ACT.Identity, bias=hb.ap[:, 0:1]), [ph, hb], [u])
                        V_(P, lambda e: e.tensor_tensor(out=u2.ap[:, 0:255], in0=u.ap[:, 0:255], in1=u.ap[:, 0:255], op=ALU.mult), [u], [u2])
                        V_(P, lambda e: e.tensor_scalar(out=u2.ap[:, 0:255], in0=u2.ap[:, 0:255], scalar1=0.044715, scalar2=1.0, op0=ALU.mult, op1=ALU.add), [u2], [u2])
                        V_(P, lambda e: e.tensor_tensor(out=u2.ap[:, 0:255], in0=u2.ap[:, 0:255], in1=u.ap[:, 0:255], op=ALU.mult), [u2, u], [u2])
                        A_(P, lambda e: e.activation(out=u2.ap[:, 0:255], in_=u2.ap[:, 0:255], func=ACT.Tanh, scale=0.7978845608028654), [u2], [u2])
                        V_(P, lambda e: e.tensor_scalar(out=u2.ap[:, 0:255], in0=u2.ap[:, 0:255], scalar1=1.0, scalar2=0.5, op0=ALU.add, op1=ALU.mult), [u2], [u2])
                        V_(P, lambda e: e.tensor_tensor(out=gT.ap[:, hc, 0:255], in0=u2.ap[:, 0:255], in1=u.ap[:, 0:255], op=ALU.mult), [u2, u], [gT])
                    for t in range(2):
                        cn = 128 if t == 0 else 127
                        for hc in range(2):
                            T_(P, lambda e: e.matmul(out=pm.ap[0:cn, 0:64], lhsT=gT.ap[:, hc, t * 128:t * 128 + cn], rhs=W2b.ap[:, hc, :],
                                                     start=(hc == 0), stop=(hc == 1)), [gT, W2b], [pm])
                        if i == 1:
                            A_(P, lambda e: e.copy(out=vca.ap[0:cn, g, t, 0:64], in_=pm.ap[0:cn, 0:64]), [pm], [vca])
                        else:
                            A_(P, lambda e: e.copy(out=kf.ap[0:cn, :], in_=pm.ap[0:cn, 0:64]), [pm], [kf])
                            A_(P, lambda e: e.activation(out=u.ap[0:cn, 0:64], in_=kf.ap[0:cn, :], func=ACT.Square, accum_out=sm.ap[0:cn, 8:9]), [kf], [u, sm])
                            A_(P, lambda e: e.activation(out=sm.ap[0:cn, 8:9], in_=sm.ap[0:cn, 8:9], func=ACT.Sqrt, scale=1.0 / 64, bias=eps_t.ap[0:cn, 0:1]), [sm, eps_t], [sm])
                            V_(P, lambda e: e.reciprocal(out=sm.ap[0:cn, 8:9], in_=sm.ap[0:cn, 8:9]), [sm], [sm])
                            V_(P, lambda e: e.memset(kd.ap, 0.0), [], [kd])
                            for c in range(2):
                                V_(P, lambda e: e.scalar_tensor_tensor(out=kd.ap[0:cn, c * 64:(c + 1) * 64], in0=kf.ap[0:cn, :], scalar=sm.ap[0:cn, 8:9], in1=g5_t.ap[0:cn, :],
                                                                       op0=ALU.mult, op1=ALU.mult), [kf, sm, g5_t], [kd])
                            pb = ptr.ap.bitcast(BF16)
                            T_(P, lambda e: e.transpose(out=pb[:, 0:128], in_=kd.ap, identity=idn.ap), [kd, idn], [ptr])
                            A_(P, lambda e: e.copy(out=kcT.ap[:, g, t * 128:t * 128 + cn], in_=pb[:, 0:cn]), [ptr], [kcT])
        P.barrier()
        P.es = es3
        from functools import partial
        pipe, Ob, smr, otms = L['pipe'], L['Ob'], L['smr'], L['otms']
        CM = P.sb("CM", [128, 2, NS, 128], F32)
        SELB = P.sb("SELB", [128, NS, 64], F32)
        Et = P.sb("Et", [128, S], BF16)
        P.dma('sync', CM.ap, cm, [], [CM])
        P.dma('sync', SELB.ap, selb, [], [SELB])
        P.dma('sync', Et.ap[0:64], E_d, [], [Et])
        P.dma('sync', Et.ap[64:128], E_d, [], [Et])
        KS = P.sb("KS", [128, S], BF16)
        KW = P.sb("KW", [128, S], BF16)
        VS = P.sb("VS", [128, 32, 65], BF16)
        VW = P.sb("VW", [128, 32, 65], BF16)
        QC = P.sb("QCbd", [128, 2, NS, 2, 128], BF16)
        BMS = P.sb("BMS", [128, 4, 6, 128], F32)
        BMW = P.sb("BMW", [128, 4, 12, 128], F32)
        accs = [P.sb("acc%d" % i, [128, 4, 64], F32) for i in range(2)]
        imps = [P.sb("imp%d" % i, [128, 64], F32) for i in range(2)]
        scs = [P.sb("sc%d" % i, [128, 64], F32) for i in range(2)]
        sc2s = [P.sb("sc2_%d" % i, [128, 64], F32) for i in range(2)]
        m8s = [P.sb("m8_%d" % i, [128, 8], F32) for i in range(2)]
        selnbs = [P.sb("selnb%d" % i, [128, 128], BF16) for i in range(2)]
        selTs = [P.sb("sel2_%d" % i, [128, 256], BF16) for i in range(2)]
        for _b in selTs:
            V_(P, lambda e: e.memset(_b.ap, 0.0), [], [_b])
        V_(P, lambda e: e.memset(QC.ap, 0.0), [], [QC])
        V_(P, lambda e: e.memset(VS.ap[:, :, 64:65], 1.0), [], [VS])
        V_(P, lambda e: e.memset(VW.ap[:, :, 64:65], 1.0), [], [VW])

        def selT_store(selnb, selT):
            pb = ptr.ap.bitcast(BF16)
            T_(P, lambda e: e.transpose(out=pb[:, 0:128], in_=selnb.ap, identity=idn.ap), [selnb, idn], [ptr])
            A_(P, lambda e: e.copy(out=selT.ap[0:64, 0:128], in_=pb[0:64, 0:128]), [ptr], [selT])
            V_(P, lambda e: e.tensor_copy(out=selT.ap[0:64, 128:256], in_=pb[0:64, 0:128]), [ptr], [selT])

        def epi_cmp(h, r, s, O, sm, acc, imp, sc, sc2, m8, selnb, selT, state):
            V_(P, lambda e: e.tensor_tensor(out=sm.ap[:, 4:5], in0=O.ap[:, 64:65], in1=consts.ap[:, 31:32], op=ALU.add), [O, consts], [sm])
            V_(P, lambda e: e.reciprocal(out=sm.ap[:, 4:5], in_=sm.ap[:, 4:5]), [sm], [sm])
            V_(P, lambda e: e.tensor_tensor(out=sm.ap[:, 5:6], in0=sm.ap[:, 4:5], in1=cg_t.ap[:, s, 3 * h:3 * h + 1], op=ALU.mult), [sm, cg_t], [sm])
            V_(P, lambda e: e.tensor_scalar(out=acc.ap[:, r, :], in0=O.ap[:, 0:64], scalar1=sm.ap[:, 5:6], scalar2=None, op0=ALU.mult), [O, sm], [acc])
            if r == 0:
                V_(P, lambda e: e.tensor_scalar(out=imp.ap, in0=O.ap[:, 65:129], scalar1=sm.ap[:, 4:5], scalar2=None, op0=ALU.mult), [O, sm], [imp])
            else:
                V_(P, lambda e: e.scalar_tensor_tensor(out=imp.ap, in0=O.ap[:, 65:129], scalar=sm.ap[:, 4:5], in1=imp.ap, op0=ALU.mult, op1=ALU.add), [O, sm, imp], [imp])
            if r == 3:
                V_(P, lambda e: e.tensor_tensor(out=sc.ap, in0=imp.ap, in1=SELB.ap[:, s, :], op=ALU.add), [imp, SELB], [sc])
                V_(P, lambda e: e.max(out=m8.ap, in_=sc.ap), [sc], [m8])
                V_(P, lambda e: e.match_replace(out=sc2.ap, in_to_replace=m8.ap, in_values=sc.ap, imm_value=-3.0e9), [m8, sc], [sc2])
                V_(P, lambda e: e.max(out=m8.ap, in_=sc2.ap), [sc2], [m8])
                for c in range(2):
                    V_(P, lambda e: e.tensor_scalar(out=selnb.ap[:, c * 64:(c + 1) * 64], in0=sc.ap, scalar1=m8.ap[:, 7:8], scalar2=NEG, op0=ALU.is_lt, op1=ALU.mult), [sc, m8], [selnb])

                def fire():
                    selT_store(selnb, selT)
                    state['selT_ready'] = True
                pipe.defer(2, fire)

        def epi_br(h, r, s, g, br, O, sm, acc, otm_, last):
            V_(P, lambda e: e.reciprocal(out=sm.ap[:, 4:5], in_=O.ap[:, 64:65]), [O], [sm])
            V_(P, lambda e: e.tensor_tensor(out=sm.ap[:, 5:6], in0=sm.ap[:, 4:5], in1=cg_t.ap[:, s, 3 * h + 1 + br:3 * h + 2 + br], op=ALU.mult), [sm, cg_t], [sm])
            V_(P, lambda e: e.scalar_tensor_tensor(out=acc.ap[:, r, :], in0=O.ap[:, 0:64], scalar=sm.ap[:, 5:6], in1=acc.ap[:, r, :], op0=ALU.mult, op1=ALU.add), [O, sm, acc], [acc])
            if last:
                V_(P, lambda e: e.tensor_copy(out=otm_.ap[:, 0:256], in_=acc.ap.rearrange("p r d -> p (r d)")), [acc], [otm_])
                pipe.defer(2, partial(store_ot, s, 8 + 2 * g, 2, otm_))

        def push_cmp(g, s, r, nct, O, done):
            p0 = 64 * (r % 2)
            qap = QC.ap[p0:p0 + 64, r // 2, s, r % 2, :]
            k = Wk['scnt']
            Wk['scnt'] += 1
            Sb = Wk['sps'][k % len(Wk['sps'])]
            PT = Wk['pts'][k % len(Wk['pts'])]
            tmp = Wk['stmps'][Wk['tcnt'] % len(Wk['stmps'])]
            Wk['tcnt'] += 1

            def s_fn():
                for t in range(nct):
                    T_(P, lambda e: e.matmul(out=Sb.ap[:, t * 128:(t + 1) * 128], lhsT=kcT.ap[p0:p0 + 64, g, t * 128:(t + 1) * 128], rhs=qap, start=True, stop=True), [kcT, QC], [Sb])

            def e_fn():
                for t in range(nct):
                    V_(P, lambda e: e.tensor_tensor(out=tmp.ap[:, t * 128:(t + 1) * 128], in0=Sb.ap[:, t * 128:(t + 1) * 128], in1=CM.ap[:, t, s, :], op=ALU.add), [Sb, CM], [tmp])
                A_(P, lambda e: e.activation(out=PT.ap[:, 0:nct * 128], in_=tmp.ap[:, 0:nct * 128], func=ACT.Exp), [tmp], [PT])

            def pv_fn():
                for t in range(nct):
                    T_(P, lambda e: e.matmul(out=O.ap[:, 0:129], lhsT=PT.ap[:, t * 128:(t + 1) * 128], rhs=vca.ap[:, g, t, :], start=(t == 0), stop=(t == nct - 1)), [PT, vca], [O])
                done()

            pipe.push(s_fn, e_fn, pv_fn)

        ucnt = 0
        gs = 0
        for g in range(2):
            load_kT(P, KS, KS.ap, 8 + g)
            load_kT(P, KW, KW.ap, 10 + g)
            load_V(P, VS, VS.ap[:, :, 0:64], 640 + g * 64, 64)
            load_V(P, VW, VW.ap[:, :, 0:64], 768 + g * 64, 64)
            for pr_ in range(2):
                for c in range(2):
                    P.dma('sync', QC.ap[64 * c:64 * c + 64, pr_, :, c, :], qt[8 + 2 * g + pr_][64 * c:64 * c + 64, :].rearrange("p (s q) -> p s q", q=128), [qt_tok], [QC])
            P.dma('sync', BMS.ap, bms[g], [], [BMS])
            P.dma('sync', BMW.ap, bmw[g], [], [BMW])
            for s in range(NS):
                n = nproc(s)
                nct = 1 if 8 * n <= 128 else 2
                i2 = gs % 2
                gs += 1
                acc, imp, sc, sc2, m8, selnb, selT, otm_ = accs[i2], imps[i2], scs[i2], sc2s[i2], m8s[i2], selnbs[i2], selTs[i2], otms[gs % 4]
                state = dict(selT_ready=False)
                for r in range(4):
                    h = 4 * g + r
                    O = Ob[ucnt % 4]
                    done = partial(epi_cmp, h, r, s, O, smr[ucnt % 8], acc, imp, sc, sc2, m8, selnb, selT, state)
                    ucnt += 1
                    push_cmp(g, s, r, nct, O, done)
                for pr_ in range(2):
                    Os = (Ob[(2 * ucnt) % 4], Ob[(2 * ucnt + 1) % 4])
                    d0 = partial(epi_br, 4 * g + 2 * pr_, 2 * pr_, s, g, 1, Os[0], smr[(2 * ucnt) % 8], acc, otm_, False)
                    d1 = partial(epi_br, 4 * g + 2 * pr_ + 1, 2 * pr_ + 1, s, g, 1, Os[1], smr[(2 * ucnt + 1) % 8], acc, otm_, False)
                    ucnt += 1

                    def done(d0=d0, d1=d1):
                        d0()
                        d1()
                    kbs = [(kb, kb - (n - 6)) for kb in range(max(0, n - 6), n)]
                    attn_pair_p(pipe, P, Wk, QC.ap[:, pr_, s, :, :].rearrange("p j q -> p (j q)"), QC,
                                lambda kb: KW.ap[:, kb * 128:(kb + 1) * 128], KW, kbs,
                                lambda kb: VW.ap[:, kb, :], VW, 65, Os,
                                lambda rr: BMW.ap[:, 2 * pr_:2 * pr_ + 2, (s % 2) * 6 + rr, :], BMW, (None, None), on_done=done)
                assert state['selT_ready'], "selection mask must be emitted before the selected-block branch"
                for pr_ in range(2):
                    Os = (Ob[(2 * ucnt) % 4], Ob[(2 * ucnt + 1) % 4])
                    h0 = 4 * g + 2 * pr_
                    d0 = partial(epi_br, h0, 2 * pr_, s, g, 0, Os[0], smr[(2 * ucnt) % 8], acc, otm_, False)
                    d1 = partial(epi_br, h0 + 1, 2 * pr_ + 1, s, g, 0, Os[1], smr[(2 * ucnt + 1) % 8], acc, otm_, pr_ == 1)
                    ucnt += 1

                    def done(d0=d0, d1=d1):
                        d0()
                        d1()
                    attn_pair_p(pipe, P, Wk, QC.ap[:, pr_, s, :, :].rearrange("p j q -> p (j q)"), QC,
                                lambda kb: KS.ap[:, kb * 128:(kb + 1) * 128], KS, near_far(n, 3),
                                lambda kb: VS.ap[:, kb, :], VS, 65, Os,
                                lambda rr: BMS.ap[:, 2 * pr_:2 * pr_ + 2, (s % 2) * 3 + rr, :], BMS,
                                (consts.ap[:, 12 + h0:13 + h0], consts.ap[:, 13 + h0:14 + h0]), sel=(Et, selT.ap, selT), on_done=done)
            pipe.flush()


RANK_SLOT_C = {(0, 0): 0, (0, 1): 3, (1, 0): 1, (1, 1): 2}
PAIRS = [[0, 1], [2, 3], [4, 5], [6, 7]]
NL = 4
SAME_ENGINE_SYNC = True


def build_fused(n_layers=NL):
    nc = bass.Bass("TRN2", target_bir_lowering=False)

    def din(name, shape, dt=F32):
        return nc.dram_tensor(name, list(shape), dt, kind="ExternalInput").ap()

    x = din("x", [TOK, D])
    w_in = din("w_in", [NL, D, 6680])
    wbr = din("wbr", [NL, 3, 512, D])
    wout = din("wout", [NL, D, D])
    wup = din("wup", [NL, D, 4096])
    wdn = din("wdn", [NL, 4096, D])
    nm = din("nm", [NL, 128, 8])
    nmlp = din("nmlp", [NL, 128, 8])
    gains = din("gains", [NL, 128, 8, 64])
    w1 = din("w1", [NL, 2, 2048, 256])
    w2 = din("w2", [NL, 2, 256, 64])
    posT = din("posT", [NL, 2, 64, 32])
    dl = din("dl", [NL, 128, 4, 64])
    lamc = din("lamc", [NL, 128, 2])
    subln = din("subln", [NL, 128, 128])
    sinks = din("sinks", [NL, 128, 8])
    cfar = din("cfar", [128, 20])
    bma = din("bma", [4, 128, 6, 128])
    bmb = din("bmb", [2, 128, 4, 6, 128])
    bms = din("bms", [2, 128, 4, 6, 128])
    bmw = din("bmw", [2, 128, 4, 12, 128])
    cm = din("cm", [128, 2, NS, 128])
    selb = din("selb", [128, NS, 64])
    E_d = din("E", [64, S], BF16)
    ov_d = din("ov", [128, 2, 64], BF16)
    idn_d = din("idn", [128, 128], BF16)
    xo = nc.dram_tensor("xo", [TOK, D], F32, kind="ExternalOutput").ap()
    xdbg_all = nc.dram_tensor("xdbg", [NL, TOK, D], F32, kind="ExternalOutput").ap() if _DBG else None
    qt_d = nc.dram_tensor("qt_d", [12, 128, TOK], BF16).ap()
    cgs_d = nc.dram_tensor("cgs_d", [TOK, 24], F32).ap()
    xmid = nc.dram_tensor("xmid", [TOK, D], F32).ap()
    xbuf = [nc.dram_tensor("xbuf%d" % i, [TOK, D], F32).ap() for i in range(2)]
    exK = [[nc.dram_tensor("exK%d_%d" % (i, q), [4 * 128, TOK], BF16).ap() for q in range(3)] for i in range(2)]
    exV = [[nc.dram_tensor("exV%d_%d" % (i, u), [TOK // 2, 896], BF16).ap() for u in range(2)] for i in range(2)]
    exKo = [[nc.dram_tensor("exKo%d_%d" % (i, q), [2 * 4 * 128, TOK], BF16).ap() for q in range(3)] for i in range(2)]
    exVo = [[nc.dram_tensor("exVo%d_%d" % (i, u), [TOK, 896], BF16).ap() for u in range(2)] for i in range(2)]

    with ExitStack() as es:
        P = Prog(nc, es, same_engine_sync=SAME_ENGINE_SYNC)
        idn = P.sb("idn_t", [128, 128], BF16)
        eps_t = P.sb("eps_t", [128, 1], F32)
        ps = [P.ps("ps%d" % i) for i in range(8)]
        P.dma('sync', idn.ap, idn_d, [], [idn])
        V_(P, lambda e: e.memset(eps_t.ap, EPS), [], [eps_t])
        toks = dict(qt=Tok("qt"), cgs=Tok("cgs"), xmid=Tok("xmid"), xb=[Tok("xb0"), Tok("xb1")], xo=Tok("xo"), x=Tok("x"),
                    exK=[[Tok("exK") for q in range(3)] for i in range(2)], exV=[[Tok("exV") for u in range(2)] for i in range(2)],
                    exKo=[[Tok("exKo") for q in range(3)] for i in range(2)], exVo=[[Tok("exVo") for u in range(2)] for i in range(2)])
        for l in range(n_layers):
            par = l % 2
            if l == 0:
                xsrc, xsrc_tok = x, toks['x']
            else:
                xsrc, xsrc_tok = xbuf[(l - 1) % 2], toks['xb'][(l - 1) % 2]
            if l == n_layers - 1:
                xdst, xdst_tok = xo, toks['xo']
            else:
                xdst, xdst_tok = xbuf[l % 2], toks['xb'][l % 2]
            Ko, Vo = exKo[par], exVo[par]
            Ko_tok, Vo_tok = toks['exKo'][par], toks['exVo'][par]

            def load_kT(P, buf, dst_ap, tile, Ko=Ko, Ko_tok=Ko_tok):
                d4 = dst_ap.rearrange("p (m c t) -> p m c t", c=4, t=128)
                kq, kr = divmod(tile, 4)
                for rank in range(2):
                    src = Ko[kq][rank * 512 + kr * 128:rank * 512 + (kr + 1) * 128, :].rearrange("p (m o t) -> p m o t", o=2, t=128)
                    for odd in range(2):
                        P.dma('sync', d4[:, :, RANK_SLOT_C[(rank, odd)], :], src[:, :, odd, :], [Ko_tok[kq]], [buf])

            def load_V(P, buf, dst_ap3, col0, ncols, Vo=Vo, Vo_tok=Vo_tok):
                d4 = dst_ap3.rearrange("p (m c) v -> p m c v", c=4)
                for u in range(2):
                    for rank in range(2):
                        src = Vo[u][rank * 1024:(rank + 1) * 1024, col0:col0 + ncols].rearrange("(m o p) v -> p m o v", o=2, p=128)
                        for odd in range(2):
                            P.dma('sync', d4[:, 4 * u:4 * u + 4, RANK_SLOT_C[(rank, odd)], :], src[:, :, odd, :], [Vo_tok[u]], [buf])

            c = dict(idn=idn, eps_t=eps_t, ps=ps,
                     x_rows=(lambda s, xsrc=xsrc: xsrc[s * 128:(s + 1) * 128, :]), x_tok=xsrc_tok,
                     wv=w_in[l].rearrange("(kc p) c -> p kc c", p=128), nm=nm[l], gains=gains[l],
                     qt_d=qt_d, qt_tok=toks['qt'], exK3=[a.rearrange("(k p) t -> k p t", p=128) for a in exK[par]], exK_tok=toks['exK'][par],
                     exV=exV[par], exV_tok=toks['exV'][par], cgs_d=cgs_d, cgs_tok=toks['cgs'],
                     wmg=w_in[l][:, C1:6680], wbr=wbr[l], wout=wout[l], wup=wup[l], wdn=wdn[l], nmlp=nmlp[l],
                     w1=w1[l], w2=w2[l], posT=posT[l], g5=gains[l][:, 5, :], dl=dl[l], lamc=lamc[l], subln=subln[l], sinks=sinks[l],
                     cfar=cfar, bma=bma, bmb=bmb, bms=bms, bmw=bmw, cm=cm, selb=selb, E_d=E_d, ov_d=ov_d,
                     xdbg=(xdbg_all[l] if _DBG else None), xmid=xmid, xmid_tok=toks['xmid'], xout=xdst, xout_tok=xdst_tok, load_kT=load_kT, load_V=load_V)
            emit_phase1(P, c)
            P.barrier()
            for q in range(3):
                P.collective(exK[par][q], Ko[q], [toks['exK'][par][q]], [Ko_tok[q]], PAIRS)
            for u in range(2):
                P.collective(exV[par][u], Vo[u], [toks['exV'][par][u]], [Vo_tok[u]], PAIRS)
            emit_phase23(P, c)
        P.es = es
        P.finish()
    return nc


BF = ml_dtypes.bfloat16


def _bucket(dist):
    n = np.maximum(dist, 0)
    nf = np.maximum(n, 1).astype(np.float32)
    large = 16 + (np.log(nf / np.float32(16)) / np.float32(np.log(8.0)) * np.float32(16)).astype(np.int32)
    large = np.minimum(large, 31)
    return np.where(n < 16, n, large)


def _bm_tile(rel_bias, col, delta, window):
    k = np.arange(128)[:, None]
    q = np.arange(128)[None, :]
    dist = delta * 128 + q - k
    vis = dist >= 0
    if window is not None:
        vis = vis & (dist < window)
    vals = rel_bias[_bucket(dist), col].astype(np.float32)
    return np.where(vis, vals, np.float32(NEG)).astype(np.float32)


def _bm_table(rel_bias, j, cols, nnear, window):
    out = np.zeros((len(cols), 128, 2 * nnear, 128), np.float32)
    for par in range(2):
        for r in range(nnear):
            delta = qi_of(j, par) - nproc(par) + nnear - r
            for ci, col in enumerate(cols):
                out[ci, :, par * nnear + r, :] = _bm_tile(rel_bias, col, delta, window)
    return out


def _core_tables(rel_bias, j):
    t = {}
    t['bma'] = _bm_table(rel_bias, j, list(range(0, 4)), 3, None)
    bmb = _bm_table(rel_bias, j, list(range(4, 12)), 3, 128)
    t['bmb'] = np.ascontiguousarray(bmb.reshape(2, 4, 128, 6, 128).transpose(0, 2, 1, 3, 4))
    bms = _bm_table(rel_bias, j, list(range(12, 20)), 3, None)
    t['bms'] = np.ascontiguousarray(bms.reshape(2, 4, 128, 6, 128).transpose(0, 2, 1, 3, 4))
    bmw = _bm_table(rel_bias, j, list(range(12, 20)), 6, 512)
    t['bmw'] = np.ascontiguousarray(bmw.reshape(2, 4, 128, 12, 128).transpose(0, 2, 1, 3, 4))
    cm = np.zeros((128, 2, NS, 128), np.float32)
    selb = np.zeros((128, NS, 64), np.float32)
    for s in range(NS):
        qi = qi_of(j, s)
        tq = 128 * qi + np.arange(128)
        for tl in range(2):
            c = tl * 128 + np.arange(128)
            vis = (16 * c[:, None] + 31 <= tq[None, :]) & (c[:, None] < 255)
            cm[:, tl, s, :] = np.where(vis, 0.0, NEG)
        jb = np.arange(64)[None, :]
        valid = 64 * jb <= tq[:, None]
        cur = tq[:, None] // 64
        forced = (jb == 0) | (jb == cur) | (jb == cur - 1)
        selb[:, s, :] = np.where(valid, np.where(forced, 1.0e4, 0.0), -1.0e9)
    t['cm'] = cm
    t['selb'] = selb
    return t


_CONST = {}


def _consts():
    if not _CONST:
        tt = np.arange(S)
        _CONST['E'] = (np.arange(64)[:, None] == (tt[None, :] // 64)).astype(BF)
        c = (np.arange(2)[None, :, None] * 128 + np.arange(128)[:, None, None])
        jb = np.arange(64)[None, None, :]
        ov = (16 * c < 64 * jb + 64) & (16 * c + 32 > 64 * jb) & (c < 255)
        _CONST['ov'] = ov.astype(BF)
        _CONST['idn'] = np.eye(128).astype(BF)
    return _CONST


_PROGS = {}


def _rep(v):
    v = np.asarray(v, np.float32)
    return np.ascontiguousarray(np.broadcast_to(v[None], (128,) + v.shape))


def _repl(v):
    v = np.asarray(v, np.float32)
    return np.ascontiguousarray(np.broadcast_to(v[:, None], (v.shape[0], 128) + v.shape[1:]))


def kernel(x, w_in, qk_gain, diff_lambda, diff_subln, sinks, cmp_pos, cmp_w1, cmp_w2,
           w_branch, w_out, norm_mix, norm_mlp, w_up, w_down, rel_bias, _n_layers=NL):
    import math
    x = np.asarray(x, np.float32)
    rel_bias = np.asarray(rel_bias, np.float32)
    if ('fused', _n_layers) not in _PROGS:
        _PROGS[('fused', _n_layers)] = build_fused(_n_layers)
    cst = _consts()
    cores = [(b, j) for b in range(4) for j in range(2)]
    tabs = [_core_tables(rel_bias, j) for j in range(2)]
    f32 = lambda a: np.ascontiguousarray(np.asarray(a, np.float32))
    lamc = np.array([[0.8 - 0.6 * math.exp(-0.3 * l), 1.0 - (0.8 - 0.6 * math.exp(-0.3 * l))] for l in range(NL)], np.float32)
    shared = dict(
        w_in=f32(w_in), wbr=f32(w_branch), wout=f32(w_out), wup=f32(w_up), wdn=f32(w_down),
        nm=np.ascontiguousarray(f32(norm_mix).reshape(NL, 8, 128).transpose(0, 2, 1)),
        nmlp=np.ascontiguousarray(f32(norm_mlp).reshape(NL, 8, 128).transpose(0, 2, 1)),
        gains=_repl(qk_gain), w1=f32(cmp_w1), w2=f32(cmp_w2),
        posT=np.ascontiguousarray(f32(cmp_pos).transpose(0, 1, 3, 2)),
        dl=_repl(diff_lambda), lamc=_repl(lamc), subln=_repl(diff_subln), sinks=_repl(sinks),
        cfar=_rep(rel_bias[31]), E=cst['E'], ov=cst['ov'], idn=cst['idn'])
    in_maps = []
    for (b, j) in cores:
        d = dict(shared)
        d['x'] = np.concatenate([x[b, qi_of(j, s) * 128:(qi_of(j, s) + 1) * 128] for s in range(NS)], 0)
        d.update(tabs[j])
        in_maps.append(d)
    res = run_bass_kernel_spmd(_PROGS[('fused', _n_layers)], in_maps, core_ids=list(range(8))).results
    if _DBG:
        _DBG_OUT['res'] = res
    out = np.zeros((4, S, D), np.float32)
    for i, (b, j) in enumerate(cores):
        xo = np.asarray(res[i]['xo'])
        for s in range(NS):
            qi = qi_of(j, s)
            out[b, qi * 128:(qi + 1) * 128] = xo[s * 128:(s + 1) * 128]
    return out
```

```python
import numpy as np
import ml_dtypes
from contextlib import ExitStack
import concourse.bass as bass
import concourse.mybir as mybir
from concourse.bass_utils import run_bass_kernel_spmd

F32 = mybir.dt.float32
BF16 = mybir.dt.bfloat16
ALU = mybir.AluOpType
ACT = mybir.ActivationFunctionType
AX = mybir.AxisListType

ENGINES = ('tensor', 'vector', 'scalar', 'gpsimd', 'sync')
NDMA_SEMS = 24
SEM_EPOCH = 30000


class Buf:
    __slots__ = ('ap', 'w', 'r', 'name')

    def __init__(self, ap, name):
        self.ap = ap
        self.w = None
        self.r = {}
        self.name = name


class Tok(Buf):
    def __init__(self, name=''):
        Buf.__init__(self, None, name)


class _Rec:
    def __init__(self):
        self.name = None

    def __getattr__(self, name):
        def f(*args, **kwargs):
            self.name = name
            self.args = args
            self.kwargs = kwargs
            return self
        return f


class Prog:
    def __init__(self, nc, es, same_engine_sync=True):
        self.nc = nc
        self.es = es
        self.sem_es = es
        self.streams = {e: [] for e in ENGINES}
        self.cur = {}
        self.waited = {e: {} for e in ENGINES}
        self.nsem = 0
        self.dma_sems = []
        self.dma_pools = {}
        self.dma_rrs = {}
        self.same_engine_sync = same_engine_sync
        self.n_ops = 0

    def new_sem(self, name):
        s = self.sem_es.enter_context(self.nc.semaphore(name))
        sid = self.nsem
        self.nsem += 1
        return s, sid

    def sb(self, name, shape, dtype):
        self.n_sb = getattr(self, 'n_sb', 0) + 1
        t = self.es.enter_context(self.nc.sbuf_tensor("%s_u%d" % (name, self.n_sb), list(shape), dtype))
        return Buf(t.ap(), name)

    def ps(self, name):
        t = self.es.enter_context(self.nc.psum_tensor(name, [128, 512], F32))
        return Buf(t.ap(), name)

    def _event(self, e):
        c = self.cur.get(e)
        if c is None or c[1] >= SEM_EPOCH:
            s, sid = self.new_sem("s_%s_%d" % (e, self.nsem))
            c = [s, 0, sid]
            self.cur[e] = c
        c[1] += 1
        return (c[0], c[1], c[2], e)

    def _wait(self, e, ev):
        sem, val, sid, src = ev
        w = self.waited[e]
        if w.get(sid, 0) >= val:
            return
        w[sid] = val
        self.streams[e].append(('w', sem, val))

    def _deps(self, e, reads, writes):
        for t in reads:
            if t.w is not None:
                self._dep1(e, t.w)
        for t in writes:
            if t.w is not None:
                self._dep1(e, t.w)
            for ev in t.r.values():
                self._dep1(e, ev)

    def _dep1(self, e, ev):
        if ev[3] == e and (e == 'tensor' or not self.same_engine_sync):
            return
        self._wait(e, ev)

    def _mark(self, ev, reads, writes):
        for t in reads:
            t.r[ev[2]] = ev
        for t in writes:
            t.w = ev
            t.r = {}

    def op(self, e, fn, reads=(), writes=()):
        rec = _Rec()
        fn(rec)
        self._deps(e, reads, writes)
        ev = self._event(e)
        self.streams[e].append(('i', rec, ev[0]))
        self._mark(ev, reads, writes)
        self.n_ops += 1
        return ev

    def dma(self, e, out, in_, reads=(), writes=()):
        pool = self.dma_pools.setdefault(e, [])
        if not pool:
            for i in range(NDMA_SEMS if e == 'sync' else 8):
                s, sid = self.new_sem("s_dma_%s_%d" % (e, i))
                ent = [s, 0, sid]
                pool.append(ent)
                self.dma_sems.append(ent)
        self.dma_rrs[e] = self.dma_rrs.get(e, 0) + 1
        ds = pool[self.dma_rrs[e] % len(pool)]
        if ds[1] > 0:
            self._wait(e, (ds[0], ds[1], ds[2], 'dma'))
        self._deps(e, reads, writes)
        ds[1] += 16
        ev = (ds[0], ds[1], ds[2], 'dma')
        self.streams[e].append(('d', out, in_, ds[0]))
        self._mark(ev, reads, writes)
        self.n_ops += 1
        return ev

    def collective(self, in_ap, out_ap, reads, writes, groups):
        e = 'gpsimd'
        self._deps(e, reads, writes)
        sem, sid = self.new_sem("s_cc_%d" % self.nsem)
        ev = (sem, 1, sid, 'cc')
        self.streams[e].append(('c', in_ap, out_ap, sem, groups))
        self._mark(ev, reads, writes)
        self.n_ops += 1
        return ev

    def barrier(self):
        evs = []
        for e, c in self.cur.items():
            evs.append((c[0], c[1], c[2], e))
        for ds in self.dma_sems:
            if ds[1] > 0:
                evs.append((ds[0], ds[1], ds[2], 'dma'))
        for e in ENGINES:
            for ev in evs:
                if ev[3] != e:
                    self._wait(e, ev)

    def finish(self):
        for ds in self.dma_sems:
            if ds[1] > 0:
                self._wait('sync', (ds[0], ds[1], ds[2], 'dma'))
        streams = self.streams

        def replay(eng, name):
            for it in streams[name]:
                if it[0] == 'w':
                    eng.wait_ge(it[1], it[2])
                elif it[0] == 'i':
                    getattr(eng, it[1].name)(*it[1].args, **it[1].kwargs).then_inc(it[2], 1)
                elif it[0] == 'c':
                    eng.collective_compute("AllGather", ALU.bypass, replica_groups=it[4], ins=[it[1]], outs=[it[2]]).then_inc(it[3])
                else:
                    eng.dma_start(out=it[1], in_=it[2]).then_inc(it[3], 16)

        with self.nc.Block() as block:
            @block.sync
            def _(eng):
                replay(eng, 'sync')

            @block.tensor
            def _(eng):
                replay(eng, 'tensor')

            @block.vector
            def _(eng):
                replay(eng, 'vector')

            @block.scalar
            def _(eng):
                replay(eng, 'scalar')

            @block.gpsimd
            def _(eng):
                replay(eng, 'gpsimd')


D = 1024
S = 4096
NS = 16
TOK = NS * 128
EPS = 1e-6
NEG = -30000.0
C1 = 3608


def qi_of(j, s):
    m, odd = divmod(s, 2)
    if j == 0:
        return 4 * m + (3 if odd else 0)
    return 4 * m + (2 if odd else 1)


def nproc(s):
    m, odd = divmod(s, 2)
    return 4 * m + (4 if odd else 2)


def V_(P, fn, reads, writes):
    return P.op('vector', fn, reads, writes)


def A_(P, fn, reads, writes):
    return P.op('scalar', fn, reads, writes)


def T_(P, fn, reads, writes):
    return P.op('tensor', fn, reads, writes)


def G_(P, fn, reads, writes):
    return P.op('gpsimd', fn, reads, writes)


def emit_rstd(P, ss, n, eps_t):
    A_(P, lambda e: e.activation(out=ss.ap, in_=ss.ap, func=ACT.Sqrt, scale=1.0 / n, bias=eps_t.ap[:, 0:1]), [ss, eps_t], [ss])
    V_(P, lambda e: e.reciprocal(out=ss.ap, in_=ss.ap), [ss], [ss])


def emit_xnorm_T(P, xsrc_ap, xsrc_tok, hT, s, W):
    sq, ss, xb, pT, idn, eps_t = W['sq1024'], W['ss1'], W['xb'], W['pT'], W['idn'], W['eps']
    A_(P, lambda e: e.activation(out=sq.ap, in_=xsrc_ap, func=ACT.Square, accum_out=ss.ap[:, 0:1]), [xsrc_tok], [sq, ss])
    emit_rstd(P, ss, 1024, eps_t)
    V_(P, lambda e: e.tensor_scalar(out=xb.ap, in0=xsrc_ap, scalar1=ss.ap[:, 0:1], scalar2=None, op0=ALU.mult), [xsrc_tok, ss], [xb])
    pb = pT.ap.bitcast(BF16)
    for kc in range(8):
        T_(P, lambda e, kc=kc: e.transpose(out=pb[:, kc * 128:(kc + 1) * 128], in_=xb.ap[:, kc * 128:(kc + 1) * 128], identity=idn.ap), [xb, idn], [pT])
    A_(P, lambda e: e.copy(out=hT.ap[:, :, s * 128:(s + 1) * 128], in_=pb.rearrange("p (k t) -> p k t", k=8)), [pT], [hT])


PH1_GROUPS = [
    (0, 512, [(0, 512, 'n', dict(gi=0, q=True, dup=False, fm0=0))]),
    (512, 512, [(0, 512, 'n', dict(gi=1, q=False, dup=False, fm0=12))]),
    (1024, 512, [(0, 512, 'v', dict(tmcol=0))]),
    (1536, 512, [(0, 512, 'n', dict(gi=2, q=True, dup=False, fm0=4))]),
    (2048, 256, [(0, 128, 'n', dict(gi=3, q=False, dup=True, fm0=16)), (128, 128, 'v', dict(tmcol=512))]),
    (2304, 512, [(0, 512, 'n', dict(gi=4, q=True, dup=False, fm0=8))]),
    (2816, 512, [(0, 128, 'raw', dict(fm0=18)), (128, 128, 'raw', dict(fm0=19)),
                 (256, 128, 'n', dict(gi=6, q=False, dup=True, fm0=20)), (384, 128, 'v', dict(tmcol=640))]),
    (3328, 280, [(0, 128, 'n', dict(gi=7, q=False, dup=True, fm0=22)), (128, 128, 'v', dict(tmcol=768)),
                 (256, 24, 'cg', dict())]),
]


def emit_phase1(P, cx):
    dbg = False
    idn, eps_t, ps = cx['idn'], cx['eps_t'], cx['ps']
    wv, nm, gains = cx['wv'], cx['nm'], cx['gains']
    with ExitStack() as es1:
        P.es = es1
        W = dict(idn=idn, eps=eps_t, pT=ps[6])
        W['sq1024'] = P.sb("sq1024", [128, 1024], F32)
        W['ss1'] = P.sb("ss1", [128, 1], F32)
        W['xb'] = P.sb("xb", [128, 1024], BF16)
        nm_t = P.sb("nm_t", [128, 8], F32)
        g_t = P.sb("g_t", [128, 8, 64], F32)
        hT = P.sb("hT", [128, 8, TOK], BF16)
        xs = [P.sb("xs%d" % i, [128, 1024], F32) for i in range(2)]
        wst = [P.sb("wst%d" % i, [128, 8, 512], F32) for i in range(2)]
        wbf = [P.sb("wbf%d" % i, [128, 8, 512], BF16) for i in range(2)]
        fms = [P.sb("fms%d" % i, [128, 4, TOK], BF16) for i in range(2)]
        tms = P.sb("tms", [128, NS, 896], BF16)
        cg_t = P.sb("cg_t1", [128, NS, 24], F32)
        pj = ps[0:2]
        pq = ps[2:4]
        Tt = [P.sb("Tt%d" % i, [128, 512], F32) for i in range(4)]
        SQs = [P.sb("SQ%d" % i, [128, 512], F32) for i in range(4)]
        SSs = [P.sb("SS%d" % i, [128, 8], F32) for i in range(4)]
        TB = [P.sb("TB%d" % i, [128, 512], BF16) for i in range(4)]

        P.dma('sync', nm_t.ap, nm, [], [nm_t])
        P.dma('sync', g_t.ap, gains, [], [g_t])

        def load_w(gidx):
            c0, wd, _ = PH1_GROUPS[gidx]
            b = gidx % 2
            P.dma('sync', wst[b].ap[:, :, 0:wd], wv[:, :, c0:c0 + wd], [], [wst[b]])

        def conv_w(gidx):
            c0, wd, _ = PH1_GROUPS[gidx]
            b = gidx % 2
            for kc in range(8):
                if kc % 2 == 0:
                    V_(P, lambda e, kc=kc: e.tensor_scalar(out=wbf[b].ap[:, kc, 0:wd], in0=wst[b].ap[:, kc, 0:wd], scalar1=nm_t.ap[:, kc:kc + 1],
                                                           scalar2=None, op0=ALU.mult), [wst[b], nm_t], [wbf[b]])
                else:
                    A_(P, lambda e, kc=kc: e.activation(out=wbf[b].ap[:, kc, 0:wd], in_=wst[b].ap[:, kc, 0:wd], func=ACT.Copy,
                                                        scale=nm_t.ap[:, kc:kc + 1]), [wst[b], nm_t], [wbf[b]])

        load_w(0)
        for s in range(NS):
            b = s % 2
            P.dma('gpsimd', xs[b].ap, cx['x_rows'](s), [cx['x_tok']], [xs[b]])
            emit_xnorm_T(P, xs[b].ap, xs[b], hT, s, W)
        items = [(gidx, s_) for gidx in range(len(PH1_GROUPS)) for s_ in range(NS)]
        ginfo = {}
        for gidx, (c0, wd, segs) in enumerate(PH1_GROUPS):
            fpos = {}
            _p = 0
            for (off, sw, kind, pr) in segs:
                if kind in ('n', 'raw'):
                    fpos[pr['fm0']] = _p
                    _p += 2 if pr.get('dup') else (1 if kind == 'raw' else sw // 128)
            ginfo[gidx] = (fpos, _p, any(sg[2] == 'n' for sg in segs), any(sg[3].get('q') for sg in segs))

        def stageA(i):
            gidx, s = items[i]
            c0, wd, segs = PH1_GROUPS[gidx]
            if s == 0:
                if gidx + 1 < len(PH1_GROUPS):
                    load_w(gidx + 1)
                conv_w(gidx)
            wb = wbf[gidx % 2]
            pp, T = pj[i % 2], Tt[i % 4]
            for kc in range(8):
                T_(P, lambda e: e.matmul(out=pp.ap[:, 0:wd], lhsT=hT.ap[:, kc, s * 128:(s + 1) * 128], rhs=wb.ap[:, kc, 0:wd],
                                         start=(kc == 0), stop=(kc == 7)), [hT, wb], [pp])
            A_(P, lambda e: e.copy(out=T.ap[:, 0:wd], in_=pp.ap[:, 0:wd]), [pp], [T])

        def stageB1(i):
            gidx, s = items[i]
            c0, wd, segs = PH1_GROUPS[gidx]
            fpos, ntot, has_n, isq = ginfo[gidx]
            T, SQ, SS = Tt[i % 4], SQs[i % 4], SSs[i % 4]
            nh = wd // 64
            if has_n:
                nw = nh * 64
                G_(P, lambda e: e.tensor_tensor(out=SQ.ap[:, 0:nw], in0=T.ap[:, 0:nw], in1=T.ap[:, 0:nw], op=ALU.mult), [T], [SQ])
                V_(P, lambda e: e.tensor_reduce(out=SS.ap[:, 0:nh], in_=SQ.ap[:, 0:nw].rearrange("p (h d) -> p h d", d=64), axis=AX.X, op=ALU.add), [SQ], [SS])
                A_(P, lambda e: e.activation(out=SS.ap, in_=SS.ap, func=ACT.Sqrt, scale=1.0 / 64, bias=eps_t.ap[:, 0:1]), [SS, eps_t], [SS])

        def stageB(i):
            gidx, s = items[i]
            c0, wd, segs = PH1_GROUPS[gidx]
            fpos, ntot, has_n, isq = ginfo[gidx]
            T, tb, SQ, SS = Tt[i % 4], TB[i % 4], SQs[i % 4], SSs[i % 4]
            nh = wd // 64
            if has_n:
                V_(P, lambda e: e.reciprocal(out=SS.ap, in_=SS.ap), [SS], [SS])
            for (off, sw, kind, pr) in segs:
                if kind == 'n':
                    h0, hn = off // 64, sw // 64
                    t0 = fpos[pr['fm0']] * 128
                    t3 = T.ap[:, off:off + sw].rearrange("p (h d) -> p h d", d=64)
                    V_(P, lambda e: e.tensor_tensor(out=t3, in0=t3, in1=SS.ap[:, h0:h0 + hn].unsqueeze(2).broadcast_to([128, hn, 64]), op=ALU.mult), [T, SS], [T])
                    gb = g_t.ap[:, pr['gi'], :].unsqueeze(1).broadcast_to([128, hn, 64])
                    if pr['dup']:
                        o4 = tb.ap[:, t0:t0 + 256].rearrange("p (g c d) -> p g c d", g=2, c=2)
                        for c in range(2):
                            V_(P, lambda e: e.tensor_tensor(out=o4[:, :, c, :], in0=t3, in1=gb, op=ALU.mult), [T, g_t], [tb])
                    else:
                        o3 = tb.ap[:, t0:t0 + sw].rearrange("p (h d) -> p h d", d=64)
                        V_(P, lambda e: e.tensor_tensor(out=o3, in0=t3, in1=gb, op=ALU.mult), [T, g_t], [tb])
                elif kind == 'raw':
                    t0 = fpos[pr['fm0']] * 128
                    V_(P, lambda e: e.tensor_copy(out=tb.ap[:, t0:t0 + sw], in_=T.ap[:, off:off + sw]), [T], [tb])
                elif kind == 'v':
                    tc_ = pr['tmcol']
                    V_(P, lambda e: e.tensor_copy(out=tms.ap[:, s, tc_:tc_ + sw], in_=T.ap[:, off:off + sw]), [T], [tms])
                else:
                    A_(P, lambda e: e.activation(out=cg_t.ap[:, s, :], in_=T.ap[:, off:off + sw], func=ACT.Sigmoid), [T], [cg_t])

        def stageC(i):
            gidx, s = items[i]
            c0, wd, segs = PH1_GROUPS[gidx]
            fpos, ntot, has_n, isq = ginfo[gidx]
            fb = fms[gidx % 2]
            if ntot > 0:
                tb, ptr = TB[i % 4], pq[i % 2]
                pb = ptr.ap.bitcast(BF16)
                for k in range(ntot):
                    T_(P, lambda e: e.transpose(out=pb[:, k * 128:(k + 1) * 128], in_=tb.ap[:, k * 128:(k + 1) * 128], identity=idn.ap), [tb, idn], [ptr])
                dst = fb.ap[:, 0:ntot, s * 128:(s + 1) * 128]
                src = pb[:, 0:ntot * 128].rearrange("p (k t) -> p k t", k=ntot)
                if isq:
                    A_(P, lambda e: e.activation(out=dst, in_=src, func=ACT.Copy, scale=0.125), [ptr], [fb])
                else:
                    V_(P, lambda e: e.tensor_copy(out=dst, in_=src), [ptr], [fb])
            if s == NS - 1:
                for (off, sw, kind, pr) in segs:
                    if kind in ('n', 'raw'):
                        ntile = 2 if pr.get('dup') else (1 if kind == 'raw' else sw // 128)
                        f0 = pr['fm0']
                        fi = fpos[f0]
                        if f0 < 12:
                            P.dma('sync', cx['qt_d'][f0:f0 + ntile].rearrange("k p t -> p k t"), fb.ap[:, fi:fi + ntile, :], [fb], [cx['qt_tok']])
                        else:
                            kq, kr = divmod(f0 - 12, 4)
                            P.dma('sync', cx['exK3'][kq][kr:kr + ntile].rearrange("k p t -> p k t"), fb.ap[:, fi:fi + ntile, :], [fb], [cx['exK_tok'][kq]])

        nit = len(items)
        for step in range(nit + 3):
            if step < nit:
                stageA(step)
            if 0 <= step - 1 < nit:
                stageB1(step - 1)
            if 0 <= step - 2 < nit:
                stageB(step - 2)
            if 0 <= step - 3 < nit:
                stageC(step - 3)
        for u in range(2):
            P.dma('sync', cx['exV'][u].rearrange("(s p) c -> p s c", p=128), tms.ap[:, 8 * u:8 * u + 8, :], [tms], [cx['exV_tok'][u]])
        P.dma('sync', cx['cgs_d'].rearrange("(s p) c -> p s c", p=128), cg_t.ap, [cg_t], [cx['cgs_tok']])


def attn_unit(P, Wk, q_ap, q_buf, kt_ap_fn, kt_buf, kbs, v_ap_fn, v_buf, ncols, O, bm_ap_fn, bm_buf, cfar_ap, sel=None):
    far = [(kb, r) for kb, r in kbs if r is None]
    near = [(kb, r) for kb, r in kbs if r is not None]
    chunks = [far[i:i + 4] for i in range(0, len(far), 4)] + [near[i:i + 4] for i in range(0, len(near), 4)]
    total = len(kbs)
    done = 0
    for ch in chunks:
        Sb = Wk['sps'][Wk['scnt'] % 2]
        PT = Wk['pts'][Wk['scnt'] % 2]
        Wk['scnt'] += 1
        n = len(ch)
        for i, (kb, r) in enumerate(ch):
            T_(P, lambda e: e.matmul(out=Sb.ap[:, i * 128:(i + 1) * 128], lhsT=kt_ap_fn(kb), rhs=q_ap, start=True, stop=(sel is None)),
               [kt_buf, q_buf], [Sb])
            if sel is not None:
                E, selT_ap, selT_buf, ep0 = sel
                T_(P, lambda e: e.matmul(out=Sb.ap[:, i * 128:(i + 1) * 128], lhsT=E.ap[ep0:ep0 + 64, kb * 128:(kb + 1) * 128], rhs=selT_ap,
                                         start=False, stop=True), [E, selT_buf], [Sb])
        if ch[0][1] is None:
            A_(P, lambda e: e.activation(out=PT.ap[:, 0:n * 128], in_=Sb.ap[:, 0:n * 128], func=ACT.Exp, bias=cfar_ap), [Sb, Wk['consts']], [PT])
        else:
            tmp = Wk['stmp']
            for i, (kb, r) in enumerate(ch):
                V_(P, lambda e: e.tensor_tensor(out=tmp.ap[:, i * 128:(i + 1) * 128], in0=Sb.ap[:, i * 128:(i + 1) * 128], in1=bm_ap_fn(r), op=ALU.add),
                   [Sb, bm_buf], [tmp])
            A_(P, lambda e: e.activation(out=PT.ap[:, 0:n * 128], in_=tmp.ap[:, 0:n * 128], func=ACT.Exp), [tmp], [PT])
        for i, (kb, r) in enumerate(ch):
            T_(P, lambda e: e.matmul(out=O.ap[:, 0:ncols], lhsT=PT.ap[:, i * 128:(i + 1) * 128], rhs=v_ap_fn(kb), start=(done == 0), stop=(done == total - 1)),
               [PT, v_buf], [O])
            done += 1


class Pipe:
    def __init__(self):
        self.pending = None
        self.deferred = []

    def push(self, s_fn, e_fn, pv_fn):
        s_fn()
        e_fn()
        if self.pending is not None:
            self.pending()
        self.pending = pv_fn
        self._tick()

    def _tick(self):
        cur, self.deferred = self.deferred, []
        for item in cur:
            item[0] -= 1
            if item[0] <= 0:
                item[1]()
            else:
                self.deferred.append(item)

    def defer(self, n, fn):
        self.deferred.append([n, fn])

    def flush(self):
        if self.pending is not None:
            self.pending()
            self.pending = None
        while self.deferred:
            self._tick()


def _push_chunk(pipe, P, Wk, ch, base, total, q_ap, q_buf, kt_aps, kt_buf, v_aps, v_buf, ncols, O, bm_aps, bm_buf, cfar_ap, sel, on_done):
    n = len(ch)
    k = Wk['scnt']
    Wk['scnt'] += 1
    Sb = Wk['sps'][k % len(Wk['sps'])]
    PT = Wk['pts'][k % len(Wk['pts'])]
    is_far = ch[0][1] is None

    def s_fn():
        for i, (kb, r) in enumerate(ch):
            T_(P, lambda e: e.matmul(out=Sb.ap[:, i * 128:(i + 1) * 128], lhsT=kt_aps[i], rhs=q_ap, start=True, stop=(sel is None)), [kt_buf, q_buf], [Sb])
            if sel is not None:
                E, selT_ap, selT_buf, ep0 = sel
                T_(P, lambda e: e.matmul(out=Sb.ap[:, i * 128:(i + 1) * 128], lhsT=E.ap[ep0:ep0 + 64, kb * 128:(kb + 1) * 128], rhs=selT_ap,
                                         start=False, stop=True), [E, selT_buf], [Sb])

    def e_fn():
        if is_far:
            A_(P, lambda e: e.activation(out=PT.ap[:, 0:n * 128], in_=Sb.ap[:, 0:n * 128], func=ACT.Exp, bias=cfar_ap), [Sb, Wk['consts']], [PT])
        else:
            tmp = Wk['stmps'][Wk['tcnt'] % len(Wk['stmps'])]
            Wk['tcnt'] += 1
            for i in range(n):
                V_(P, lambda e: e.tensor_tensor(out=tmp.ap[:, i * 128:(i + 1) * 128], in0=Sb.ap[:, i * 128:(i + 1) * 128], in1=bm_aps[i], op=ALU.add), [Sb, bm_buf], [tmp])
            A_(P, lambda e: e.activation(out=PT.ap[:, 0:n * 128], in_=tmp.ap[:, 0:n * 128], func=ACT.Exp), [tmp], [PT])

    def pv_fn():
        for i in range(n):
            T_(P, lambda e: e.matmul(out=O.ap[:, 0:ncols], lhsT=PT.ap[:, i * 128:(i + 1) * 128], rhs=v_aps[i], start=(base + i == 0), stop=(base + i == total - 1)),
               [PT, v_buf], [O])
        if on_done is not None:
            on_done()

    pipe.push(s_fn, e_fn, pv_fn)


def attn_unit_p(pipe, P, Wk, q_ap, q_buf, kt_ap_fn, kt_buf, kbs, v_ap_fn, v_buf, ncols, O, bm_ap_fn, bm_buf, cfar_ap, sel=None, on_done=None):
    far = [(kb, r) for kb, r in kbs if r is None]
    near = [(kb, r) for kb, r in kbs if r is not None]
    chunks = [far[i:i + 4] for i in range(0, len(far), 4)] + [near[i:i + 4] for i in range(0, len(near), 4)]
    base = 0
    for ci, ch in enumerate(chunks):
        _push_chunk(pipe, P, Wk, ch, base, len(kbs), q_ap, q_buf, [kt_ap_fn(kb) for kb, r in ch], kt_buf, [v_ap_fn(kb) for kb, r in ch], v_buf, ncols, O,
                    [bm_ap_fn(r) if r is not None else None for kb, r in ch], bm_buf, cfar_ap, sel, on_done if ci == len(chunks) - 1 else None)
        base += len(ch)


def _push_chunk2(pipe, P, Wk, ch, base, total, qbd_ap, q_buf, kt_aps, kt_buf, v_aps, v_buf, ncols, Os, bm_aps, bm_buf, cfar_aps, sel, on_done):
    n = len(ch)
    k = Wk['scnt']
    Wk['scnt'] += 1
    Sb = Wk['sps'][k % len(Wk['sps'])]
    PT = Wk['pts'][k % len(Wk['pts'])]
    is_far = ch[0][1] is None

    def s_fn():
        for i, (kb, r) in enumerate(ch):
            T_(P, lambda e: e.matmul(out=Sb.ap[:, i * 256:(i + 1) * 256], lhsT=kt_aps[i], rhs=qbd_ap, start=True, stop=(sel is None)), [kt_buf, q_buf], [Sb])
            if sel is not None:
                E, sel2_ap, sel2_buf = sel
                T_(P, lambda e: e.matmul(out=Sb.ap[:, i * 256:(i + 1) * 256], lhsT=E.ap[:, kb * 128:(kb + 1) * 128], rhs=sel2_ap,
                                         start=False, stop=True), [E, sel2_buf], [Sb])

    def e_fn():
        if is_far:
            if cfar_aps[0] is cfar_aps[1]:
                A_(P, lambda e: e.activation(out=PT.ap[:, 0:n * 256], in_=Sb.ap[:, 0:n * 256], func=ACT.Exp, bias=cfar_aps[0]), [Sb, Wk['consts']], [PT])
            else:
                for j in range(2):
                    A_(P, lambda e: e.activation(out=PT.ap[:, 0:n * 256].rearrange("p (i j q) -> p i j q", j=2, q=128)[:, :, j, :],
                                                 in_=Sb.ap[:, 0:n * 256].rearrange("p (i j q) -> p i j q", j=2, q=128)[:, :, j, :],
                                                 func=ACT.Exp, bias=cfar_aps[j]), [Sb, Wk['consts']], [PT])
        else:
            tmp = Wk['stmps'][Wk['tcnt'] % len(Wk['stmps'])]
            Wk['tcnt'] += 1
            for i in range(n):
                V_(P, lambda e: e.tensor_tensor(out=tmp.ap[:, i * 256:(i + 1) * 256].rearrange("p (j q) -> p j q", j=2),
                                                in0=Sb.ap[:, i * 256:(i + 1) * 256].rearrange("p (j q) -> p j q", j=2), in1=bm_aps[i], op=ALU.add), [Sb, bm_buf], [tmp])
            A_(P, lambda e: e.activation(out=PT.ap[:, 0:n * 256], in_=tmp.ap[:, 0:n * 256], func=ACT.Exp), [tmp], [PT])

    def pv_fn():
        for i in range(n):
            for j in range(2):
                T_(P, lambda e: e.matmul(out=Os[j].ap[:, 0:ncols], lhsT=PT.ap[:, i * 256 + j * 128:i * 256 + (j + 1) * 128], rhs=v_aps[i],
                                         start=(base + i == 0), stop=(base + i == total - 1)), [PT, v_buf], [Os[j]])
        if on_done is not None:
            on_done()

    pipe.push(s_fn, e_fn, pv_fn)


def attn_pair_p(pipe, P, Wk, qbd_ap, q_buf, kt_ap_fn, kt_buf, kbs, v_ap_fn, v_buf, ncols, Os, bm_ap_fn, bm_buf, cfar_aps, sel=None, on_done=None):
    far = [(kb, r) for kb, r in kbs if r is None]
    near = [(kb, r) for kb, r in kbs if r is not None]
    chunks = [far[i:i + 2] for i in range(0, len(far), 2)] + [near[i:i + 2] for i in range(0, len(near), 2)]
    base = 0
    for ci, ch in enumerate(chunks):
        _push_chunk2(pipe, P, Wk, ch, base, len(kbs), qbd_ap, q_buf, [kt_ap_fn(kb) for kb, r in ch], kt_buf, [v_ap_fn(kb) for kb, r in ch], v_buf, ncols, Os,
                     [bm_ap_fn(r) if r is not None else None for kb, r in ch], bm_buf, cfar_aps, sel, on_done if ci == len(chunks) - 1 else None)
        base += len(ch)


def near_far(n, nnear):
    kbs = []
    for kb in range(n):
        r = kb - (n - nnear)
        kbs.append((kb, r if r >= 0 else None))
    return kbs


_SKIP = set()
_DBG = False
_DBG_OUT = {}


def emit_phase23(P, c):
    (x_rows, x_tok, qt, qt_tok, cgs, cgs_tok, wmg, wbr, wout, wup, wdn, nm, nmlp, w1, w2, posT, g5, dl, lamc, subln, sinks, cfar,
     bma, bmb, bms, bmw, cm, selb, E_d, ov_d, xmid, xmid_tok, xout, xout_tok) = (c[k] for k in (
        'x_rows', 'x_tok', 'qt_d', 'qt_tok', 'cgs_d', 'cgs_tok', 'wmg', 'wbr', 'wout', 'wup', 'wdn', 'nm', 'nmlp', 'w1', 'w2', 'posT', 'g5', 'dl',
        'lamc', 'subln', 'sinks', 'cfar', 'bma', 'bmb', 'bms', 'bmw', 'cm', 'selb', 'E_d', 'ov_d', 'xmid', 'xmid_tok', 'xout', 'xout_tok'))
    idn, eps_t, ps = c['idn'], c['eps_t'], c['ps']
    load_kT, load_V = c['load_kT'], c['load_V']
    xdbg = c.get('xdbg')
    with ExitStack() as es:
        P.es = es
        consts = P.sb("consts", [128, 64], F32)
        slg = P.sb("slg", [128, 128], F32)
        cg_t = P.sb("cg_t", [128, NS, 24], F32)
        Wk = dict(sps=[ps[0], ps[1], ps[7]], scnt=0, tcnt=0, consts=consts)
        O_ps = ps[2:4]
        Ob = [ps[2], ps[3], ps[5], ps[6]]
        ptr = ps[4]
        pm = ps[5]
        P.dma('sync', consts.ap[:, 0:20], cfar, [], [consts])
        P.dma('sync', consts.ap[:, 20:28], sinks, [], [consts])
        P.dma('sync', consts.ap[:, 29:31], lamc, [], [consts])
        P.dma('sync', slg.ap, subln, [], [slg])
        P.dma('sync', cg_t.ap, cgs.rearrange("(s p) c -> p s c", p=128), [cgs_tok], [cg_t])
        V_(P, lambda e: e.memset(consts.ap[:, 31:32], 1e-30), [], [consts])
        A_(P, lambda e: e.activation(out=consts.ap[:, 20:28], in_=consts.ap[:, 20:28], func=ACT.Exp), [consts], [consts])
        V_(P, lambda e: e.tensor_scalar(out=slg.ap, in0=slg.ap, scalar1=consts.ap[:, 30:31], scalar2=None, op0=ALU.mult), [slg, consts], [slg])

        with ExitStack() as es2:
            P.es = es2
            OT = P.sb("OT", [128, 12, TOK], BF16)
            Wk['pts'] = [P.sb("pt%d" % i, [128, 512], BF16) for i in range(3)]
            Wk['stmps'] = [P.sb("stmp%d" % i, [128, 512], F32) for i in range(2)]
            Wk['stmp'] = Wk['stmps'][0]
            otms = [P.sb("otm%d" % i, [128, 256], BF16) for i in range(4)]
            otm = otms[0]
            smr = [P.sb("smr%d" % i, [128, 8], F32) for i in range(8)]
            a0s = [P.sb("a0_%d" % i, [128, 128], F32) for i in range(4)]
            oos = [P.sb("oo_%d" % i, [128, 128], F32) for i in range(4)]
            pipe = Pipe()
            sm = P.sb("sm", [128, 16], F32)
            dl_t = P.sb("dl_t", [128, 4, 64], F32)
            P.dma('sync', dl_t.ap, dl, [], [dl_t])
            d4 = dl_t.ap.rearrange("p (a b) d -> p a b d", b=2)
            lt = P.sb("lt", [128, 2, 64], F32)
            V_(P, lambda e: e.tensor_tensor(out=lt.ap, in0=d4[:, :, 0, :], in1=d4[:, :, 1, :], op=ALU.mult), [dl_t], [lt])
            V_(P, lambda e: e.tensor_reduce(out=sm.ap[:, 0:2], in_=lt.ap, axis=AX.X, op=ALU.add), [lt], [sm])
            A_(P, lambda e: e.activation(out=sm.ap[:, 0:2], in_=sm.ap[:, 0:2], func=ACT.Exp), [sm], [sm])
            V_(P, lambda e: e.tensor_tensor(out=sm.ap[:, 2:3], in0=sm.ap[:, 1:2], in1=sm.ap[:, 0:1], op=ALU.subtract), [sm], [sm])
            V_(P, lambda e: e.tensor_tensor(out=consts.ap[:, 28:29], in0=sm.ap[:, 2:3], in1=consts.ap[:, 29:30], op=ALU.subtract), [sm, consts], [consts])

            def store_ot(s, t0, ntile, otm=otm):
                pb = ptr.ap.bitcast(BF16)
                for i in range(ntile):
                    T_(P, lambda e: e.transpose(out=pb[:, i * 128:(i + 1) * 128], in_=otm.ap[:, i * 128:(i + 1) * 128], identity=idn.ap), [otm, idn], [ptr])
                A_(P, lambda e: e.copy(out=OT.ap[:, t0:t0 + ntile, s * 128:(s + 1) * 128], in_=pb[:, 0:ntile * 128].rearrange("p (k t) -> p k t", k=ntile)), [ptr], [OT])

            from functools import partial
            with ExitStack() as es3:
                P.es = es3
                KA = P.sb("KA", [128, S], BF16)
                VA = P.sb("VA", [128, 32, 129], BF16)
                QA = P.sb("QAbd", [128, NS, 2, 128], BF16)
                BMA = P.sb("BMA", [128, 6, 128], F32)
                V_(P, lambda e: e.memset(VA.ap[:, :, 128:129], 1.0), [], [VA])
                V_(P, lambda e: e.memset(QA.ap, 0.0), [], [QA])

                def epi_A(h, s, O0, O1, sm, a0, oo, otm_):
                    def st1():
                        V_(P, lambda e: e.reciprocal(out=sm.ap[:, 4:5], in_=O0.ap[:, 128:129]), [O0], [sm])
                        V_(P, lambda e: e.reciprocal(out=sm.ap[:, 5:6], in_=O1.ap[:, 128:129]), [O1], [sm])
                        V_(P, lambda e: e.tensor_tensor(out=sm.ap[:, 6:7], in0=sm.ap[:, 5:6], in1=consts.ap[:, 28:29], op=ALU.mult), [sm, consts], [sm])
                        V_(P, lambda e: e.tensor_scalar(out=a0.ap, in0=O0.ap[:, 0:128], scalar1=sm.ap[:, 4:5], scalar2=None, op0=ALU.mult), [O0, sm], [a0])
                        V_(P, lambda e: e.scalar_tensor_tensor(out=oo.ap, in0=O1.ap[:, 0:128], scalar=sm.ap[:, 6:7], in1=a0.ap, op0=ALU.mult, op1=ALU.add), [O1, sm, a0], [oo])
                        pipe.defer(1, st2)

                    def st2():
                        A_(P, lambda e: e.activation(out=a0.ap, in_=oo.ap, func=ACT.Square, accum_out=sm.ap[:, 7:8]), [oo], [a0, sm])
                        A_(P, lambda e: e.activation(out=sm.ap[:, 7:8], in_=sm.ap[:, 7:8], func=ACT.Sqrt, scale=1.0 / 128, bias=eps_t.ap[:, 0:1]), [sm, eps_t], [sm])
                        pipe.defer(1, st3)

                    def st3():
                        V_(P, lambda e: e.reciprocal(out=sm.ap[:, 7:8], in_=sm.ap[:, 7:8]), [sm], [sm])
                        V_(P, lambda e: e.scalar_tensor_tensor(out=otm_.ap[:, 0:128], in0=oo.ap, scalar=sm.ap[:, 7:8], in1=slg.ap, op0=ALU.mult, op1=ALU.mult), [oo, sm, slg], [otm_])
                        pipe.defer(1, partial(store_ot, s, h, 1, otm_))

                    pipe.defer(1, st1)

                ucnt = 0
                for h in range(4):
                    load_kT(P, KA, KA.ap, h)
                    load_V(P, VA, VA.ap[:, :, 0:128], h * 128, 128)
                    for c in range(2):
                        P.dma('sync', QA.ap[64 * c:64 * c + 64, :, c, :], qt[h][64 * c:64 * c + 64, :].rearrange("p (s q) -> p s q", q=128), [qt_tok], [QA])
                    P.dma('sync', BMA.ap, bma[h], [], [BMA])
                    for s in range(NS):
                        n = nproc(s)
                        kbs = near_far(n, 3)
                        O0, O1 = Ob[(2 * ucnt) % 4], Ob[(2 * ucnt + 1) % 4]
                        done = partial(epi_A, h, s, O0, O1, smr[ucnt % 8], a0s[ucnt % 4], oos[ucnt % 4], otms[ucnt % 4])
                        ucnt += 1
                        cf = consts.ap[:, h:h + 1]
                        attn_pair_p(pipe, P, Wk, QA.ap[:, s, :, :].rearrange("p j q -> p (j q)"), QA,
                                    lambda kb: KA.ap[:, kb * 128:(kb + 1) * 128], KA, kbs,
                                    lambda kb: VA.ap[:, kb, :], VA, 129, (O0, O1),
                                    lambda r: BMA.ap[:, (s % 2) * 3 + r, :].unsqueeze(1).broadcast_to([128, 2, 128]), BMA, (cf, cf), on_done=done)
                    pipe.flush()
            P.barrier()
            with ExitStack() as es3:
                P.es = es3
                KB = P.sb("KB", [128, S], BF16)
                VB = P.sb("VB", [128, 32, 65], BF16)
                QB = P.sb("QBbd", [128, 2, NS, 2, 128], BF16)
                BMB = P.sb("BMB", [128, 4, 6, 128], F32)
                V_(P, lambda e: e.memset(VB.ap[:, :, 64:65], 1.0), [], [VB])
                V_(P, lambda e: e.memset(QB.ap, 0.0), [], [QB])

                def epi_B(h, r, s, g, O, sm, otm_):
                    def st1():
                        V_(P, lambda e: e.tensor_tensor(out=sm.ap[:, 4:5], in0=O.ap[:, 64:65], in1=consts.ap[:, 20 + h:21 + h], op=ALU.add), [O, consts], [sm])
                        V_(P, lambda e: e.reciprocal(out=sm.ap[:, 4:5], in_=sm.ap[:, 4:5]), [sm], [sm])
                        V_(P, lambda e: e.tensor_scalar(out=otm_.ap[:, r * 64:(r + 1) * 64], in0=O.ap[:, 0:64], scalar1=sm.ap[:, 4:5], scalar2=None, op0=ALU.mult), [O, sm], [otm_])
                        if r == 3:
                            pipe.defer(1, partial(store_ot, s, 4 + 2 * g, 2, otm_))
                    pipe.defer(1, st1)

                ucnt = 0
                for g in range(2):
                    load_kT(P, KB, KB.ap, 4 + g)
                    load_V(P, VB, VB.ap[:, :, 0:64], 512 + g * 64, 64)
                    for pr_ in range(2):
                        for c in range(2):
                            P.dma('sync', QB.ap[64 * c:64 * c + 64, pr_, :, c, :], qt[4 + 2 * g + pr_][64 * c:64 * c + 64, :].rearrange("p (s q) -> p s q", q=128), [qt_tok], [QB])
                    P.dma('sync', BMB.ap, bmb[g], [], [BMB])
                    for s in range(NS):
                        n = nproc(s)
                        kbs = [(kb, kb - (n - 3)) for kb in range(max(0, n - 3), n)]
                        otm_ = otms[s % 4]
                        for pr_ in range(2):
                            Os = (Ob[(2 * ucnt) % 4], Ob[(2 * ucnt + 1) % 4])
                            d0 = partial(epi_B, 4 * g + 2 * pr_, 2 * pr_, s, g, Os[0], smr[(2 * ucnt) % 8], otm_)
                            d1 = partial(epi_B, 4 * g + 2 * pr_ + 1, 2 * pr_ + 1, s, g, Os[1], smr[(2 * ucnt + 1) % 8], otm_)
                            ucnt += 1

                            def done(d0=d0, d1=d1):
                                d0()
                                d1()
                            attn_pair_p(pipe, P, Wk, QB.ap[:, pr_, s, :, :].rearrange("p j q -> p (j q)"), QB,
                                        lambda kb: KB.ap[:, kb * 128:(kb + 1) * 128], KB, kbs,
                                        lambda kb: VB.ap[:, kb, :], VB, 65, Os,
                                        lambda rr: BMB.ap[:, 2 * pr_:2 * pr_ + 2, (s % 2) * 3 + rr, :], BMB, (None, None), on_done=done)
                    pipe.flush()
            P.barrier()
            emit_family_c(P, locals())
            P.barrier()
            P.es = es2
            emit_phase3a(P, locals())
            P.barrier()
        P.es = es
        emit_phase3b(P, locals())
        P.barrier()


def emit_phase3a(P, L):
    x_rows, x_tok, xmid, xmid_tok, wmg, wbr, wout, nm, OT, idn, eps_t, ps = (L[k] for k in ('x_rows', 'x_tok', 'xmid', 'xmid_tok', 'wmg', 'wbr', 'wout', 'nm', 'OT', 'idn', 'eps_t', 'ps'))
    with ExitStack() as es3:
        P.es = es3
        W = dict(idn=idn, eps=eps_t, pT=ps[6])
        W['sq1024'] = P.sb("sq1024", [128, 1024], F32)
        W['ss1'] = P.sb("ss1", [128, 1], F32)
        W['xb'] = P.sb("xb", [128, 1024], BF16)
        nm_t = P.sb("nm_t", [128, 8], F32)
        P.dma('sync', nm_t.ap, nm, [], [nm_t])
        hT = P.sb("hT", [128, 8, TOK], BF16)
        xs = [P.sb("xs%d" % i, [128, 1024], F32) for i in range(2)]
        for s in range(NS):
            b = s % 2
            P.dma('gpsimd', xs[b].ap, x_rows(s), [x_tok], [xs[b]])
            emit_xnorm_T(P, xs[b].ap, xs[b], hT, s, W)
        WB = P.sb("WB", [128, 12, 1024], BF16)
        WO = P.sb("WO", [128, 8, 1024], BF16)
        stg = [P.sb("stg%d" % i, [128, 2, 1024], F32) for i in range(2)]
        wbv = wbr.rearrange("n (mc p) d -> p (n mc) d", p=128)
        wov = wout.rearrange("(dc p) d -> p dc d", p=128)
        for i in range(10):
            st = stg[i % 2]
            if i < 6:
                P.dma('sync', st.ap, wbv[:, 2 * i:2 * i + 2, :], [], [st])
                dst = WB.ap[:, 2 * i:2 * i + 2, :]
                dbuf = WB
            else:
                P.dma('sync', st.ap, wov[:, 2 * (i - 6):2 * (i - 6) + 2, :], [], [st])
                dst = WO.ap[:, 2 * (i - 6):2 * (i - 6) + 2, :]
                dbuf = WO
            V_(P, lambda e: e.tensor_copy(out=dst[:, 0:1, :], in_=st.ap[:, 0:1, :]), [st], [dbuf])
            A_(P, lambda e: e.copy(out=dst[:, 1:2, :], in_=st.ap[:, 1:2, :]), [st], [dbuf])
        wg_s = [P.sb("wg_s0", [128, 8, 3, 128], F32)] * 2
        wg_b = [P.sb("wg_b%d" % i, [128, 8, 3, 128], BF16) for i in range(2)]
        zT = P.sb("zT", [128, 8, 512], BF16)
        Gt = [P.sb("Gt%d" % i, [128, 512], F32) for i in range(2)]
        zacc = P.sb("zacc", [128, 512], F32)
        ztmp = P.sb("ztmp", [128, 512], F32)
        xn = xs
        wgv = wmg.rearrange("(kc p) (n d) -> p kc n d", p=128, n=3)
        pg = ps[0:2]
        py = ps[2:4]
        po = ps[4:6]
        cnt = 0
        for T in range(4):
            ts = slice(T * 512, (T + 1) * 512)
            for dc in range(8):
                b = cnt % 2
                cnt += 1
                for n in range(3):
                    P.dma('sync', wg_s[b].ap[:, :, n, :], wgv[:, :, n, dc * 128:(dc + 1) * 128], [], [wg_s[b]])
                for kc in range(8):
                    if kc % 2 == 0:
                        V_(P, lambda e: e.tensor_scalar(out=wg_b[b].ap[:, kc], in0=wg_s[b].ap[:, kc], scalar1=nm_t.ap[:, kc:kc + 1], scalar2=None, op0=ALU.mult), [wg_s[b], nm_t], [wg_b[b]])
                    else:
                        A_(P, lambda e: e.activation(out=wg_b[b].ap[:, kc], in_=wg_s[b].ap[:, kc], func=ACT.Copy, scale=nm_t.ap[:, kc:kc + 1]), [wg_s[b], nm_t], [wg_b[b]])
                for n in range(3):
                    g_ps = pg[n % 2]
                    y_ps = py[n % 2]
                    G = Gt[n % 2]
                    for kc in range(8):
                        T_(P, lambda e: e.matmul(out=g_ps.ap, lhsT=wg_b[b].ap[:, kc, n, :], rhs=hT.ap[:, kc, ts], start=(kc == 0), stop=(kc == 7)), [wg_b[b], hT], [g_ps])
                    A_(P, lambda e: e.activation(out=G.ap, in_=g_ps.ap, func=ACT.Sigmoid), [g_ps], [G])
                    for mc in range(4):
                        T_(P, lambda e: e.matmul(out=y_ps.ap, lhsT=WB.ap[:, 4 * n + mc, dc * 128:(dc + 1) * 128], rhs=OT.ap[:, 4 * n + mc, ts], start=(mc == 0), stop=(mc == 3)), [WB, OT], [y_ps])
                    if n == 0:
                        V_(P, lambda e: e.tensor_tensor(out=zacc.ap, in0=G.ap, in1=y_ps.ap, op=ALU.mult), [G, y_ps], [zacc])
                    else:
                        V_(P, lambda e: e.tensor_tensor(out=ztmp.ap, in0=G.ap, in1=y_ps.ap, op=ALU.mult), [G, y_ps], [ztmp])
                        if n == 1:
                            V_(P, lambda e: e.tensor_tensor(out=zacc.ap, in0=zacc.ap, in1=ztmp.ap, op=ALU.add), [zacc, ztmp], [zacc])
                        else:
                            V_(P, lambda e: e.tensor_tensor(out=zT.ap[:, dc, :], in0=zacc.ap, in1=ztmp.ap, op=ALU.add), [zacc, ztmp], [zT])
            for si in range(4):
                s = T * 4 + si
                xb_ = xn[s % 2]
                P.dma('gpsimd', xb_.ap, x_rows(s), [x_tok], [xb_])
                for half in range(2):
                    o_ps = po[half]
                    for dc in range(8):
                        T_(P, lambda e: e.matmul(out=o_ps.ap, lhsT=zT.ap[:, dc, si * 128:(si + 1) * 128], rhs=WO.ap[:, dc, half * 512:(half + 1) * 512], start=(dc == 0), stop=(dc == 7)), [zT, WO], [o_ps])
                    V_(P, lambda e: e.tensor_tensor(out=xb_.ap[:, half * 512:(half + 1) * 512], in0=xb_.ap[:, half * 512:(half + 1) * 512], in1=o_ps.ap, op=ALU.add), [xb_, o_ps], [xb_])
                P.dma('sync', xmid[s * 128:(s + 1) * 128, :], xb_.ap, [xb_], [xmid_tok])


def emit_phase3b(P, L):
    xmid, xmid_tok, xout, xout_tok, wup, wdn, nmlp, idn, eps_t, ps = (L[k] for k in ('xmid', 'xmid_tok', 'xout', 'xout_tok', 'wup', 'wdn', 'nmlp', 'idn', 'eps_t', 'ps'))
    with ExitStack() as es3:
        P.es = es3
        W = dict(idn=idn, eps=eps_t, pT=ps[6])
        W['sq1024'] = P.sb("sq1024b", [128, 1024], F32)
        W['ss1'] = P.sb("ss1b", [128, 1], F32)
        W['xb'] = P.sb("xbb", [128, 1024], BF16)
        nm_t = P.sb("nmlp_t", [128, 8], F32)
        P.dma('sync', nm_t.ap, nmlp, [], [nm_t])
        X = P.sb("X", [128, NS, D], F32)
        hT = P.sb("hT2", [128, 8, TOK], BF16)
        P.dma('sync', X.ap, xmid.rearrange("(s p) d -> p s d", p=128), [xmid_tok], [X])
        for s in range(NS):
            emit_xnorm_T(P, X.ap[:, s, :], X, hT, s, W)
        wu_s = [P.sb("wu_s0", [128, 8, 512], F32)] * 2
        wu_b = [P.sb("wu_b%d" % i, [128, 8, 512], BF16) for i in range(2)]
        wd_s = [P.sb("wd_s0", [128, 4, 1024], F32)] * 2
        wd_b = [P.sb("wd_b%d" % i, [128, 4, 1024], BF16) for i in range(2)]
        aT = P.sb("aT", [128, 4, TOK], BF16)
        rl = [P.sb("rl%d" % i, [128, 512], F32) for i in range(2)]
        wuv = wup.rearrange("(kc p) f -> p kc f", p=128)
        wdv = wdn.rearrange("(fc p) d -> p fc d", p=128)
        pu = ps[0:2]
        pd = ps[2:4]

        def load(fg):
            b = fg % 2
            P.dma('sync', wu_s[b].ap, wuv[:, :, fg * 512:(fg + 1) * 512], [], [wu_s[b]])
            P.dma('gpsimd', wd_s[b].ap, wdv[:, fg * 4:(fg + 1) * 4, :], [], [wd_s[b]])

        load(0)
        cnt = 0
        for fg in range(8):
            b = fg % 2
            for kc in range(8):
                if kc % 2 == 0:
                    V_(P, lambda e: e.tensor_scalar(out=wu_b[b].ap[:, kc], in0=wu_s[b].ap[:, kc], scalar1=nm_t.ap[:, kc:kc + 1], scalar2=None, op0=ALU.mult), [wu_s[b], nm_t], [wu_b[b]])
                else:
                    A_(P, lambda e: e.activation(out=wu_b[b].ap[:, kc], in_=wu_s[b].ap[:, kc], func=ACT.Copy, scale=nm_t.ap[:, kc:kc + 1]), [wu_s[b], nm_t], [wu_b[b]])
            V_(P, lambda e: e.tensor_copy(out=wd_b[b].ap[:, 0:2], in_=wd_s[b].ap[:, 0:2]), [wd_s[b]], [wd_b[b]])
            A_(P, lambda e: e.copy(out=wd_b[b].ap[:, 2:4], in_=wd_s[b].ap[:, 2:4]), [wd_s[b]], [wd_b[b]])
            if fg + 1 < 8:
                load(fg + 1)
            for fc in range(4):
                for T in range(4):
                    u_ps = pu[cnt % 2]
                    r_ = rl[cnt % 2]
                    cnt += 1
                    for kc in range(8):
                        T_(P, lambda e: e.matmul(out=u_ps.ap, lhsT=wu_b[b].ap[:, kc, fc * 128:(fc + 1) * 128], rhs=hT.ap[:, kc, T * 512:(T + 1) * 512], start=(kc == 0), stop=(kc == 7)), [wu_b[b], hT], [u_ps])
                    A_(P, lambda e: e.activation(out=r_.ap, in_=u_ps.ap, func=ACT.Relu), [u_ps], [r_])
                    V_(P, lambda e: e.tensor_tensor(out=aT.ap[:, fc, T * 512:(T + 1) * 512], in0=r_.ap, in1=r_.ap, op=ALU.mult), [r_], [aT])
            for s in range(NS):
                for half in range(2):
                    d_ps = pd[half]
                    for fc in range(4):
                        T_(P, lambda e: e.matmul(out=d_ps.ap, lhsT=aT.ap[:, fc, s * 128:(s + 1) * 128], rhs=wd_b[b].ap[:, fc, half * 512:(half + 1) * 512], start=(fc == 0), stop=(fc == 3)), [aT, wd_b[b]], [d_ps])
                    V_(P, lambda e: e.tensor_tensor(out=X.ap[:, s, half * 512:(half + 1) * 512], in0=X.ap[:, s, half * 512:(half + 1) * 512], in1=d_ps.ap, op=ALU.add), [X, d_ps], [X])
        P.dma('sync', xout.rearrange("(s p) d -> p s d", p=128), X.ap, [X], [xout_tok])
        if L.get('xdbg') is not None:
            P.dma('sync', L['xdbg'].rearrange("(s p) d -> p s d", p=128), X.ap, [X], [])


def emit_family_c(P, L):
    qt, qt_tok, load_kT, load_V, w1, w2, posT, g5, bms, bmw, cm, selb, E_d, ov_d = (L[k] for k in
        ('qt', 'qt_tok', 'load_kT', 'load_V', 'w1', 'w2', 'posT', 'g5', 'bms', 'bmw', 'cm', 'selb', 'E_d', 'ov_d'))
    Wk, O_ps, ptr, pm, consts, eps_t, idn, cg_t, otm, sm, store_ot, ps = (L[k] for k in
        ('Wk', 'O_ps', 'ptr', 'pm', 'consts', 'eps_t', 'idn', 'cg_t', 'otm', 'sm', 'store_ot', 'ps'))
    ph = ps[6]
    with ExitStack() as es3:
        P.es = es3
        kcT = P.sb("kcT", [128, 2, 256], BF16)
        vca = P.sb("vca", [128, 2, 2, 129], BF16)
        V_(P, lambda e: e.memset(kcT.ap, 0.0), [], [kcT])
        V_(P, lambda e: e.memset(vca.ap, 0.0), [], [vca])
        V_(P, lambda e: e.memset(vca.ap[:, :, :, 64:65], 1.0), [], [vca])
        for g in range(2):
            P.dma('sync', vca.ap[:, g, :, 65:129], ov_d, [], [vca])
        with ExitStack() as es4:
            P.es = es4
            CKV = P.sb("CKV", [128, 2, S], BF16)
            load_kT(P, CKV, CKV.ap[:, 0, :], 6)
            load_kT(P, CKV, CKV.ap[:, 1, :], 7)
            W1s = P.sb("W1s", [128, 32, 256], F32)
            W1b = P.sb("W1b", [128, 32, 256], BF16)
            W2s = P.sb("W2s", [128, 2, 64], F32)
            W2b = P.sb("W2b", [128, 2, 64], BF16)
            pos_s = P.sb("pos_s", [128, 32], F32)
            pos_b = P.sb("pos_b", [128, 32], BF16)
            g5_t = P.sb("g5_t", [128, 64], F32)
            hb = P.sb("hb", [128, 1], F32)
            u = P.sb("u", [128, 256], F32)
            u2 = P.sb("u2", [128, 256], F32)
            gT = P.sb("gT", [128, 2, 256], BF16)
            kd = P.sb("kd", [128, 128], BF16)
            kf = P.sb("kf", [128, 64], F32)
            V_(P, lambda e: e.memset(gT.ap, 0.0), [], [gT])
            P.dma('sync', g5_t.ap, g5, [], [g5_t])
            for i in range(2):
                for half in range(2):
                    P.dma('sync', W1s.ap[64 * half:64 * half + 64], w1[i].rearrange("(l d) h -> d l h", d=64), [], [W1s])
                    P.dma('sync', pos_s.ap[64 * half:64 * half + 64], posT[i], [], [pos_s])
                P.dma('sync', W2s.ap, w2[i].rearrange("(c p) d -> p c d", p=128), [], [W2s])
                for q in range(4):
                    if q % 2 == 0:
                        V_(P, lambda e: e.tensor_copy(out=W1b.ap[:, q * 8:(q + 1) * 8, :], in_=W1s.ap[:, q * 8:(q + 1) * 8, :]), [W1s], [W1b])
                    else:
                        A_(P, lambda e: e.copy(out=W1b.ap[:, q * 8:(q + 1) * 8, :], in_=W1s.ap[:, q * 8:(q + 1) * 8, :]), [W1s], [W1b])
                V_(P, lambda e: e.tensor_copy(out=W2b.ap, in_=W2s.ap), [W2s], [W2b])
                V_(P, lambda e: e.tensor_copy(out=pos_b.ap, in_=pos_s.ap), [pos_s], [pos_b])
                for g in range(2):
                    r0 = 64 * g
                    src3 = CKV.ap[r0:r0 + 64, i, :].rearrange("p (c s) -> p c s", s=16)
                    for hc in range(2):
                        for l in range(32):
                            T_(P, lambda e: e.matmul(out=pm.ap[:, 0:1], lhsT=W1b.ap[r0:r0 + 64, l, hc * 128:(hc + 1) * 128], rhs=pos_b.ap[r0:r0 + 64, l:l + 1],
                                                     start=(l == 0), stop=(l == 31)), [W1b, pos_b], [pm])
                        A_(P, lambda e: e.copy(out=hb.ap, in_=pm.ap[:, 0:1]), [pm], [hb])
                        for l in range(32):
                            rhs = src3[:, 0:255, l] if l < 16 else src3[:, 1:256, l - 16]
                            T_(P, lambda e: e.matmul(out=ph.ap[:, 0:255], lhsT=W1b.ap[r0:r0 + 64, l, hc * 128:(hc + 1) * 128], rhs=rhs,
                                                     start=(l == 0), stop=(l == 31)), [W1b, CKV], [ph])
                        A_(P, lambda e: e.activation(out=u.ap[:, 0:255], in_=ph.ap[:, 0:255], func=ACT.Identity, bias=hb.ap[:, 0:1]), [ph, hb], [u])
                        V_(P, lambda e: e.tensor_tensor(out=u2.ap[:, 0:255], in0=u.ap[:, 0:255], in1=u.ap[:, 0:255], op=ALU.mult), [u], [u2])
                        V_(P, lambda e: e.tensor_scalar(out=u2.ap[:, 0:255], in0=u2.ap[:, 0:255], scalar1=0.044715, scalar2=1.0, op0=ALU.mult, op1=ALU.add), [u2], [u2])
                        V_(P, lambda e: e.tensor_tensor(out=u2.ap[:, 0:255], in0=u2.ap[:, 0:255], in1=u.ap[:, 0:255], op=ALU.mult), [u2, u], [u2])
                        A_(P, lambda e: e.activation(out=u2.ap[:, 0:255], in_=u2.ap[:, 0:255], func=ACT.Tanh, scale=0.7978845608028654), [u2], [u2])
                        V_(P, lambda e: e.tensor_scalar(out=u2.ap[:, 0:255], in0=u2.ap[:, 0:255], scalar1=1.0, scalar2=0.5, op0=ALU.add, op1=ALU.mult), [u2], [u2])
                        V_(P, lambda e: e.tensor_tensor(out=gT.ap[:, hc, 0:255], in0=u2.ap[:, 0:255], in1=u.ap[:, 0:255], op=ALU.mult), [u2, u], [gT])
                    for t in range(2):
                        cn = 128 if t == 0 else 127
                        for hc in range(2):
                            T_(P, lambda e: e.matmul(out=pm.ap[0:cn, 0:64], lhsT=gT.ap[:, hc, t * 128:t * 128 + cn], rhs=W2b.ap[:, hc, :],
                                                     start=(hc == 0), stop=(hc == 1)), [gT, W2b], [pm])
                        if i == 1:
                            A_(P, lambda e: e.copy(out=vca.ap[0:cn, g, t, 0:64], in_=pm.ap[0:cn, 0:64]), [pm], [vca])
                        else:
                            A_(P, lambda e: e.copy(out=kf.ap[0:cn, :], in_=pm.ap[0:cn, 0:64]), [pm], [kf])
                            A_(P, lambda e: e.activation(out=u.ap[0:cn, 0:64], in_=kf.ap[0:cn, :], func=ACT.Square, accum_out=sm.ap[0:cn, 8:9]), [kf], [u, sm])
                            A_(P, lambda e: e.activation(out=sm.ap[0:cn, 8:9], in_=sm.ap[0:cn, 8:9], func=ACT.Sqrt, scale=1.0 / 64, bias=eps_t.ap[0:cn, 0:1]), [sm, eps_t], [sm])
                            V_(P, lambda e: e.reciprocal(out=sm.ap[0:cn, 8:9], in_=sm.ap[0:cn, 8:9]), [sm], [sm])
                            V_(P, lambda e: e.memset(kd.ap, 0.0), [], [kd])
                            for c in range(2):
                                V_(P, lambda e: e.scalar_tensor_tensor(out=kd.ap[0:cn, c * 64:(c + 1) * 64], in0=kf.ap[0:cn, :], scalar=sm.ap[0:cn, 8:9], in1=g5_t.ap[0:cn, :],
                                                                       op0=ALU.mult, op1=ALU.mult), [kf, sm, g5_t], [kd])
                            pb = ptr.ap.bitcast(BF16)
                            T_(P, lambda e: e.transpose(out=pb[:, 0:128], in_=kd.ap, identity=idn.ap), [kd, idn], [ptr])
                            A_(P, lambda e: e.copy(out=kcT.ap[:, g, t * 128:t * 128 + cn], in_=pb[:, 0:cn]), [ptr], [kcT])
        P.barrier()
        P.es = es3
        from functools import partial
        pipe, Ob, smr, otms = L['pipe'], L['Ob'], L['smr'], L['otms']
        CM = P.sb("CM", [128, 2, NS, 128], F32)
        SELB = P.sb("SELB", [128, NS, 64], F32)
        Et = P.sb("Et", [128, S], BF16)
        P.dma('sync', CM.ap, cm, [], [CM])
        P.dma('sync', SELB.ap, selb, [], [SELB])
        P.dma('sync', Et.ap[0:64], E_d, [], [Et])
        P.dma('sync', Et.ap[64:128], E_d, [], [Et])
        KS = P.sb("KS", [128, S], BF16)
        KW = P.sb("KW", [128, S], BF16)
        VS = P.sb("VS", [128, 32, 65], BF16)
        VW = P.sb("VW", [128, 32, 65], BF16)
        QC = P.sb("QCbd", [128, 2, NS, 2, 128], BF16)
        BMS = P.sb("BMS", [128, 4, 6, 128], F32)
        BMW = P.sb("BMW", [128, 4, 12, 128], F32)
        accs = [P.sb("acc%d" % i, [128, 4, 64], F32) for i in range(2)]
        imps = [P.sb("imp%d" % i, [128, 64], F32) for i in range(2)]
        scs = [P.sb("sc%d" % i, [128, 64], F32) for i in range(2)]
        sc2s = [P.sb("sc2_%d" % i, [128, 64], F32) for i in range(2)]
        m8s = [P.sb("m8_%d" % i, [128, 8], F32) for i in range(2)]
        selnbs = [P.sb("selnb%d" % i, [128, 128], BF16) for i in range(2)]
        selTs = [P.sb("sel2_%d" % i, [128, 256], BF16) for i in range(2)]
        for _b in selTs:
            V_(P, lambda e: e.memset(_b.ap, 0.0), [], [_b])
        V_(P, lambda e: e.memset(QC.ap, 0.0), [], [QC])
        V_(P, lambda e: e.memset(VS.ap[:, :, 64:65], 1.0), [], [VS])
        V_(P, lambda e: e.memset(VW.ap[:, :, 64:65], 1.0), [], [VW])

        def selT_store(selnb, selT):
            pb = ptr.ap.bitcast(BF16)
            T_(P, lambda e: e.transpose(out=pb[:, 0:128], in_=selnb.ap, identity=idn.ap), [selnb, idn], [ptr])
            A_(P, lambda e: e.copy(out=selT.ap[0:64, 0:128], in_=pb[0:64, 0:128]), [ptr], [selT])
            V_(P, lambda e: e.tensor_copy(out=selT.ap[0:64, 128:256], in_=pb[0:64, 0:128]), [ptr], [selT])

        def epi_cmp(h, r, s, O, sm, acc, imp, sc, sc2, m8, selnb, selT, state):
            V_(P, lambda e: e.tensor_tensor(out=sm.ap[:, 4:5], in0=O.ap[:, 64:65], in1=consts.ap[:, 31:32], op=ALU.add), [O, consts], [sm])
            V_(P, lambda e: e.reciprocal(out=sm.ap[:, 4:5], in_=sm.ap[:, 4:5]), [sm], [sm])
            V_(P, lambda e: e.tensor_tensor(out=sm.ap[:, 5:6], in0=sm.ap[:, 4:5], in1=cg_t.ap[:, s, 3 * h:3 * h + 1], op=ALU.mult), [sm, cg_t], [sm])
            V_(P, lambda e: e.tensor_scalar(out=acc.ap[:, r, :], in0=O.ap[:, 0:64], scalar1=sm.ap[:, 5:6], scalar2=None, op0=ALU.mult), [O, sm], [acc])
            if r == 0:
                V_(P, lambda e: e.tensor_scalar(out=imp.ap, in0=O.ap[:, 65:129], scalar1=sm.ap[:, 4:5], scalar2=None, op0=ALU.mult), [O, sm], [imp])
            else:
                V_(P, lambda e: e.scalar_tensor_tensor(out=imp.ap, in0=O.ap[:, 65:129], scalar=sm.ap[:, 4:5], in1=imp.ap, op0=ALU.mult, op1=ALU.add), [O, sm, imp], [imp])
            if r == 3:
                V_(P, lambda e: e.tensor_tensor(out=sc.ap, in0=imp.ap, in1=SELB.ap[:, s, :], op=ALU.add), [imp, SELB], [sc])
                V_(P, lambda e: e.max(out=m8.ap, in_=sc.ap), [sc], [m8])
                V_(P, lambda e: e.match_replace(out=sc2.ap, in_to_replace=m8.ap, in_values=sc.ap, imm_value=-3.0e9), [m8, sc], [sc2])
                V_(P, lambda e: e.max(out=m8.ap, in_=sc2.ap), [sc2], [m8])
                for c in range(2):
                    V_(P, lambda e: e.tensor_scalar(out=selnb.ap[:, c * 64:(c + 1) * 64], in0=sc.ap, scalar1=m8.ap[:, 7:8], scalar2=NEG, op0=ALU.is_lt, op1=ALU.mult), [sc, m8], [selnb])

                def fire():
                    selT_store(selnb, selT)
                    state['selT_ready'] = True
                pipe.defer(2, fire)

        def epi_br(h, r, s, g, br, O, sm, acc, otm_, last):
            V_(P, lambda e: e.reciprocal(out=sm.ap[:, 4:5], in_=O.ap[:, 64:65]), [O], [sm])
            V_(P, lambda e: e.tensor_tensor(out=sm.ap[:, 5:6], in0=sm.ap[:, 4:5], in1=cg_t.ap[:, s, 3 * h + 1 + br:3 * h + 2 + br], op=ALU.mult), [sm, cg_t], [sm])
            V_(P, lambda e: e.scalar_tensor_tensor(out=acc.ap[:, r, :], in0=O.ap[:, 0:64], scalar=sm.ap[:, 5:6], in1=acc.ap[:, r, :], op0=ALU.mult, op1=ALU.add), [O, sm, acc], [acc])
            if last:
                V_(P, lambda e: e.tensor_copy(out=otm_.ap[:, 0:256], in_=acc.ap.rearrange("p r d -> p (r d)")), [acc], [otm_])
                pipe.defer(2, partial(store_ot, s, 8 + 2 * g, 2, otm_))

        def push_cmp(g, s, r, nct, O, done):
            p0 = 64 * (r % 2)
            qap = QC.ap[p0:p0 + 64, r // 2, s, r % 2, :]
            k = Wk['scnt']
            Wk['scnt'] += 1
            Sb = Wk['sps'][k % len(Wk['sps'])]
            PT = Wk['pts'][k % len(Wk['pts'])]
            tmp = Wk['stmps'][Wk['tcnt'] % len(Wk['stmps'])]
            Wk['tcnt'] += 1

            def s_fn():
                for t in range(nct):
                    T_(P, lambda e: e.matmul(out=Sb.ap[:, t * 128:(t + 1) * 128], lhsT=kcT.ap[p0:p0 + 64, g, t * 128:(t + 1) * 128], rhs=qap, start=True, stop=True), [kcT, QC], [Sb])

            def e_fn():
                for t in range(nct):
                    V_(P, lambda e: e.tensor_tensor(out=tmp.ap[:, t * 128:(t + 1) * 128], in0=Sb.ap[:, t * 128:(t + 1) * 128], in1=CM.ap[:, t, s, :], op=ALU.add), [Sb, CM], [tmp])
                A_(P, lambda e: e.activation(out=PT.ap[:, 0:nct * 128], in_=tmp.ap[:, 0:nct * 128], func=ACT.Exp), [tmp], [PT])

            def pv_fn():
                for t in range(nct):
                    T_(P, lambda e: e.matmul(out=O.ap[:, 0:129], lhsT=PT.ap[:, t * 128:(t + 1) * 128], rhs=vca.ap[:, g, t, :], start=(t == 0), stop=(t == nct - 1)), [PT, vca], [O])
                done()

            pipe.push(s_fn, e_fn, pv_fn)

        ucnt = 0
        gs = 0
        for g in range(2):
            load_kT(P, KS, KS.ap, 8 + g)
            load_kT(P, KW, KW.ap, 10 + g)
            load_V(P, VS, VS.ap[:, :, 0:64], 640 + g * 64, 64)
            load_V(P, VW, VW.ap[:, :, 0:64], 768 + g * 64, 64)
            for pr_ in range(2):
                for c in range(2):
                    P.dma('sync', QC.ap[64 * c:64 * c + 64, pr_, :, c, :], qt[8 + 2 * g + pr_][64 * c:64 * c + 64, :].rearrange("p (s q) -> p s q", q=128), [qt_tok], [QC])
            P.dma('sync', BMS.ap, bms[g], [], [BMS])
            P.dma('sync', BMW.ap, bmw[g], [], [BMW])
            for s in range(NS):
                n = nproc(s)
                nct = 1 if 8 * n <= 128 else 2
                i2 = gs % 2
                gs += 1
                acc, imp, sc, sc2, m8, selnb, selT, otm_ = accs[i2], imps[i2], scs[i2], sc2s[i2], m8s[i2], selnbs[i2], selTs[i2], otms[gs % 4]
                state = dict(selT_ready=False)
                for r in range(4):
                    h = 4 * g + r
                    O = Ob[ucnt % 4]
                    done = partial(epi_cmp, h, r, s, O, smr[ucnt % 8], acc, imp, sc, sc2, m8, selnb, selT, state)
                    ucnt += 1
                    push_cmp(g, s, r, nct, O, done)
                for pr_ in range(2):
                    Os = (Ob[(2 * ucnt) % 4], Ob[(2 * ucnt + 1) % 4])
                    d0 = partial(epi_br, 4 * g + 2 * pr_, 2 * pr_, s, g, 1, Os[0], smr[(2 * ucnt) % 8], acc, otm_, False)
                    d1 = partial(epi_br, 4 * g + 2 * pr_ + 1, 2 * pr_ + 1, s, g, 1, Os[1], smr[(2 * ucnt + 1) % 8], acc, otm_, False)
                    ucnt += 1

                    def done(d0=d0, d1=d1):
                        d0()
                        d1()
                    kbs = [(kb, kb - (n - 6)) for kb in range(max(0, n - 6), n)]
                    attn_pair_p(pipe, P, Wk, QC.ap[:, pr_, s, :, :].rearrange("p j q -> p (j q)"), QC,
                                lambda kb: KW.ap[:, kb * 128:(kb + 1) * 128], KW, kbs,
                                lambda kb: VW.ap[:, kb, :], VW, 65, Os,
                                lambda rr: BMW.ap[:, 2 * pr_:2 * pr_ + 2, (s % 2) * 6 + rr, :], BMW, (None, None), on_done=done)
                assert state['selT_ready'], "selection mask must be emitted before the selected-block branch"
                for pr_ in range(2):
                    Os = (Ob[(2 * ucnt) % 4], Ob[(2 * ucnt + 1) % 4])
                    h0 = 4 * g + 2 * pr_
                    d0 = partial(epi_br, h0, 2 * pr_, s, g, 0, Os[0], smr[(2 * ucnt) % 8], acc, otm_, False)
                    d1 = partial(epi_br, h0 + 1, 2 * pr_ + 1, s, g, 0, Os[1], smr[(2 * ucnt + 1) % 8], acc, otm_, pr_ == 1)
                    ucnt += 1

                    def done(d0=d0, d1=d1):
                        d0()
                        d1()
                    attn_pair_p(pipe, P, Wk, QC.ap[:, pr_, s, :, :].rearrange("p j q -> p (j q)"), QC,
                                lambda kb: KS.ap[:, kb * 128:(kb + 1) * 128], KS, near_far(n, 3),
                                lambda kb: VS.ap[:, kb, :], VS, 65, Os,
                                lambda rr: BMS.ap[:, 2 * pr_:2 * pr_ + 2, (s % 2) * 3 + rr, :], BMS,
                                (consts.ap[:, 12 + h0:13 + h0], consts.ap[:, 13 + h0:14 + h0]), sel=(Et, selT.ap, selT), on_done=done)
            pipe.flush()


RANK_SLOT_C = {(0, 0): 0, (0, 1): 3, (1, 0): 1, (1, 1): 2}
PAIRS = [[0, 1], [2, 3], [4, 5], [6, 7]]
NL = 4
SAME_ENGINE_SYNC = True


def build_fused(n_layers=NL):
    nc = bass.Bass("TRN2", target_bir_lowering=False)

    def din(name, shape, dt=F32):
        return nc.dram_tensor(name, list(shape), dt, kind="ExternalInput").ap()

    x = din("x", [TOK, D])
    w_in = din("w_in", [NL, D, 6680])
    wbr = din("wbr", [NL, 3, 512, D])
    wout = din("wout", [NL, D, D])
    wup = din("wup", [NL, D, 4096])
    wdn = din("wdn", [NL, 4096, D])
    nm = din("nm", [NL, 128, 8])
    nmlp = din("nmlp", [NL, 128, 8])
    gains = din("gains", [NL, 128, 8, 64])
    w1 = din("w1", [NL, 2, 2048, 256])
    w2 = din("w2", [NL, 2, 256, 64])
    posT = din("posT", [NL, 2, 64, 32])
    dl = din("dl", [NL, 128, 4, 64])
    lamc = din("lamc", [NL, 128, 2])
    subln = din("subln", [NL, 128, 128])
    sinks = din("sinks", [NL, 128, 8])
    cfar = din("cfar", [128, 20])
    bma = din("bma", [4, 128, 6, 128])
    bmb = din("bmb", [2, 128, 4, 6, 128])
    bms = din("bms", [2, 128, 4, 6, 128])
    bmw = din("bmw", [2, 128, 4, 12, 128])
    cm = din("cm", [128, 2, NS, 128])
    selb = din("selb", [128, NS, 64])
    E_d = din("E", [64, S], BF16)
    ov_d = din("ov", [128, 2, 64], BF16)
    idn_d = din("idn", [128, 128], BF16)
    xo = nc.dram_tensor("xo", [TOK, D], F32, kind="ExternalOutput").ap()
    xdbg_all = nc.dram_tensor("xdbg", [NL, TOK, D], F32, kind="ExternalOutput").ap() if _DBG else None
    qt_d = nc.dram_tensor("qt_d", [12, 128, TOK], BF16).ap()
    cgs_d = nc.dram_tensor("cgs_d", [TOK, 24], F32).ap()
    xmid = nc.dram_tensor("xmid", [TOK, D], F32).ap()
    xbuf = [nc.dram_tensor("xbuf%d" % i, [TOK, D], F32).ap() for i in range(2)]
    exK = [[nc.dram_tensor("exK%d_%d" % (i, q), [4 * 128, TOK], BF16).ap() for q in range(3)] for i in range(2)]
    exV = [[nc.dram_tensor("exV%d_%d" % (i, u), [TOK // 2, 896], BF16).ap() for u in range(2)] for i in range(2)]
    exKo = [[nc.dram_tensor("exKo%d_%d" % (i, q), [2 * 4 * 128, TOK], BF16).ap() for q in range(3)] for i in range(2)]
    exVo = [[nc.dram_tensor("exVo%d_%d" % (i, u), [TOK, 896], BF16).ap() for u in range(2)] for i in range(2)]

    with ExitStack() as es:
        P = Prog(nc, es, same_engine_sync=SAME_ENGINE_SYNC)
        idn = P.sb("idn_t", [128, 128], BF16)
        eps_t = P.sb("eps_t", [128, 1], F32)
        ps = [P.ps("ps%d" % i) for i in range(8)]
        P.dma('sync', idn.ap, idn_d, [], [idn])
        V_(P, lambda e: e.memset(eps_t.ap, EPS), [], [eps_t])
        toks = dict(qt=Tok("qt"), cgs=Tok("cgs"), xmid=Tok("xmid"), xb=[Tok("xb0"), Tok("xb1")], xo=Tok("xo"), x=Tok("x"),
                    exK=[[Tok("exK") for q in range(3)] for i in range(2)], exV=[[Tok("exV") for u in range(2)] for i in range(2)],
                    exKo=[[Tok("exKo") for q in range(3)] for i in range(2)], exVo=[[Tok("exVo") for u in range(2)] for i in range(2)])
        for l in range(n_layers):
            par = l % 2
            if l == 0:
                xsrc, xsrc_tok = x, toks['x']
            else:
                xsrc, xsrc_tok = xbuf[(l - 1) % 2], toks['xb'][(l - 1) % 2]
            if l == n_layers - 1:
                xdst, xdst_tok = xo, toks['xo']
            else:
                xdst, xdst_tok = xbuf[l % 2], toks['xb'][l % 2]
            Ko, Vo = exKo[par], exVo[par]
            Ko_tok, Vo_tok = toks['exKo'][par], toks['exVo'][par]

            def load_kT(P, buf, dst_ap, tile, Ko=Ko, Ko_tok=Ko_tok):
                d4 = dst_ap.rearrange("p (m c t) -> p m c t", c=4, t=128)
                kq, kr = divmod(tile, 4)
                for rank in range(2):
                    src = Ko[kq][rank * 512 + kr * 128:rank * 512 + (kr + 1) * 128, :].rearrange("p (m o t) -> p m o t", o=2, t=128)
                    for odd in range(2):
                        P.dma('sync', d4[:, :, RANK_SLOT_C[(rank, odd)], :], src[:, :, odd, :], [Ko_tok[kq]], [buf])

            def load_V(P, buf, dst_ap3, col0, ncols, Vo=Vo, Vo_tok=Vo_tok):
                d4 = dst_ap3.rearrange("p (m c) v -> p m c v", c=4)
                for u in range(2):
                    for rank in range(2):
                        src = Vo[u][rank * 1024:(rank + 1) * 1024, col0:col0 + ncols].rearrange("(m o p) v -> p m o v", o=2, p=128)
                        for odd in range(2):
                            P.dma('sync', d4[:, 4 * u:4 * u + 4, RANK_SLOT_C[(rank, odd)], :], src[:, :, odd, :], [Vo_tok[u]], [buf])

            c = dict(idn=idn, eps_t=eps_t, ps=ps,
                     x_rows=(lambda s, xsrc=xsrc: xsrc[s * 128:(s + 1) * 128, :]), x_tok=xsrc_tok,
                     wv=w_in[l].rearrange("(kc p) c -> p kc c", p=128), nm=nm[l], gains=gains[l],
                     qt_d=qt_d, qt_tok=toks['qt'], exK3=[a.rearrange("(k p) t -> k p t", p=128) for a in exK[par]], exK_tok=toks['exK'][par],
                     exV=exV[par], exV_tok=toks['exV'][par], cgs_d=cgs_d, cgs_tok=toks['cgs'],
                     wmg=w_in[l][:, C1:6680], wbr=wbr[l], wout=wout[l], wup=wup[l], wdn=wdn[l], nmlp=nmlp[l],
                     w1=w1[l], w2=w2[l], posT=posT[l], g5=gains[l][:, 5, :], dl=dl[l], lamc=lamc[l], subln=subln[l], sinks=sinks[l],
                     cfar=cfar, bma=bma, bmb=bmb, bms=bms, bmw=bmw, cm=cm, selb=selb, E_d=E_d, ov_d=ov_d,
                     xdbg=(xdbg_all[l] if _DBG else None), xmid=xmid, xmid_tok=toks['xmid'], xout=xdst, xout_tok=xdst_tok, load_kT=load_kT, load_V=load_V)
            emit_phase1(P, c)
            P.barrier()
            for q in range(3):
                P.collective(exK[par][q], Ko[q], [toks['exK'][par][q]], [Ko_tok[q]], PAIRS)
            for u in range(2):
                P.collective(exV[par][u], Vo[u], [toks['exV'][par][u]], [Vo_tok[u]], PAIRS)
            emit_phase23(P, c)
        P.es = es
        P.finish()
    return nc


BF = ml_dtypes.bfloat16


def _bucket(dist):
    n = np.maximum(dist, 0)
    nf = np.maximum(n, 1).astype(np.float32)
    large = 16 + (np.log(nf / np.float32(16)) / np.float32(np.log(8.0)) * np.float32(16)).astype(np.int32)
    large = np.minimum(large, 31)
    return np.where(n < 16, n, large)


def _bm_tile(rel_bias, col, delta, window):
    k = np.arange(128)[:, None]
    q = np.arange(128)[None, :]
    dist = delta * 128 + q - k
    vis = dist >= 0
    if window is not None:
        vis = vis & (dist < window)
    vals = rel_bias[_bucket(dist), col].astype(np.float32)
    return np.where(vis, vals, np.float32(NEG)).astype(np.float32)


def _bm_table(rel_bias, j, cols, nnear, window):
    out = np.zeros((len(cols), 128, 2 * nnear, 128), np.float32)
    for par in range(2):
        for r in range(nnear):
            delta = qi_of(j, par) - nproc(par) + nnear - r
            for ci, col in enumerate(cols):
                out[ci, :, par * nnear + r, :] = _bm_tile(rel_bias, col, delta, window)
    return out


def _core_tables(rel_bias, j):
    t = {}
    t['bma'] = _bm_table(rel_bias, j, list(range(0, 4)), 3, None)
    bmb = _bm_table(rel_bias, j, list(range(4, 12)), 3, 128)
    t['bmb'] = np.ascontiguousarray(bmb.reshape(2, 4, 128, 6, 128).transpose(0, 2, 1, 3, 4))
    bms = _bm_table(rel_bias, j, list(range(12, 20)), 3, None)
    t['bms'] = np.ascontiguousarray(bms.reshape(2, 4, 128, 6, 128).transpose(0, 2, 1, 3, 4))
    bmw = _bm_table(rel_bias, j, list(range(12, 20)), 6, 512)
    t['bmw'] = np.ascontiguousarray(bmw.reshape(2, 4, 128, 12, 128).transpose(0, 2, 1, 3, 4))
    cm = np.zeros((128, 2, NS, 128), np.float32)
    selb = np.zeros((128, NS, 64), np.float32)
    for s in range(NS):
        qi = qi_of(j, s)
        tq = 128 * qi + np.arange(128)
        for tl in range(2):
            c = tl * 128 + np.arange(128)
            vis = (16 * c[:, None] + 31 <= tq[None, :]) & (c[:, None] < 255)
            cm[:, tl, s, :] = np.where(vis, 0.0, NEG)
        jb = np.arange(64)[None, :]
        valid = 64 * jb <= tq[:, None]
        cur = tq[:, None] // 64
        forced = (jb == 0) | (jb == cur) | (jb == cur - 1)
        selb[:, s, :] = np.where(valid, np.where(forced, 1.0e4, 0.0), -1.0e9)
    t['cm'] = cm
    t['selb'] = selb
    return t


_CONST = {}


def _consts():
    if not _CONST:
        tt = np.arange(S)
        _CONST['E'] = (np.arange(64)[:, None] == (tt[None, :] // 64)).astype(BF)
        c = (np.arange(2)[None, :, None] * 128 + np.arange(128)[:, None, None])
        jb = np.arange(64)[None, None, :]
        ov = (16 * c < 64 * jb + 64) & (16 * c + 32 > 64 * jb) & (c < 255)
        _CONST['ov'] = ov.astype(BF)
        _CONST['idn'] = np.eye(128).astype(BF)
    return _CONST


_PROGS = {}


def _rep(v):
    v = np.asarray(v, np.float32)
    return np.ascontiguousarray(np.broadcast_to(v[None], (128,) + v.shape))


def _repl(v):
    v = np.asarray(v, np.float32)
    return np.ascontiguousarray(np.broadcast_to(v[:, None], (v.shape[0], 128) + v.shape[1:]))


def kernel(x, w_in, qk_gain, diff_lambda, diff_subln, sinks, cmp_pos, cmp_w1, cmp_w2,
           w_branch, w_out, norm_mix, norm_mlp, w_up, w_down, rel_bias, _n_layers=NL):
    import math
    x = np.asarray(x, np.float32)
    rel_bias = np.asarray(rel_bias, np.float32)
    if ('fused', _n_layers) not in _PROGS:
        _PROGS[('fused', _n_layers)] = build_fused(_n_layers)
    cst = _consts()
    cores = [(b, j) for b in range(4) for j in range(2)]
    tabs = [_core_tables(rel_bias, j) for j in range(2)]
    f32 = lambda a: np.ascontiguousarray(np.asarray(a, np.float32))
    lamc = np.array([[0.8 - 0.6 * math.exp(-0.3 * l), 1.0 - (0.8 - 0.6 * math.exp(-0.3 * l))] for l in range(NL)], np.float32)
    shared = dict(
        w_in=f32(w_in), wbr=f32(w_branch), wout=f32(w_out), wup=f32(w_up), wdn=f32(w_down),
        nm=np.ascontiguousarray(f32(norm_mix).reshape(NL, 8, 128).transpose(0, 2, 1)),
        nmlp=np.ascontiguousarray(f32(norm_mlp).reshape(NL, 8, 128).transpose(0, 2, 1)),
        gains=_repl(qk_gain), w1=f32(cmp_w1), w2=f32(cmp_w2),
        posT=np.ascontiguousarray(f32(cmp_pos).transpose(0, 1, 3, 2)),
        dl=_repl(diff_lambda), lamc=_repl(lamc), subln=_repl(diff_subln), sinks=_repl(sinks),
        cfar=_rep(rel_bias[31]), E=cst['E'], ov=cst['ov'], idn=cst['idn'])
    in_maps = []
    for (b, j) in cores:
        d = dict(shared)
        d['x'] = np.concatenate([x[b, qi_of(j, s) * 128:(qi_of(j, s) + 1) * 128] for s in range(NS)], 0)
        d.update(tabs[j])
        in_maps.append(d)
    res = run_bass_kernel_spmd(_PROGS[('fused', _n_layers)], in_maps, core_ids=list(range(8))).results
    if _DBG:
        _DBG_OUT['res'] = res
    out = np.zeros((4, S, D), np.float32)
    for i, (b, j) in enumerate(cores):
        xo = np.asarray(res[i]['xo'])
        for s in range(NS):
            qi = qi_of(j, s)
            out[b, qi * 128:(qi + 1) * 128] = xo[s * 128:(s + 1) * 128]
    return out
```
